# Optimizing a Trainium2 kernel written in Bass

```python
import jax, jax.numpy as jnp
from jax import lax
import numpy as np

D_MODEL = 1024
BATCH = 2
SEQ = 8192
DEPTH = 1

HEAD_DIM = 64
MOBA_HEADS = 8
SB_HEADS = 8
MOBA_WIDTH = MOBA_HEADS * HEAD_DIM
SB_WIDTH = SB_HEADS * HEAD_DIM
MOBA_BLOCK = 256
MOBA_TOPK = 3
MOBA_QCHUNK = 64
SB_QBLOCK = 128
ROPE_THETA = 10000.0
D_FF = 2816
CONV_WIDTH = 3
N_BRANCHES = 2
RMS_EPS = 1e-6
NEG = -1e30
IN_SIZES = (MOBA_WIDTH, MOBA_WIDTH, MOBA_WIDTH, SB_WIDTH, SB_WIDTH, SB_WIDTH, N_BRANCHES * D_MODEL)
IN_COLS = sum(IN_SIZES)
IN_SPLITS = tuple(int(s) for s in np.cumsum(IN_SIZES)[:-1])

kernel_name = "hybrid_moba_stickbreaking_convglu"


def rms_norm(x, g):
    xf = x.astype(jnp.float32)
    y = xf * lax.rsqrt(jnp.mean(xf * xf, axis=-1, keepdims=True) + RMS_EPS)
    return (y * g.astype(jnp.float32)).astype(x.dtype)


def to_heads(t, n_heads):
    b, s, _ = t.shape
    return t.reshape(b, s, n_heads, HEAD_DIM).transpose(0, 2, 1, 3)


def from_heads(t):
    b, h, s, d = t.shape
    return t.transpose(0, 2, 1, 3).reshape(b, s, h * d)


def rope(t, pos):
    half = HEAD_DIM // 2
    inv = ROPE_THETA ** (-jnp.arange(half, dtype=jnp.float32) / half)
    ang = pos[:, None] * inv[None, :]
    cos, sin = jnp.cos(ang), jnp.sin(ang)
    tf = t.astype(jnp.float32)
    t1, t2 = tf[..., :half], tf[..., half:]
    out = jnp.concatenate([t1 * cos - t2 * sin, t2 * cos + t1 * sin], axis=-1)
    return out.astype(t.dtype)


def moba_attention(q, k, v):
    b, h, s, d = q.shape
    nb = -(-s // MOBA_BLOCK)
    pad = nb * MOBA_BLOCK - s
    kp = jnp.pad(k, ((0, 0), (0, 0), (0, pad), (0, 0)))
    vp = jnp.pad(v, ((0, 0), (0, 0), (0, pad), (0, 0)))
    k_blk = kp.reshape(b, h, nb, MOBA_BLOCK, d)
    v_blk = vp.reshape(b, h, nb, MOBA_BLOCK, d)
    k_mean = jnp.mean(k_blk.astype(jnp.float32), axis=3)
    gate = jnp.einsum('bhtd,bhnd->bhtn', q.astype(jnp.float32), k_mean)
    q_block = jnp.arange(s) // MOBA_BLOCK
    past = jnp.arange(nb)[None, :] < q_block[:, None]
    gate = jnp.where(past, gate, NEG)
    n_sel = min(MOBA_TOPK, nb)
    _, idx = lax.top_k(gate, n_sel)

    nc = s // MOBA_QCHUNK
    q_c = q.reshape(b, h, nc, MOBA_QCHUNK, d).transpose(2, 0, 1, 3, 4)
    idx_c = idx.reshape(b, h, nc, MOBA_QCHUNK, n_sel).transpose(2, 0, 1, 3, 4)
    scale = HEAD_DIM ** -0.5
    gather = jax.vmap(jax.vmap(lambda blocks, ids: blocks[ids]))

    def chunk(args):
        c, qc, ic = args
        blk = (c * MOBA_QCHUNK) // MOBA_BLOCK
        t = c * MOBA_QCHUNK + jnp.arange(MOBA_QCHUNK)
        k_own = lax.dynamic_index_in_dim(k_blk, blk, axis=2, keepdims=False)
        v_own = lax.dynamic_index_in_dim(v_blk, blk, axis=2, keepdims=False)
        s_own = jnp.einsum('bhqd,bhkd->bhqk', qc, k_own, preferred_element_type=jnp.float32) * scale
        kpos = blk * MOBA_BLOCK + jnp.arange(MOBA_BLOCK)
        s_own = jnp.where(kpos[None, :] <= t[:, None], s_own, NEG)
        k_sel = gather(k_blk, ic)
        v_sel = gather(v_blk, ic)
        s_sel = jnp.einsum('bhqd,bhqnkd->bhqnk', qc, k_sel, preferred_element_type=jnp.float32) * scale
        valid = jnp.arange(n_sel) < blk
        s_sel = jnp.where(valid[:, None], s_sel, NEG)
        logits = jnp.concatenate([s_own, s_sel.reshape(b, h, MOBA_QCHUNK, n_sel * MOBA_BLOCK)], axis=-1)
        p = jax.nn.softmax(logits, axis=-1)
        p_own = p[..., :MOBA_BLOCK].astype(v.dtype)
        p_sel = p[..., MOBA_BLOCK:].reshape(b, h, MOBA_QCHUNK, n_sel, MOBA_BLOCK).astype(v.dtype)
        return (jnp.einsum('bhqk,bhkd->bhqd', p_own, v_own)
                + jnp.einsum('bhqnk,bhqnkd->bhqd', p_sel, v_sel))

    out = lax.map(chunk, (jnp.arange(nc), q_c, idx_c))
    return out.transpose(1, 2, 0, 3, 4).reshape(b, h, s, d)


def stick_breaking_attention(q, k, v):
    b, h, s, d = q.shape
    nc = s // SB_QBLOCK
    q_c = q.reshape(b, h, nc, SB_QBLOCK, d).transpose(2, 0, 1, 3, 4)
    spos = jnp.arange(s)
    scale = HEAD_DIM ** -0.5

    def block(args):
        c, qc = args
        t = c * SB_QBLOCK + jnp.arange(SB_QBLOCK)
        z = jnp.einsum('bhqd,bhkd->bhqk', qc, k, preferred_element_type=jnp.float32) * scale
        causal = spos[None, :] < t[:, None]
        log_1m = jnp.where(causal, -jax.nn.softplus(z), 0.0)
        after = lax.cumsum(log_1m, axis=3, reverse=True) - log_1m
        a = jnp.where(causal, jnp.exp(jax.nn.log_sigmoid(z) + after), 0.0)
        return jnp.einsum('bhqk,bhkd->bhqd', a.astype(v.dtype), v)

    out = lax.map(block, (jnp.arange(nc), q_c))
    return out.transpose(1, 2, 0, 3, 4).reshape(b, h, s, d)


def causal_depthwise_conv(u, w, bias):
    s = u.shape[1]
    up = jnp.pad(u, ((0, 0), (CONV_WIDTH - 1, 0), (0, 0)))
    out = up[:, 0:s] * w[0]
    for i in range(1, CONV_WIDTH):
        out = out + up[:, i:i + s] * w[i]
    return out + bias


def setup_inputs(seed: int = 0) -> dict:
    key = jax.random.key(seed)
    ks = jax.random.split(key, 14)
    f32 = jnp.float32
    nrm = lambda k, shape, fan: jax.random.normal(k, shape, f32) * (fan ** -0.5)
    return {
        "x": jax.random.normal(ks[0], (BATCH, SEQ, D_MODEL), f32),
        "g_mix": 1.0 + 0.01 * jax.random.normal(ks[1], (DEPTH, D_MODEL), f32),
        "w_in": nrm(ks[2], (DEPTH, D_MODEL, IN_COLS), D_MODEL),
        "b_gate": 0.01 * jax.random.normal(ks[3], (DEPTH, N_BRANCHES * D_MODEL), f32),
        "w_branch_a": nrm(ks[4], (DEPTH, MOBA_WIDTH, D_MODEL), MOBA_WIDTH),
        "w_branch_b": nrm(ks[5], (DEPTH, SB_WIDTH, D_MODEL), SB_WIDTH),
        "w_out": nrm(ks[6], (DEPTH, D_MODEL, D_MODEL), D_MODEL),
        "g_ffn": 1.0 + 0.01 * jax.random.normal(ks[7], (DEPTH, D_MODEL), f32),
        "w_up": nrm(ks[8], (DEPTH, D_MODEL, 2 * D_FF), D_MODEL),
        "conv_w": nrm(ks[9], (DEPTH, CONV_WIDTH, 2 * D_FF), CONV_WIDTH),
        "conv_b": 0.01 * jax.random.normal(ks[10], (DEPTH, 2 * D_FF), f32),
        "w_down": nrm(ks[11], (DEPTH, D_FF, D_MODEL), D_FF),
        "g_final": 1.0 + 0.01 * jax.random.normal(ks[12], (D_MODEL,), f32),
    }


def reference(x, g_mix, w_in, b_gate, w_branch_a, w_branch_b, w_out, g_ffn, w_up, conv_w, conv_b, w_down, g_final):
    b, s, _ = x.shape
    pos = jnp.arange(s, dtype=jnp.float32)
    for layer in range(DEPTH):
        h = rms_norm(x, g_mix[layer])
        proj = h @ w_in[layer]
        qa, ka, va, qb, kb, vb, gates = jnp.split(proj, IN_SPLITS, axis=-1)
        qa = rope(to_heads(qa, MOBA_HEADS), pos)
        ka = rope(to_heads(ka, MOBA_HEADS), pos)
        ya = from_heads(moba_attention(qa, ka, to_heads(va, MOBA_HEADS)))
        yb = from_heads(stick_breaking_attention(to_heads(qb, SB_HEADS), to_heads(kb, SB_HEADS),
                                                 to_heads(vb, SB_HEADS)))
        g = jax.nn.sigmoid(gates + b_gate[layer]).reshape(b, s, N_BRANCHES, D_MODEL)
        merged = g[:, :, 0] * (ya @ w_branch_a[layer]) + g[:, :, 1] * (yb @ w_branch_b[layer])
        x = x + merged @ w_out[layer]
        h = rms_norm(x, g_ffn[layer])
        u = causal_depthwise_conv(h @ w_up[layer], conv_w[layer], conv_b[layer])
        u_gate, u_val = jnp.split(u, 2, axis=-1)
        x = x + (jax.nn.silu(u_gate) * u_val) @ w_down[layer]
    return rms_norm(x, g_final)
```

```python
import contextlib
import numpy as np
import concourse.bass as bass
import concourse.mybir as mybir
from concourse.bass_utils import run_bass_kernel_spmd

F32 = mybir.dt.float32
BF16 = mybir.dt.bfloat16
AF = mybir.ActivationFunctionType
ALU = mybir.AluOpType
AX = mybir.AxisListType

HD = 64
BIG = 30000.0
NEGINF = -1e30
EPS = 1e-6


class Cfg:
    def __init__(s, D=1024, NH=8, QS=2048, F=2816, B=2):
        s.D, s.NH, s.QS, s.F, s.B = D, NH, QS, F, B
        s.KC = D // 128
        s.NG = NH // 2
        s.S = 4 * QS
        s.NT = s.S // 128
        s.NBLK = s.S // 256
        s.Q0 = s.S - QS - 128
        s.NQC = (QS + 128) // 128
        s.NQ = s.NQC * 128
        s.NF = F // 128
        s.HW = NH * HD
        s.stop = 0
        s.tiles = []
        c = 0
        while c < s.NQC:
            n = min(4, s.NQC - c)
            s.tiles.append((c, n))
            c += n


class _Rec:
    def __init__(s):
        s.call = None

    def __getattr__(s, name):
        def f(*a, **k):
            s.call = (name, a, k)
            return s
        return f


class TR:
    ENG = ('sp', 'act', 'dve', 'pool', 'pe')

    def __init__(s, nc, stack):
        s.nc = nc
        s.sem, s.cnt, s.lastw, s.readers = {}, {}, {}, {}
        s.waited = {e: {} for e in s.ENG}
        s.q = {e: [] for e in s.ENG}
        s.children = {}
        s.parent = {}
        s.stack = stack
        s.n = 0

    def alias(s, parent, kids):
        for k in kids:
            s.parent[k] = parent
            s.children.setdefault(parent, set()).add(k)

    def _rel(s, k):
        out = [k]
        if k in s.parent:
            out.append(s.parent[k])
        elif '.' in k:
            p = k.split('.')[0]
            out.append(p)
            s.children.setdefault(p, set()).add(k)
        out.extend(s.children.get(k, ()))
        return out

    def _sem(s, key):
        if key not in s.sem:
            s.sem[key] = s.stack.enter_context(s.nc.semaphore(key))
            s.cnt[key] = 0
        return s.sem[key]

    def op(s, e, fn, R=(), W=(), dma=None):
        deps = {}

        def add(ev):
            if ev is not None and ev[1] > deps.get(ev[0], 0):
                deps[ev[0]] = ev[1]
        for r in R:
            for rr in s._rel(r):
                add(s.lastw.get(rr))
        for w in W:
            for ww in s._rel(w):
                add(s.lastw.get(ww))
                for k, v in s.readers.get(ww, {}).items():
                    add((k, v))
        key, inc = ('E_' + e, 1) if dma is None else ('D_' + dma, 16)
        s._sem(key)
        waits = []
        for k, v in deps.items():
            if e == 'pe' and k == 'E_pe':
                continue
            if s.waited[e].get(k, 0) >= v:
                continue
            waits.append((s.sem[k], v))
            s.waited[e][k] = v
        rec = _Rec()
        fn(rec)
        s.cnt[key] += inc
        s.q[e].append((waits, rec.call, s.sem[key], inc))
        s.n += 1
        ev = (key, s.cnt[key])
        for r in R:
            d = s.readers.setdefault(r, {})
            d[key] = max(d.get(key, 0), ev[1])
        for w in W:
            s.lastw[w] = ev
            s.readers[w] = {}
        return ev

    def barrier(s, engines=None):
        for e in (engines or s.ENG):
            waits = []
            for k, v in s.cnt.items():
                if v > 0 and s.waited[e].get(k, 0) < v:
                    waits.append((s.sem[k], v))
                    s.waited[e][k] = v
            if waits:
                s.q[e].append((waits, None, None, 0))

    def flush(s):
        with s.nc.Block() as block:
            sect = {'sp': block.sync, 'act': block.scalar, 'dve': block.vector, 'pool': block.gpsimd,
                    'pe': block.tensor}
            for e in s.ENG:
                q = s.q[e]
                s.q[e] = []
                if not q:
                    continue

                def body(eng, q=q):
                    for waits, call, sem, inc in q:
                        for (wsem, v) in waits:
                            eng.wait_ge(wsem, v)
                        if call is not None:
                            name, a, k = call
                            getattr(eng, name)(*a, **k).then_inc(sem, inc)
                sect[e](body)


class _Stop(Exception):
    pass


def build(cfg):
    D, KC, NG, S, NT, NBLK, Q0, NQC, NQ, NF, QS = (cfg.D, cfg.KC, cfg.NG, cfg.S, cfg.NT, cfg.NBLK, cfg.Q0,
                                                  cfg.NQC, cfg.NQ, cfg.NF, cfg.QS)
    HW = cfg.HW
    NJ = HW // 128
    scale = HD ** -0.5
    NBW = min(512, D)
    NBN = D // NBW
    nc = bass.Bass("TRN2", target_bir_lowering=False)

    def din(name, shape):
        return nc.dram_tensor(name, list(shape), F32, kind="ExternalInput").ap()
    xk = din("xk", [S, D])
    wg = din("wg", [NG, 128, KC * 1024])
    cosT = din("cosT", [128, S])
    sinT = din("sinT", [128, S])
    kbias_d = din("kbias", [128, NT])
    bflag_d = din("bflag", [128, 32])
    halof_d = din("halof", [128, 1])
    ident_d = din("ident", [128, 128])
    tneg_d = din("tneg", [128, 128])
    eall_d = din("eall", [32, 32 * 128])
    masks_d = din("masks", [128, 8 * 512])
    gmix_d = din("gmix", [128, KC])
    gffn_d = din("gffn", [128, KC])
    gfin_d = din("gfin", [128, D])
    wgate_d = din("wgate", [128, 2 * KC, KC * 128])
    bgate_d = din("bgate", [128, 2 * KC])
    wa_d = din("wa", [128, KC, NJ * 128])
    wb_d = din("wb", [128, KC, NJ * 128])
    wout_d = din("wout", [128, KC * D])
    wup_d = din("wup", [128, NF, KC * 256])
    cw_d = din("cw", [128, NF * 6])
    cb_d = din("cb", [128, NF * 2])
    wdn_d = din("wdn", [128, NF * D])
    y = nc.dram_tensor("y", [QS, D], F32, kind="ExternalOutput").ap()
    hsc = nc.dram_tensor("hsc", [S // 512, 128, KC * 512], BF16).ap()
    WC_COLS = 2 * KC * KC * 128 + 2 * KC * NJ * 128 + KC * D + NF * (KC * 256 + D)
    wcache = nc.dram_tensor("wcache", [128, WC_COLS], BF16).ap()

    with contextlib.ExitStack() as stack:
        tr = TR(nc, stack)
        op = tr.op

        def sb(name, shape, dt=F32):
            return stack.enter_context(nc.sbuf_tensor(name, list(shape), dt))

        def ps(name, shape, dt=F32):
            return stack.enter_context(nc.psum_tensor(name, list(shape), dt))

        ident = sb("s_ident", [128, 128], BF16)
        tneg = sb("s_tneg", [128, 128], BF16)
        onesneg = sb("s_onesneg", [128, 128], BF16)
        eall = sb("s_eall", [32, 32 * 128], BF16)
        masks = sb("s_masks", [128, 8 * 512], BF16)
        kbias = sb("s_kbias", [128, NT])
        bflag = sb("s_bflag", [128, 32])
        halof = sb("s_halof", [128, 1])
        gmix = sb("s_gmix", [128, KC])
        gffn = sb("s_gffn", [128, KC])
        bgate = sb("s_bgate", [128, 2 * KC])
        cw = sb("s_cw", [128, NF * 6])
        cb = sb("s_cb", [128, NF * 2])
        stg = [sb("s_stg0", [128, 2048])]
        stg_i = [0]

        def load_cast(dst, dst_key, src_ap, n, cast_eng='dve'):
            i = stg_i[0] % len(stg)
            stg_i[0] += 1
            st = stg[i]
            op('sp', lambda e: e.dma_start(out=st[:, 0:n], in_=src_ap), W=['stg%d' % i], dma='stg%d' % i)
            if cast_eng == 'act':
                op('act', lambda e: e.activation(out=dst, in_=st[:, 0:n], func=AF.Copy), R=['stg%d' % i], W=[dst_key])
            else:
                op(cast_eng, lambda e: e.tensor_copy(out=dst, in_=st[:, 0:n]), R=['stg%d' % i], W=[dst_key])

        def load_f32(dst, key, src_ap):
            op('sp', lambda e: e.dma_start(out=dst, in_=src_ap), W=[key], dma=key)

        load_cast(ident[:, :], 'ident', ident_d, 128)
        load_cast(tneg[:, :], 'tneg', tneg_d, 128)
        for i in range(4):
            load_cast(masks[:, i * 1024:(i + 1) * 1024], 'masks', masks_d[:, i * 1024:(i + 1) * 1024], 1024)
        op('dve', lambda e: e.memset(onesneg[:, :], -1.0), W=['onesneg'])
        for hh in range(2):
            i = stg_i[0] % len(stg)
            stg_i[0] += 1
            op('sp', lambda e, i=i, hh=hh: e.dma_start(out=stg[i][0:32, 0:2048], in_=eall_d[:, hh * 2048:(hh + 1) * 2048]),
               W=['stg%d' % i], dma='stg%d' % i)
            op('dve', lambda e, i=i, hh=hh: e.tensor_copy(out=eall[:, hh * 2048:(hh + 1) * 2048], in_=stg[i][0:32, 0:2048]),
               R=['stg%d' % i], W=['eall'])
        load_f32(kbias[:, :], 'kbias', kbias_d)
        load_f32(bflag[:, :], 'bflag', bflag_d)
        load_f32(halof[:, :], 'halof', halof_d)
        load_f32(gmix[:, :], 'gmix', gmix_d)
        load_f32(gffn[:, :], 'gffn', gffn_d)
        load_f32(bgate[:, :], 'bgate', bgate_d)
        load_f32(cw[:, :], 'cw', cw_d)
        load_f32(cb[:, :], 'cb', cb_d)

        yaT = sb("s_yaT", [128, NJ, NQ], BF16)
        ybT = sb("s_ybT", [128, NJ, NQ], BF16)

        xc = [sb("s_xc0", [128, D]), sb("s_xc1", [128, D])]
        junk = sb("s_junk", [128, D], BF16)
        hn2 = [sb("s_hn", [128, D], BF16), sb("s_hn1", [128, D], BF16)]
        st42 = [sb("s_st4", [128, 4]), sb("s_st41", [128, 4])]
        st4 = st42[0]
        nrm_i = [0]
        tp_ps = ps("tp_ps", [128, max(KC * 128, 512)], BF16)
        xc_i = [0]

        def norm_A(src_key, src):
            ni = nrm_i[0] % 2
            nrm_i[0] += 1
            st4, hn = st42[ni], hn2[ni]
            sk, hk, jk = 'st4_%d' % ni, 'hn_%d' % ni, 'junk'
            op('dve', lambda e: e.scalar_tensor_tensor(out=junk[:, :], in0=src, scalar=1.0, in1=src,
                                                       op0=ALU.mult, op1=ALU.mult, accum_out=st4[:, 0:1]),
               R=[src_key], W=[jk, sk])
            op('dve', lambda e: e.tensor_scalar(out=st4[:, 1:2], in0=st4[:, 0:1], scalar1=1.0 / D, scalar2=EPS,
                                                op0=ALU.mult, op1=ALU.add), R=[sk], W=[sk])
            op('act', lambda e: e.activation(out=st4[:, 2:3], in_=st4[:, 1:2], func=AF.Ln), R=[sk], W=[sk])
            op('act', lambda e: e.activation(out=st4[:, 3:4], in_=st4[:, 2:3], func=AF.Exp, scale=-0.5),
               R=[sk], W=[sk])
            op('act', lambda e: e.activation(out=hn[:, :], in_=src, func=AF.Identity, scale=st4[:, 3:4]),
               R=[src_key, sk], W=[hk])
            return (hn, hk)

        def norm_B(state, gt, gkey, dst_fn, dst_key):
            hn, hk = state
            for kc in range(KC):
                op('pe', lambda e, kc=kc: e.transpose(out=tp_ps[:, kc * 128:(kc + 1) * 128],
                                                     in_=hn[:, kc * 128:(kc + 1) * 128], identity=ident[:, :]),
                   R=[hk, 'ident'], W=['tp_ps'])
            for kc in range(KC):
                op('dve', lambda e, kc=kc: e.tensor_scalar(out=dst_fn(kc), in0=tp_ps[:, kc * 128:(kc + 1) * 128],
                                                           scalar1=gt[:, kc:kc + 1], scalar2=None, op0=ALU.mult),
                   R=['tp_ps', gkey], W=[dst_key])

        def norm_T(src_key, src, gt, gkey, dst_fn, dst_key):
            norm_B(norm_A(src_key, src), gt, gkey, dst_fn, dst_key)

        def load_x_chunk(t):
            i = xc_i[0] % 2
            xc_i[0] += 1
            op('sp', lambda e: e.dma_start(out=xc[i][:, :], in_=xk[t * 128:(t + 1) * 128, :]),
               W=['xc%d' % i], dma='xc%d' % i)
            return i

        def chk(k):
            if cfg.stop == k:
                raise _Stop()

        with contextlib.ExitStack() as st1:
          try:
              def sb1(name, shape, dt=F32):
                  return st1.enter_context(nc.sbuf_tensor(name, list(shape), dt))

              def ps1(name, shape, dt=F32):
                  return st1.enter_context(nc.psum_tensor(name, list(shape), dt))
              wgb = sb1("s_wgb", [128, KC * 1024], BF16)
              hT = sb1("s_hT", [128, KC, 512], BF16)
              cs = sb1("s_cs", [128, 512])
              sn = sb1("s_sn", [128, 512])
              kaT = sb1("s_kaT", [128, S], BF16)
              kbT = sb1("s_kbT", [128, S], BF16)
              vA = sb1("s_vA", [128, NT, 2 * 66], BF16)
              vB = sb1("s_vB", [128, NT, 128], BF16)
              qaT = sb1("s_qaT", [128, NQ], BF16)
              qbT = sb1("s_qbT", [128, NQ], BF16)
              km = sb1("s_km", [128, 32])
              kmb = sb1("s_kmb", [128, 32], BF16)
              gm = sb1("s_gm", [128, 32])
              m8 = sb1("s_m8", [128, 8])
              sel = sb1("s_sel", [128, 32])
              sel2 = sb1("s_sel2", [128, 32])
              biasq = sb1("s_biasq", [128, 32], BF16)
              biasT = [sb1("s_biasT0", [32, NQ], BF16), sb1("s_biasT1", [32, NQ], BF16)]
              e2p = [sb1("s_e2a", [128, 1024]), sb1("s_e2b", [128, 1024])]
              ebuf = [e2p[0][:, 0:512], e2p[0][:, 512:1024]]
              rt1, rt2 = ebuf[0], ebuf[1]
              tr.alias('e2a', ['e0', 'e1'])
              bfp = [sb1("s_bp%d" % q, [128, 1024], BF16) for q in range(5)]
              ytok = sb1("s_ytok", [128, 4, 128], BF16)
              rden = sb1("s_rden", [128, 1])
              pj2 = ps1("pj2", [128, 1024])
              sc2 = ps1("sc2", [128, 1024])
              af2 = ps1("af2", [128, 1024])
              acc_ps = ps1("acc", [128, 512])
              pj = [pj2[:, 0:512], pj2[:, 512:1024]]
              sc_ps = [sc2[:, 0:512], sc2[:, 512:1024]]
              af_ps = af2[:, 0:512]
              sm_ps = tp_ps[:, 0:512]
              tr.alias('pj2', ['pj0', 'pj1'])
              tr.alias('sc2', ['sc0', 'sc1'])
              tr.alias('af2', ['af', 'af.1'])
              tr.alias('tp_ps', ['sm'])
              chk(100)
              for hh in range(2):
                  op('dve', lambda e, hh=hh: e.memset(vA[:, :, hh * 66 + 64: hh * 66 + 65], 1.0), W=['vA'])

              pj_i = [0]

              def proj(col0, ncols_tok, tok_off):
                  i = pj_i[0] % 2
                  pj_i[0] += 1
                  for kc in range(KC):
                      op('pe', lambda e, kc=kc: e.matmul(pj[i][:, 0:ncols_tok],
                                                         lhsT=wgb[:, kc * 1024 + col0: kc * 1024 + col0 + 128],
                                                         rhs=hT[:, kc, tok_off:tok_off + ncols_tok],
                                                         start=(kc == 0), stop=(kc == KC - 1)),
                         R=['wgb', 'hT'], W=['pj%d' % i])
                  return i

              def rope_to(dst, dst_key, c_main, c_perm, n, tok_off):
                  i1 = proj(c_main, n, tok_off)
                  op('dve', lambda e: e.tensor_tensor(out=rt1[:, 0:n], in0=pj[i1][:, 0:n],
                                                      in1=cs[:, tok_off:tok_off + n], op=ALU.mult),
                     R=['pj%d' % i1, 'cs'], W=['e0'])
                  i2 = proj(c_perm, n, tok_off)
                  op('dve', lambda e: e.tensor_tensor(out=rt2[:, 0:n], in0=pj[i2][:, 0:n],
                                                      in1=sn[:, tok_off:tok_off + n], op=ALU.mult),
                     R=['pj%d' % i2, 'sn'], W=['e1'])
                  op('pool', lambda e: e.tensor_tensor(out=dst, in0=rt1[:, 0:n], in1=rt2[:, 0:n], op=ALU.add),
                     R=['e0', 'e1'], W=[dst_key])

              chk(1)
              for g in range(NG):
                  for kc in range(0, KC, 2):
                      n = min(2, KC - kc) * 1024
                      load_cast(wgb[:, kc * 1024: kc * 1024 + n], 'wgb', wg[g, :, kc * 1024: kc * 1024 + n], n,
                                cast_eng='dve')
                  chk(11)
                  for tt in range(S // 512):
                      if g == 0:
                          pend = None
                          for c in range(4):
                              t = tt * 4 + c
                              i = load_x_chunk(t)
                              stt = norm_A('xc%d' % i, xc[i][:, :])
                              if pend is not None:
                                  norm_B(pend[0], gmix, 'gmix', lambda kc, c=pend[1]: hT[:, kc, c * 128:(c + 1) * 128], 'hT')
                              pend = (stt, c)
                              chk(12)
                          norm_B(pend[0], gmix, 'gmix', lambda kc, c=pend[1]: hT[:, kc, c * 128:(c + 1) * 128], 'hT')
                          op('sp', lambda e: e.dma_start(out=hsc[tt], in_=hT[:, :, :].rearrange("p k t -> p (k t)")),
                             R=['hT'], W=['hsc%d' % tt], dma='hscw')
                      else:
                          op('sp', lambda e: e.dma_start(out=hT[:, :, :].rearrange("p k t -> p (k t)"), in_=hsc[tt]),
                             R=['hsc%d' % tt], W=['hT'], dma='hTl')
                      load_f32(cs[:, :], 'cs', cosT[:, tt * 512:(tt + 1) * 512])
                      load_f32(sn[:, :], 'sn', sinT[:, tt * 512:(tt + 1) * 512])
                      tk = 'kv%d' % tt
                      rope_to(kaT[:, tt * 512:(tt + 1) * 512], 'kaT' + tk, 256, 384, 512, 0)
                      chk(13)
                      i = proj(640, 512, 0)
                      op('act', lambda e, i=i: e.activation(out=kbT[:, tt * 512:(tt + 1) * 512], in_=pj[i][:, :],
                                                            func=AF.Copy), R=['pj%d' % i], W=['kbT' + tk])
                      chk(14)
                      for c in range(4):
                          t = tt * 4 + c
                          i = pj_i[0] % 2
                          pj_i[0] += 1
                          for kc in range(KC):
                              op('pe', lambda e, kc=kc, c=c, i=i: e.matmul(
                                  pj[i][:, 0:256], lhsT=hT[:, kc, c * 128:(c + 1) * 128],
                                  rhs=wgb[:, kc * 1024 + 768: kc * 1024 + 1024],
                                  start=(kc == 0), stop=(kc == KC - 1)), R=['wgb', 'hT'], W=['pj%d' % i])
                          chk(151)
                          for hh in range(2):
                              op('act', lambda e, t=t, i=i, hh=hh: e.activation(
                                  out=vA[:, t, hh * 66: hh * 66 + 64], in_=pj[i][:, hh * 64:(hh + 1) * 64], func=AF.Copy),
                                 R=['pj%d' % i], W=['vA'])
                          chk(152)
                          op('act', lambda e, t=t, i=i: e.activation(out=vB[:, t, :], in_=pj[i][:, 128:256], func=AF.Copy),
                             R=['pj%d' % i], W=['vB'])
                      chk(15)
                      lo = max(tt * 512, Q0)
                      if lo < (tt + 1) * 512:
                          off = lo - tt * 512
                          n = 512 - off
                          rope_to(qaT[:, lo - Q0: lo - Q0 + n], 'qaT', 0, 128, n, off)
                          i = proj(512, n, off)
                          op('act', lambda e, i=i, lo=lo, n=n: e.activation(out=qbT[:, lo - Q0: lo - Q0 + n],
                                                                            in_=pj[i][:, 0:n], func=AF.Copy, scale=scale),
                             R=['pj%d' % i], W=['qbT'])
                  chk(2)
                  kv_keys = ['kv%d' % tt for tt in range(S // 512)]
                  for nb0 in range(0, NBLK, 8):
                      op('dve', lambda e, nb0=nb0: e.tensor_reduce(
                          out=km[:, nb0:nb0 + 8],
                          in_=kaT[:, nb0 * 256:(nb0 + 8) * 256].rearrange("p (n k) -> p n k", k=256),
                          op=ALU.add, axis=AX.X), R=['kaT' + k for k in kv_keys], W=['km'])
                  op('dve', lambda e: e.tensor_scalar(out=kmb[:, 0:NBLK], in0=km[:, 0:NBLK], scalar1=1.0 / 256,
                                                      scalar2=None, op0=ALU.mult), R=['km'], W=['kmb'])
                  for h in range(2):
                      hp = slice(64 * h, 64 * h + 64)
                      for qc in range(NQC):
                          Bq = (Q0 + qc * 128) // 256
                          op('pe', lambda e, qc=qc: e.matmul(sc_ps[0][:, 0:NBLK], lhsT=qaT[hp, qc * 128:(qc + 1) * 128],
                                                            rhs=kmb[hp, 0:NBLK], start=True, stop=True),
                             R=['qaT', 'kmb'], W=['sc0'])
                          op('dve', lambda e: e.memset(gm[:, :], NEGINF), W=['gm'])
                          op('dve', lambda e, Bq=Bq: e.tensor_tensor(out=gm[:, 0:Bq], in0=sc_ps[0][:, 0:Bq],
                                                                     in1=bflag[:, 0:Bq], op=ALU.add),
                             R=['sc0', 'bflag'], W=['gm'])
                          op('dve', lambda e: e.max(out=m8[:, :], in_=gm[:, 0:NBLK]), R=['gm'], W=['m8'])
                          op('dve', lambda e: e.tensor_scalar(out=sel[:, :], in0=gm[:, :], scalar1=m8[:, 2:3],
                                                              scalar2=None, op0=ALU.is_ge), R=['gm', 'm8'], W=['sel'])
                          op('dve', lambda e: e.tensor_scalar(out=sel2[:, :], in0=gm[:, :], scalar1=-1e29,
                                                              scalar2=None, op0=ALU.is_gt), R=['gm'], W=['sel2'])
                          op('dve', lambda e: e.tensor_tensor(out=sel[:, :], in0=sel[:, :], in1=sel2[:, :],
                                                              op=ALU.mult), R=['sel', 'sel2'], W=['sel'])
                          op('dve', lambda e: e.tensor_scalar(out=biasq[:, :], in0=sel[:, :], scalar1=BIG,
                                                              scalar2=-BIG, op0=ALU.mult, op1=ALU.add),
                             R=['sel'], W=['biasq'])
                          op('dve', lambda e, Bq=Bq: e.memset(biasq[:, Bq:Bq + 1], 0.0), W=['biasq'])
                          op('pe', lambda e: e.transpose(out=sm_ps[0:32, 0:128], in_=biasq[:, :], identity=ident[:, :]),
                             R=['biasq', 'ident'], W=['sm'])
                          op('act', lambda e, h=h, qc=qc: e.activation(out=biasT[h][:, qc * 128:(qc + 1) * 128],
                                                                        in_=sm_ps[0:32, 0:128], func=AF.Copy),
                             R=['sm'], W=['biasT%d' % h])
                  chk(3)
                  tr.barrier()
                  hpairs = [(hT[:, 2 * j:2 * j + 2, :].rearrange("p a t -> p (a t)"), 'hTp%d' % j) for j in range(KC // 2)]
                  pairs = [(bfp[q], 'bp%d' % q) for q in range(5)] + hpairs
                  NP = len(pairs)
                  sp_b = pairs[0:3]
                  ls_b = pairs[3]
                  a_b = pairs[4:4 + min(3, NP - 4)]
                  p_b = pairs[0:3]
                  NA = len(a_b)
                  Pb = [(sc2, 'sc2'), (pj2, 'pj2')]
                  e_b = [(e2p[0], 'e2a'), (e2p[1], 'e2b')]

                  def v3(ap, W):
                      return ap.rearrange("p (h w) -> p h w", h=2)[:, :, 0:W]

                  def run_pipeline(nblocks, stages, order):
                      ns = len(stages)
                      for t in range(nblocks + ns - 1):
                          for s in order:
                              i = t - s
                              if 0 <= i < nblocks:
                                  stages[s](i)

                  for (c0, ncq) in cfg.tiles:
                      W = ncq * 128
                      q0 = Q0 + c0 * 128
                      qs = slice(c0 * 128, c0 * 128 + W)
                      kb_max = (q0 + W) // 128 - 1
                      kb_diag = q0 // 128
                      m_acc = [(acc_ps, 'acc'), (af_ps, 'af')]

                      def m_s0(kb):
                          P, Pk = Pb[kb % 2]
                          n = kb // 2
                          for h in range(2):
                              hp = slice(64 * h, 64 * h + 64)
                              op('pe', lambda e: e.matmul(P[:, h * 512: h * 512 + W], lhsT=kaT[hp, kb * 128:(kb + 1) * 128],
                                                          rhs=qaT[hp, qs], start=True, stop=False),
                                 R=['kaTkv%d' % (kb // 4), 'qaT'], W=[Pk])
                              op('pe', lambda e: e.matmul(P[:, h * 512: h * 512 + W], lhsT=eall[:, n * 128:(n + 1) * 128],
                                                          rhs=biasT[h][:, qs], start=False, stop=True),
                                 R=['eall', 'biasT%d' % h], W=[Pk])

                      def m_s1(kb):
                          P, Pk = Pb[kb % 2]
                          p, pk = p_b[kb % 3]
                          op('act', lambda e: e.activation(out=v3(p, W), in_=v3(P, W), func=AF.Exp, scale=scale),
                             R=[Pk], W=[pk])
                          if kb >= kb_diag:
                              v = 4 + (kb - kb_diag)
                              for h in range(2):
                                  op('dve', lambda e: e.tensor_tensor(out=p[:, h * 512: h * 512 + W],
                                                                      in0=p[:, h * 512: h * 512 + W],
                                                                      in1=masks[:, v * 512: v * 512 + W], op=ALU.mult),
                                     R=[pk, 'masks'], W=[pk])

                      def m_s2(kb):
                          p, pk = p_b[kb % 3]
                          for h in range(2):
                              acc, acck = m_acc[h]
                              for c in range(ncq):
                                  op('pe', lambda e, c=c: e.matmul(acc[:, c * 65:(c + 1) * 65],
                                                                   lhsT=p[:, h * 512 + c * 128: h * 512 + (c + 1) * 128],
                                                                   rhs=vA[:, kb, h * 66: h * 66 + 65],
                                                                   start=(kb == 0 and c == 0),
                                                                   stop=(kb == kb_max and c == ncq - 1)),
                                     R=[pk, 'vA'], W=[acck])
                      run_pipeline(kb_max + 1, [m_s0, m_s1, m_s2], (2, 1, 0))
                      for h in range(2):
                          acc, acck = m_acc[h]
                          for c in range(ncq):
                              op('dve', lambda e, c=c: e.reciprocal(out=rden[:, :], in_=acc[:, c * 65 + 64: c * 65 + 65]),
                                 R=[acck], W=['rden'])
                              op('dve', lambda e, c=c, h=h: e.tensor_scalar(out=ytok[:, c, 64 * h:64 * h + 64],
                                                                            in0=acc[:, c * 65: c * 65 + 64],
                                                                            scalar1=rden[:, 0:1], scalar2=None,
                                                                            op0=ALU.mult),
                                 R=[acck, 'rden'], W=['ytok'])
                      for c in range(ncq):
                          op('pe', lambda e, c=c: e.transpose(out=sm_ps[:, 128:256], in_=ytok[:, c, :], identity=ident[:, :]),
                             R=['ytok', 'ident'], W=['sm'])
                          op('act', lambda e, c=c: e.activation(out=yaT[:, g, (c0 + c) * 128:(c0 + c + 1) * 128],
                                                                in_=sm_ps[:, 128:256], func=AF.Copy),
                             R=['sm'], W=['yaT'])
                      chk(4)
                      nkb = kb_max + 1
                      AFb = [(pj2, 'pj2'), (af2, 'af2')]

                      def kbi(i):
                          return kb_max - i

                      def s_s0(i):
                          kb = kbi(i)
                          P, Pk = (sc2, 'sc2')
                          for h in range(2):
                              hp = slice(64 * h, 64 * h + 64)
                              op('pe', lambda e: e.matmul(P[:, h * 512: h * 512 + W], lhsT=kbT[hp, kb * 128:(kb + 1) * 128],
                                                          rhs=qbT[hp, qs], start=True, stop=True),
                                 R=['kbTkv%d' % (kb // 4), 'qbT'], W=[Pk])

                      def s_s1(i):
                          kb = kbi(i)
                          P, Pk = (sc2, 'sc2')
                          eb, ek = e_b[i % 2]
                          op('act', lambda e: e.activation(out=v3(eb, W), in_=v3(P, W), func=AF.Exp,
                                                           bias=kbias[:, kb:kb + 1]), R=[Pk, 'kbias'], W=[ek])
                          if kb >= kb_diag:
                              v = kb - kb_diag
                              for h in range(2):
                                  op('dve', lambda e: e.tensor_tensor(out=eb[:, h * 512: h * 512 + W],
                                                                      in0=eb[:, h * 512: h * 512 + W],
                                                                      in1=masks[:, v * 512: v * 512 + W], op=ALU.mult),
                                     R=[ek, 'masks'], W=[ek])

                      def s_s2(i):
                          eb, ek = e_b[i % 2]
                          spb, spk = sp_b[i % 3]
                          op('act', lambda e: e.activation(out=v3(spb, W), in_=v3(eb, W), func=AF.Ln, bias=1.0),
                             R=[ek], W=[spk])

                      def s_s3(i):
                          kb = kbi(i)
                          first = (i == 0)
                          spb, spk = sp_b[i % 3]
                          ls, lsk = ls_b
                          afb, afk = AFb[i % 2]
                          for h in range(2):
                              hp = slice(64 * h, 64 * h + 64)
                              hs = slice(h * 512, h * 512 + W)
                              op('pe', lambda e: e.matmul(afb[:, hs], lhsT=tneg[:, :], rhs=spb[:, hs], start=True, stop=False),
                                 R=['tneg', spk], W=[afk])
                              if not first:
                                  op('pe', lambda e: e.matmul(afb[:, hs], lhsT=onesneg[:, :], rhs=ls[:, hs], start=False,
                                                              stop=False), R=['onesneg', lsk], W=[afk])
                              op('pe', lambda e: e.matmul(afb[:, hs], lhsT=kbT[hp, kb * 128:(kb + 1) * 128], rhs=qbT[hp, qs],
                                                          start=False, stop=True),
                                 R=['kbTkv%d' % (kb // 4), 'qbT'], W=[afk])
                          if kb != 0:
                              if first:
                                  op('dve', lambda e: e.tensor_copy(out=v3(ls, W), in_=v3(spb, W)), R=[spk], W=[lsk])
                              else:
                                  op('pool', lambda e: e.tensor_tensor(out=v3(ls, W), in0=v3(ls, W), in1=v3(spb, W),
                                                                       op=ALU.add), R=[spk, lsk], W=[lsk])

                      def s_s4(i):
                          kb = kbi(i)
                          ab, ak = a_b[i % NA]
                          afb, afk = AFb[i % 2]
                          op('act', lambda e: e.activation(out=v3(ab, W), in_=v3(afb, W), func=AF.Exp,
                                                           bias=kbias[:, kb:kb + 1]), R=[afk, 'kbias'], W=[ak])
                          if kb >= kb_diag:
                              v = kb - kb_diag
                              for h in range(2):
                                  op('dve', lambda e: e.tensor_tensor(out=ab[:, h * 512: h * 512 + W],
                                                                      in0=ab[:, h * 512: h * 512 + W],
                                                                      in1=masks[:, v * 512: v * 512 + W], op=ALU.mult),
                                     R=[ak, 'masks'], W=[ak])

                      def s_s5(i):
                          kb = kbi(i)
                          ab, ak = a_b[i % NA]
                          for h in range(2):
                              hp = slice(64 * h, 64 * h + 64)
                              op('pe', lambda e: e.matmul(acc_ps[hp, 0:W], lhsT=vB[:, kb, 64 * h:64 * h + 64],
                                                          rhs=ab[:, h * 512: h * 512 + W], start=(i == 0), stop=(i == nkb - 1)),
                                 R=['vB', ak], W=['acc'])
                      run_pipeline(nkb, [s_s0, s_s1, s_s2, s_s3, s_s4, s_s5], (5, 4, 2, 1, 0, 3))
                      op('act', lambda e: e.activation(out=ybT[:, g, qs], in_=acc_ps[:, 0:W], func=AF.Copy),
                         R=['acc'], W=['ybT'])
                  tr.barrier()

          except _Stop:
            pass
          if 0 < cfg.stop < 200:
            tr.barrier(engines=['sp'])
          tr.flush()
        if 0 < cfg.stop < 200:
            return nc
        tr.barrier()
        with contextlib.ExitStack() as st2:
          try:
              def sb2(name, shape, dt=F32):
                  return st2.enter_context(nc.sbuf_tensor(name, list(shape), dt))

              def ps2(name, shape, dt=F32):
                  return st2.enter_context(nc.psum_tensor(name, list(shape), dt))
              stg.append(sb2("s_stg1", [128, 2048]))
              x1 = sb2("s_x1", [128, 4, D])
              hTo = sb2("s_hTo", [128, KC, 512], BF16)
              gT = sb2("s_gT", [128, 2 * KC, 512], BF16)
              mT = sb2("s_mT", [128, KC, 512], BF16)
              mt1 = sb2("s_mt1", [128, 512])
              mt2 = sb2("s_mt2", [128, 512])
              wsm = [sb2("s_wsm%d" % q, [128, 1024], BF16) for q in range(4)]
              woutb = sb2("s_woutb", [128, KC * D], BF16)
              h2T = sb2("s_h2T", [128, KC, 512], BF16)
              ur = [sb2("s_ur0", [128, 2 + 512]), sb2("s_ur1", [128, 2 + 512])]
              carry = sb2("s_carry", [128, NF * 2, 2])
              cv = [sb2("s_cv0", [128, 512]), sb2("s_cv1", [128, 512])]
              sg = sb2("s_sg", [128, 512])
              urs = [[(ur[0], 'ur0'), (ur[1], 'ur1')],
                     [(stg[1][:, 0:514], 'stg1.a'), (stg[1][:, 514:1028], 'stg1.b')]]
              cvs = [[(cv[0], 'cv0'), (cv[1], 'cv1')],
                     [(stg[0][:, 0:512], 'stg0.a'), (stg[0][:, 512:1024], 'stg0.b')]]
              sgs = [(sg, 'sg'), (stg[0][:, 1024:1536], 'stg0.c')]
              prod = sb2("s_prod", [128, 2, 512], BF16)
              wup = [sb2("s_wup%d" % q, [128, KC * 256], BF16) for q in range(4)]
              wdn = [sb2("s_wdn%d" % q, [128, D], BF16) for q in range(4)]
              gfin = sb2("s_gfin", [128, D])
              st4f = sb2("s_st4f", [128, 4])
              ob = [xc[0], xc[1]]
              g_ps = [ps2("g0", [128, 512]), ps2("g1", [128, 512])]
              b_ps = [ps2("b0", [128, 512]), ps2("b1", [128, 512])]
              o_ps = ps2("o_ps", [128, 1024])
              wsm_i = [0]
              load_f32(gfin[:, :], 'gfin', gfin_d)
              chk(200)

              wc_off = {}
              wc_next = [0]

              def cached_load(dst, dst_key, src_ap, n, name, first, eng='act'):
                  if name not in wc_off:
                      wc_off[name] = wc_next[0]
                      wc_next[0] += n
                  o = wc_off[name]
                  if first:
                      load_cast(dst, dst_key, src_ap, n, cast_eng=eng)
                      op('sp', lambda e: e.dma_start(out=wcache[:, o:o + n], in_=dst), R=[dst_key], W=['wc_' + name],
                         dma='wcw_' + dst_key)
                  else:
                      op('sp', lambda e: e.dma_start(out=dst, in_=wcache[:, o:o + n]), R=['wc_' + name], W=[dst_key],
                         dma='L_' + dst_key)

              def wload(src_ap, n, name, first, eng='act'):
                  i = wsm_i[0] % 4
                  wsm_i[0] += 1
                  cached_load(wsm[i][:, 0:n], 'wsm%d' % i, src_ap, n, name, first, eng=eng)
                  return i

              gi = [0]
              ob_i = [0]
              for ti, (c0, ncq) in enumerate(cfg.tiles):
                  W = ncq * 128
                  qs = slice(c0 * 128, c0 * 128 + W)
                  x1k = ['x1_%d' % c for c in range(ncq)]
                  for c in range(ncq):
                      t = Q0 // 128 + c0 + c
                      op('sp', lambda e, c=c, t=t: e.dma_start(out=x1[:, c, :], in_=xk[t * 128:(t + 1) * 128, :]),
                         W=[x1k[c]], dma=x1k[c])
                      norm_T(x1k[c], x1[:, c, :], gmix, 'gmix',
                             lambda kc, c=c: hTo[:, kc, c * 128:(c + 1) * 128], 'hTo')
                      chk(201)
                  wst = min(2048, KC * D)
                  for q4 in range(0, KC * D, wst):
                      cached_load(woutb[:, q4:q4 + wst], 'woutb', wout_d[:, q4:q4 + wst], wst, 'wout%d' % q4, ti == 0)
                  for fc in range(2 * KC):
                      wi = wload(wgate_d[:, fc, :], KC * 128, 'wg%d' % fc, ti == 0)
                      b = gi[0] % 2
                      gi[0] += 1
                      for kc in range(KC):
                          op('pe', lambda e, kc=kc, wi=wi, b=b: e.matmul(g_ps[b][:, 0:W], lhsT=wsm[wi][:, kc * 128:(kc + 1) * 128],
                                                                         rhs=hTo[:, kc, 0:W], start=(kc == 0),
                                                                         stop=(kc == KC - 1)),
                             R=['wsm%d' % wi, 'hTo'], W=['g%d' % b])
                      op('act', lambda e, fc=fc, b=b: e.activation(out=gT[:, fc, 0:W], in_=g_ps[b][:, 0:W],
                                                                   func=AF.Sigmoid, bias=bgate[:, fc:fc + 1]),
                         R=['g%d' % b, 'bgate'], W=['gT'])
                      chk(202)
                  for fc in range(KC):
                      wia = wload(wa_d[:, fc, :], NJ * 128, 'wa%d' % fc, ti == 0, eng='dve')
                      wib = wload(wb_d[:, fc, :], NJ * 128, 'wb%d' % fc, ti == 0, eng='dve')
                      b = gi[0] % 2
                      gi[0] += 1
                      for j in range(NJ):
                          op('pe', lambda e, j=j, wia=wia, b=b: e.matmul(g_ps[b][:, 0:W], lhsT=wsm[wia][:, j * 128:(j + 1) * 128],
                                                                         rhs=yaT[:, j, qs], start=(j == 0), stop=(j == NJ - 1)),
                             R=['wsm%d' % wia, 'yaT'], W=['g%d' % b])
                      for j in range(NJ):
                          op('pe', lambda e, j=j, wib=wib, b=b: e.matmul(b_ps[b][:, 0:W], lhsT=wsm[wib][:, j * 128:(j + 1) * 128],
                                                                         rhs=ybT[:, j, qs], start=(j == 0), stop=(j == NJ - 1)),
                             R=['wsm%d' % wib, 'ybT'], W=['b%d' % b])
                      op('dve', lambda e, fc=fc, b=b: e.tensor_tensor(out=mt1[:, 0:W], in0=g_ps[b][:, 0:W],
                                                                      in1=gT[:, fc, 0:W], op=ALU.mult),
                         R=['g%d' % b, 'gT'], W=['mt1'])
                      op('dve', lambda e, fc=fc, b=b: e.tensor_tensor(out=mt2[:, 0:W], in0=b_ps[b][:, 0:W],
                                                                      in1=gT[:, KC + fc, 0:W], op=ALU.mult),
                         R=['b%d' % b, 'gT'], W=['mt2'])
                      op('pool', lambda e, fc=fc: e.tensor_tensor(out=mT[:, fc, 0:W], in0=mt1[:, 0:W], in1=mt2[:, 0:W],
                                                                  op=ALU.add), R=['mt1', 'mt2'], W=['mT'])
                      chk(203)
                  for c in range(ncq):
                      for nb in range(NBN):
                          for kc in range(KC):
                              op('pe', lambda e, kc=kc, c=c, nb=nb: e.matmul(
                                  o_ps[:, nb * NBW:(nb + 1) * NBW], lhsT=mT[:, kc, c * 128:(c + 1) * 128],
                                  rhs=woutb[:, kc * D + nb * NBW: kc * D + nb * NBW + NBW], start=(kc == 0),
                                  stop=(kc == KC - 1)), R=['woutb', 'mT'], W=['o_ps'])
                      op('dve', lambda e, c=c: e.tensor_tensor(out=x1[:, c, :], in0=o_ps[:, 0:D], in1=x1[:, c, :],
                                                               op=ALU.add), R=['o_ps', x1k[c]], W=[x1k[c]])
                      norm_T(x1k[c], x1[:, c, :], gffn, 'gffn',
                             lambda kc, c=c: h2T[:, kc, c * 128:(c + 1) * 128], 'h2T')
                      chk(204)
                  gbanks = [(g_ps[0], 'g0'), (g_ps[1], 'g1'), (b_ps[0], 'b0'), (b_ps[1], 'b1')]
                  if ti == 0:
                      cv3 = [cvs[0], [(mt1, 'mt1'), (mt2, 'mt2')]]
                      ur_s, sg_s = [urs[0]], [sgs[0]]
                  else:
                      cv3 = cvs + [[(mt1, 'mt1'), (mt2, 'mt2')]]
                      ur_s, sg_s = urs, sgs
                  NCV, NUR = len(cv3), len(ur_s)
                  prod4 = [(masks[:, q * 512:(q + 1) * 512], 'masks.p%d' % q) for q in range(4)]
                  wdn6 = [(wdn[q][:, 0:D], 'wdn%d' % q) for q in range(4)]
                  if D <= 1024:
                      wdn6 += [(masks[:, 2048 + q * 1024: 2048 + q * 1024 + D], 'masks.w%d' % q) for q in range(2)]
                  NWD = len(wdn6)

                  def f_load(f):
                      if f >= NF:
                          return
                      cached_load(wup[f % 4][:, :], 'wup%d' % (f % 4), wup_d[:, f, :], KC * 256, 'wup%d' % f, ti == 0,
                                  eng=('act' if f % 2 == 0 else 'dve'))

                  def f_load_dn(f):
                      wd, wdk = wdn6[f % NWD]
                      cached_load(wd, wdk, wdn_d[:, f * D:(f + 1) * D], D, 'wdn%d' % f, ti == 0,
                                  eng=('dve' if f % 2 == 0 else 'act'))

                  def f_s0(f):
                      f_load(f + 2)
                      for gv in range(2):
                          gb, gbk = gbanks[(f % 2) * 2 + gv]
                          for kc in range(KC):
                              op('pe', lambda e, kc=kc: e.matmul(
                                  gb[:, 0:W], lhsT=wup[f % 4][:, kc * 256 + gv * 128: kc * 256 + gv * 128 + 128],
                                  rhs=h2T[:, kc, 0:W], start=(kc == 0), stop=(kc == KC - 1)),
                                 R=['wup%d' % (f % 4), 'h2T'], W=[gbk])

                  def f_s1(f):
                      for gv in range(2):
                          gb, gbk = gbanks[(f % 2) * 2 + gv]
                          u, uk = ur_s[f % NUR][gv]
                          cidx = f * 2 + gv
                          if ti == 0:
                              op('dve', lambda e: e.memset(u[:, 0:2], 0.0), W=[uk])
                          else:
                              op('dve', lambda e: e.tensor_copy(out=u[:, 0:2], in_=carry[:, cidx, :]),
                                 R=['carry.%d' % cidx], W=[uk])
                          op('act', lambda e: e.activation(out=u[:, 2:2 + W], in_=gb[:, 0:W], func=AF.Copy),
                             R=[gbk], W=[uk])
                          if ti == 0:
                              op('dve', lambda e: e.tensor_scalar(out=u[:, 2:130], in0=u[:, 2:130], scalar1=halof[:, 0:1],
                                                                  scalar2=None, op0=ALU.mult), R=[uk, 'halof'], W=[uk])

                  def f_s2(f):
                      f_load_dn(f)
                      for gv in range(2):
                          u, uk = ur_s[f % NUR][gv]
                          cvt, cvk = cv3[f % NCV][gv]
                          cidx = f * 2 + gv
                          ci = f * 6 + gv * 3
                          op('dve', lambda e: e.tensor_copy(out=carry[:, cidx, :], in_=u[:, W:W + 2]),
                             R=[uk], W=['carry.%d' % cidx])
                          op('dve', lambda e: e.tensor_scalar(
                              out=cvt[:, 0:W], in0=u[:, 2:2 + W], scalar1=cw[:, ci + 2:ci + 3],
                              scalar2=cb[:, f * 2 + gv:f * 2 + gv + 1], op0=ALU.mult, op1=ALU.add),
                             R=[uk, 'cw', 'cb'], W=[cvk])
                          op('dve', lambda e: e.scalar_tensor_tensor(
                              out=cvt[:, 0:W], in0=u[:, 1:1 + W], scalar=cw[:, ci + 1:ci + 2], in1=cvt[:, 0:W],
                              op0=ALU.mult, op1=ALU.add), R=[uk, 'cw', cvk], W=[cvk])
                          op('dve', lambda e: e.scalar_tensor_tensor(
                              out=cvt[:, 0:W], in0=u[:, 0:W], scalar=cw[:, ci:ci + 1], in1=cvt[:, 0:W],
                              op0=ALU.mult, op1=ALU.add), R=[uk, 'cw', cvk], W=[cvk])

                  def f_s3(f):
                      sgt, sgk = sg_s[f % NUR]
                      cg, cgk = cv3[f % NCV][0]
                      op('act', lambda e: e.activation(out=sgt[:, 0:W], in_=cg[:, 0:W], func=AF.Sigmoid), R=[cgk], W=[sgk])
                      op('pool', lambda e: e.tensor_tensor(out=sgt[:, 0:W], in0=sgt[:, 0:W], in1=cg[:, 0:W], op=ALU.mult),
                         R=[sgk, cgk], W=[sgk])

                  def f_s4(f):
                      sgt, sgk = sg_s[f % NUR]
                      cvv, cvvk = cv3[f % NCV][1]
                      pr, prk = prod4[f % 4]
                      op('dve', lambda e: e.tensor_tensor(out=pr[:, 0:W], in0=sgt[:, 0:W], in1=cvv[:, 0:W], op=ALU.mult),
                         R=[sgk, cvvk], W=[prk])

                  def f_s5(f):
                      if f % 2 == 0 and f != NF - 1:
                          return
                      fls = [f - 1, f] if f % 2 == 1 else [f]
                      for c in range(ncq):
                          if c0 + c == 0:
                              continue
                          for nb in range(NBN):
                              for q, ff in enumerate(fls):
                                  pr, prk = prod4[ff % 4]
                                  wd, wdk = wdn6[ff % NWD]
                                  op('pe', lambda e: e.matmul(o_ps[:, nb * NBW:(nb + 1) * NBW],
                                                              lhsT=pr[:, c * 128:(c + 1) * 128],
                                                              rhs=wd[:, nb * NBW:(nb + 1) * NBW], start=(q == 0),
                                                              stop=(q == len(fls) - 1)), R=[prk, wdk], W=['o_ps'])
                          op('dve', lambda e: e.tensor_tensor(out=x1[:, c, :], in0=o_ps[:, 0:D], in1=x1[:, c, :], op=ALU.add),
                             R=['o_ps', x1k[c]], W=[x1k[c]])

                  f_load(0)
                  f_load(1)
                  stages = [f_s0, f_s1, f_s2, f_s3, f_s4, f_s5]
                  for t in range(NF + len(stages) - 1):
                      for s in range(len(stages) - 1, -1, -1):
                          f = t - s
                          if 0 <= f < NF:
                              stages[s](f)
                  chk(206)
                  if ti == 0:
                      tr.barrier(engines=['sp'])
                  chk(206)
                  for c in range(ncq):
                      if c0 + c == 0:
                          continue
                      i = ob_i[0] % 2
                      ob_i[0] += 1
                      src = x1[:, c, :]
                      op('dve', lambda e, src=src: e.scalar_tensor_tensor(out=junk[:, :], in0=src, scalar=1.0, in1=src,
                                                                          op0=ALU.mult, op1=ALU.mult, accum_out=st4f[:, 0:1]),
                         R=[x1k[c]], W=['junk', 'st4f'])
                      op('dve', lambda e: e.tensor_scalar(out=st4f[:, 1:2], in0=st4f[:, 0:1], scalar1=1.0 / D, scalar2=EPS,
                                                          op0=ALU.mult, op1=ALU.add), R=['st4f'], W=['st4f'])
                      op('act', lambda e: e.activation(out=st4f[:, 2:3], in_=st4f[:, 1:2], func=AF.Ln), R=['st4f'], W=['st4f'])
                      op('act', lambda e: e.activation(out=st4f[:, 3:4], in_=st4f[:, 2:3], func=AF.Exp, scale=-0.5),
                         R=['st4f'], W=['st4f'])
                      op('dve', lambda e, src=src, i=i: e.scalar_tensor_tensor(out=ob[i][:, :], in0=src, scalar=st4f[:, 3:4],
                                                                               in1=gfin[:, :], op0=ALU.mult, op1=ALU.mult),
                         R=[x1k[c], 'st4f', 'gfin'], W=['xc%d' % i])
                      r0 = (c0 + c - 1) * 128
                      op('sp', lambda e, r0=r0, i=i: e.dma_start(out=y[r0:r0 + 128, :], in_=ob[i][:, :]),
                         R=['xc%d' % i], W=['y%d' % i], dma='y%d' % i)
                      chk(207)
          except _Stop:
            pass
          tr.barrier(engines=['sp'])
          tr.flush()
    return nc


def _fm(v, nchunk):
    return np.ascontiguousarray(np.asarray(v, np.float32).reshape(nchunk, 128).T)


def make_core_inputs(cfg, inputs, theta=10000.0):
    D, KC, NG, S, NT, NBLK, QS, NF, F = cfg.D, cfg.KC, cfg.NG, cfg.S, cfg.NT, cfg.NBLK, cfg.QS, cfg.NF, cfg.F
    HW = cfg.HW
    NJ = HW // 128
    f32 = np.float32
    x = np.asarray(inputs["x"], f32)
    w_in = np.asarray(inputs["w_in"], f32)[0]
    perm64 = np.concatenate([np.arange(32, 64), np.arange(0, 32)])
    offs = dict(qa=0, ka=HW, va=2 * HW, qb=3 * HW, kb=4 * HW, vb=5 * HW)
    wg = np.zeros((NG, 128, KC * 1024), f32)
    for g in range(NG):
        cols = []
        base = np.arange(g * 128, (g + 1) * 128)
        pbase = np.concatenate([g * 128 + perm64, g * 128 + 64 + perm64])
        for nm, idx in (("qa", base), ("qa", pbase), ("ka", base), ("ka", pbase), ("qb", base), ("kb", base),
                        ("va", base), ("vb", base)):
            cols.append(offs[nm] + idx)
        cols = np.concatenate(cols)
        wsel = w_in[:, cols]
        wg[g] = wsel.reshape(KC, 128, 1024).transpose(1, 0, 2).reshape(128, KC * 1024)
    wgate_full = w_in[:, 6 * HW:]
    wgate = wgate_full.reshape(KC, 128, 2 * KC, 128).transpose(1, 2, 0, 3).reshape(128, 2 * KC, KC * 128)
    bgate = _fm(np.asarray(inputs["b_gate"], f32)[0], 2 * KC)
    wa = np.asarray(inputs["w_branch_a"], f32)[0].reshape(NJ, 128, KC, 128).transpose(1, 2, 0, 3).reshape(128, KC, NJ * 128)
    wb = np.asarray(inputs["w_branch_b"], f32)[0].reshape(NJ, 128, KC, 128).transpose(1, 2, 0, 3).reshape(128, KC, NJ * 128)
    wout = np.asarray(inputs["w_out"], f32)[0].reshape(KC, 128, D).transpose(1, 0, 2).reshape(128, KC * D)
    w_up = np.asarray(inputs["w_up"], f32)[0]
    wup = np.zeros((128, NF, KC * 256), f32)
    wu4 = w_up.reshape(KC, 128, 2, NF, 128)
    wup = wu4.transpose(1, 3, 0, 2, 4).reshape(128, NF, KC * 256)
    conv_w = np.asarray(inputs["conv_w"], f32)[0]
    cw = conv_w.reshape(3, 2, NF, 128).transpose(3, 2, 1, 0).reshape(128, NF * 6)
    conv_b = np.asarray(inputs["conv_b"], f32)[0]
    cb = conv_b.reshape(2, NF, 128).transpose(2, 1, 0).reshape(128, NF * 2)
    wdn = np.asarray(inputs["w_down"], f32)[0].reshape(NF, 128, D).transpose(1, 0, 2).reshape(128, NF * D)
    gmix = _fm(np.asarray(inputs["g_mix"], f32)[0], KC)
    gffn = _fm(np.asarray(inputs["g_ffn"], f32)[0], KC)
    gfin = np.ascontiguousarray(np.broadcast_to(np.asarray(inputs["g_final"], f32)[None, :], (128, D)))
    ident = np.eye(128, dtype=f32)
    kk = np.arange(128)
    tneg = -(kk[:, None] >= kk[None, :]).astype(f32)
    eall = np.zeros((32, 32, 128), f32)
    for n in range(32):
        eall[n, n, :] = 1.0
    eall = eall.reshape(32, 32 * 128)
    masks = np.zeros((128, 8, 512), f32)
    qq = np.arange(512)
    for v in range(4):
        masks[:, v, :] = ((v * 128 + kk)[:, None] < qq[None, :])
        masks[:, 4 + v, :] = ((v * 128 + kk)[:, None] <= qq[None, :])
    masks = masks.reshape(128, 8 * 512)
    half = HD // 2
    inv = (theta ** (-np.arange(half, dtype=f32) / half)).astype(f32)
    shared = dict(wg=wg, wgate=np.ascontiguousarray(wgate), bgate=bgate, wa=np.ascontiguousarray(wa),
                  wb=np.ascontiguousarray(wb), wout=np.ascontiguousarray(wout), wup=np.ascontiguousarray(wup),
                  cw=np.ascontiguousarray(cw), cb=np.ascontiguousarray(cb), wdn=np.ascontiguousarray(wdn),
                  gmix=gmix, gffn=gffn, gfin=gfin, ident=ident, tneg=tneg, eall=eall, masks=masks)
    in_maps = []
    for core in range(cfg.B * 4):
        b, j = core // 4, core % 4
        quarters = [(j + 1) % 4, (j + 2) % 4, (j + 3) % 4, j]
        xk = np.concatenate([x[b, q * QS:(q + 1) * QS] for q in quarters], axis=0)
        pos = np.concatenate([np.arange(q * QS, (q + 1) * QS) for q in quarters]).astype(f32)
        ang = pos[None, :] * np.tile(inv, 4)[:, None]
        cosT = np.cos(ang).astype(f32)
        sgn = np.tile(np.concatenate([-np.ones(32, f32), np.ones(32, f32)]), 2)
        sinT = (np.sin(ang) * sgn[:, None]).astype(f32)
        vis = np.array([1.0 if q < j else 0.0 for q in quarters[:3]] + [1.0], f32)
        kb_row = np.repeat(np.where(vis > 0, 0.0, -BIG).astype(f32), QS // 128)
        bf_row = np.full(32, 0.0, f32)
        bf_row[:NBLK] = np.repeat(np.where(vis > 0, 0.0, NEGINF).astype(f32), QS // 256)
        m = dict(shared)
        m.update(xk=np.ascontiguousarray(xk), cosT=cosT, sinT=sinT,
                 kbias=np.ascontiguousarray(np.broadcast_to(kb_row[None, :], (128, NT))),
                 bflag=np.ascontiguousarray(np.broadcast_to(bf_row[None, :], (128, 32))),
                 halof=np.full((128, 1), 1.0 if j > 0 else 0.0, f32))
        in_maps.append(m)
    return in_maps


_CACHE = {}


def kernel(**inputs):
    cfg = Cfg()
    in_maps = make_core_inputs(cfg, inputs)
    if "nc" not in _CACHE:
        _CACHE["nc"] = build(cfg)
    res = run_bass_kernel_spmd(_CACHE["nc"], in_maps, core_ids=list(range(8)))
    out = np.zeros((cfg.B, cfg.S, cfg.D), np.float32)
    for core in range(8):
        b, j = core // 4, core % 4
        out[b, j * cfg.QS:(j + 1) * cfg.QS] = np.asarray(res.results[core]["y"], np.float32)
    return out
```

```python
import contextlib
import numpy as np
import concourse.bass as bass
import concourse.mybir as mybir
from concourse.bass_utils import run_bass_kernel_spmd

F32 = mybir.dt.float32
BF16 = mybir.dt.bfloat16
AF = mybir.ActivationFunctionType
ALU = mybir.AluOpType
AX = mybir.AxisListType

HD = 64
BIG = 30000.0
NEGINF = -1e30
EPS = 1e-6


class Cfg:
    def __init__(s, D=1024, NH=8, QS=2048, F=2816, B=2):
        s.D, s.NH, s.QS, s.F, s.B = D, NH, QS, F, B
        s.KC = D // 128
        s.NG = NH // 2
        s.S = 4 * QS
        s.NT = s.S // 128
        s.NBLK = s.S // 256
        s.Q0 = s.S - QS - 128
        s.NQC = (QS + 128) // 128
        s.NQ = s.NQC * 128
        s.NF = F // 128
        s.HW = NH * HD
        s.stop = 0
        s.tiles = []
        c = 0
        while c < s.NQC:
            n = min(4, s.NQC - c)
            s.tiles.append((c, n))
            c += n


class _Rec:
    def __init__(s):
        s.call = None

    def __getattr__(s, name):
        def f(*a, **k):
            s.call = (name, a, k)
            return s
        return f


class TR:
    ENG = ('sp', 'act', 'dve', 'pool', 'pe')

    def __init__(s, nc, stack):
        s.nc = nc
        s.sem, s.cnt, s.lastw, s.readers = {}, {}, {}, {}
        s.waited = {e: {} for e in s.ENG}
        s.q = {e: [] for e in s.ENG}
        s.children = {}
        s.stack = stack
        s.n = 0

    def _rel(s, k):
        out = [k]
        if '.' in k:
            p = k.split('.')[0]
            out.append(p)
            s.children.setdefault(p, set()).add(k)
        else:
            out.extend(s.children.get(k, ()))
        return out

    def _sem(s, key):
        if key not in s.sem:
            s.sem[key] = s.stack.enter_context(s.nc.semaphore(key))
            s.cnt[key] = 0
        return s.sem[key]

    def op(s, e, fn, R=(), W=(), dma=None):
        deps = {}

        def add(ev):
            if ev is not None and ev[1] > deps.get(ev[0], 0):
                deps[ev[0]] = ev[1]
        for r in R:
            for rr in s._rel(r):
                add(s.lastw.get(rr))
        for w in W:
            for ww in s._rel(w):
                add(s.lastw.get(ww))
                for k, v in s.readers.get(ww, {}).items():
                    add((k, v))
        key, inc = ('E_' + e, 1) if dma is None else ('D_' + dma, 16)
        s._sem(key)
        waits = []
        for k, v in deps.items():
            if e == 'pe' and k == 'E_pe':
                continue
            if s.waited[e].get(k, 0) >= v:
                continue
            waits.append((s.sem[k], v))
            s.waited[e][k] = v
        rec = _Rec()
        fn(rec)
        s.cnt[key] += inc
        s.q[e].append((waits, rec.call, s.sem[key], inc))
        s.n += 1
        ev = (key, s.cnt[key])
        for r in R:
            d = s.readers.setdefault(r, {})
            d[key] = max(d.get(key, 0), ev[1])
        for w in W:
            s.lastw[w] = ev
            s.readers[w] = {}
        return ev

    def barrier(s, engines=None):
        for e in (engines or s.ENG):
            waits = []
            for k, v in s.cnt.items():
                if v > 0 and s.waited[e].get(k, 0) < v:
                    waits.append((s.sem[k], v))
                    s.waited[e][k] = v
            if waits:
                s.q[e].append((waits, None, None, 0))

    def flush(s):
        with s.nc.Block() as block:
            sect = {'sp': block.sync, 'act': block.scalar, 'dve': block.vector, 'pool': block.gpsimd,
                    'pe': block.tensor}
            for e in s.ENG:
                q = s.q[e]
                s.q[e] = []
                if not q:
                    continue

                def body(eng, q=q):
                    for waits, call, sem, inc in q:
                        for (wsem, v) in waits:
                            eng.wait_ge(wsem, v)
                        if call is not None:
                            name, a, k = call
                            getattr(eng, name)(*a, **k).then_inc(sem, inc)
                sect[e](body)


class _Stop(Exception):
    pass


def build(cfg):
    D, KC, NG, S, NT, NBLK, Q0, NQC, NQ, NF, QS = (cfg.D, cfg.KC, cfg.NG, cfg.S, cfg.NT, cfg.NBLK, cfg.Q0,
                                                  cfg.NQC, cfg.NQ, cfg.NF, cfg.QS)
    HW = cfg.HW
    NJ = HW // 128
    scale = HD ** -0.5
    NBW = min(512, D)
    NBN = D // NBW
    nc = bass.Bass("TRN2", target_bir_lowering=False)

    def din(name, shape):
        return nc.dram_tensor(name, list(shape), F32, kind="ExternalInput").ap()
    xk = din("xk", [S, D])
    wg = din("wg", [NG, 128, KC * 1024])
    cosT = din("cosT", [128, S])
    sinT = din("sinT", [128, S])
    kbias_d = din("kbias", [128, NT])
    bflag_d = din("bflag", [128, 32])
    halof_d = din("halof", [128, 1])
    ident_d = din("ident", [128, 128])
    tneg_d = din("tneg", [128, 128])
    eall_d = din("eall", [32, 32 * 128])
    masks_d = din("masks", [128, 8 * 512])
    gmix_d = din("gmix", [128, KC])
    gffn_d = din("gffn", [128, KC])
    gfin_d = din("gfin", [128, D])
    wgate_d = din("wgate", [128, 2 * KC, KC * 128])
    bgate_d = din("bgate", [128, 2 * KC])
    wa_d = din("wa", [128, KC, NJ * 128])
    wb_d = din("wb", [128, KC, NJ * 128])
    wout_d = din("wout", [128, KC * D])
    wup_d = din("wup", [128, NF, KC * 256])
    cw_d = din("cw", [128, NF * 6])
    cb_d = din("cb", [128, NF * 2])
    wdn_d = din("wdn", [128, NF * D])
    y = nc.dram_tensor("y", [QS, D], F32, kind="ExternalOutput").ap()
    hsc = nc.dram_tensor("hsc", [S // 512, 128, KC * 512], BF16).ap()
    WC_COLS = 2 * KC * KC * 128 + 2 * KC * NJ * 128 + KC * D + NF * (KC * 256 + D)
    wcache = nc.dram_tensor("wcache", [128, WC_COLS], BF16).ap()

    with contextlib.ExitStack() as stack:
        tr = TR(nc, stack)
        op = tr.op

        def sb(name, shape, dt=F32):
            return stack.enter_context(nc.sbuf_tensor(name, list(shape), dt))

        def ps(name, shape, dt=F32):
            return stack.enter_context(nc.psum_tensor(name, list(shape), dt))

        ident = sb("s_ident", [128, 128], BF16)
        tneg = sb("s_tneg", [128, 128], BF16)
        onesneg = sb("s_onesneg", [128, 128], BF16)
        eall = sb("s_eall", [32, 32 * 128], BF16)
        masks = sb("s_masks", [128, 8 * 512], BF16)
        kbias = sb("s_kbias", [128, NT])
        bflag = sb("s_bflag", [128, 32])
        halof = sb("s_halof", [128, 1])
        gmix = sb("s_gmix", [128, KC])
        gffn = sb("s_gffn", [128, KC])
        bgate = sb("s_bgate", [128, 2 * KC])
        cw = sb("s_cw", [128, NF * 6])
        cb = sb("s_cb", [128, NF * 2])
        stg = [sb("s_stg0", [128, 2048])]
        stg_i = [0]

        def load_cast(dst, dst_key, src_ap, n, cast_eng='dve'):
            i = stg_i[0] % len(stg)
            stg_i[0] += 1
            st = stg[i]
            op('sp', lambda e: e.dma_start(out=st[:, 0:n], in_=src_ap), W=['stg%d' % i], dma='stg%d' % i)
            if cast_eng == 'act':
                op('act', lambda e: e.activation(out=dst, in_=st[:, 0:n], func=AF.Copy), R=['stg%d' % i], W=[dst_key])
            else:
                op(cast_eng, lambda e: e.tensor_copy(out=dst, in_=st[:, 0:n]), R=['stg%d' % i], W=[dst_key])

        def load_f32(dst, key, src_ap):
            op('sp', lambda e: e.dma_start(out=dst, in_=src_ap), W=[key], dma=key)

        load_cast(ident[:, :], 'ident', ident_d, 128)
        load_cast(tneg[:, :], 'tneg', tneg_d, 128)
        for i in range(4):
            load_cast(masks[:, i * 1024:(i + 1) * 1024], 'masks', masks_d[:, i * 1024:(i + 1) * 1024], 1024)
        op('dve', lambda e: e.memset(onesneg[:, :], -1.0), W=['onesneg'])
        for hh in range(2):
            i = stg_i[0] % len(stg)
            stg_i[0] += 1
            op('sp', lambda e, i=i, hh=hh: e.dma_start(out=stg[i][0:32, 0:2048], in_=eall_d[:, hh * 2048:(hh + 1) * 2048]),
               W=['stg%d' % i], dma='stg%d' % i)
            op('dve', lambda e, i=i, hh=hh: e.tensor_copy(out=eall[:, hh * 2048:(hh + 1) * 2048], in_=stg[i][0:32, 0:2048]),
               R=['stg%d' % i], W=['eall'])
        load_f32(kbias[:, :], 'kbias', kbias_d)
        load_f32(bflag[:, :], 'bflag', bflag_d)
        load_f32(halof[:, :], 'halof', halof_d)
        load_f32(gmix[:, :], 'gmix', gmix_d)
        load_f32(gffn[:, :], 'gffn', gffn_d)
        load_f32(bgate[:, :], 'bgate', bgate_d)
        load_f32(cw[:, :], 'cw', cw_d)
        load_f32(cb[:, :], 'cb', cb_d)

        yaT = sb("s_yaT", [128, NJ, NQ], BF16)
        ybT = sb("s_ybT", [128, NJ, NQ], BF16)

        xc = [sb("s_xc0", [128, D]), sb("s_xc1", [128, D])]
        junk = sb("s_junk", [128, D], BF16)
        hn2 = [sb("s_hn", [128, D], BF16), sb("s_hn1", [128, D], BF16)]
        st42 = [sb("s_st4", [128, 4]), sb("s_st41", [128, 4])]
        st4 = st42[0]
        nrm_i = [0]
        tp_ps = ps("tp_ps", [128, KC * 128], BF16)
        xc_i = [0]

        def norm_A(src_key, src):
            ni = nrm_i[0] % 2
            nrm_i[0] += 1
            st4, hn = st42[ni], hn2[ni]
            sk, hk, jk = 'st4_%d' % ni, 'hn_%d' % ni, 'junk'
            op('dve', lambda e: e.scalar_tensor_tensor(out=junk[:, :], in0=src, scalar=1.0, in1=src,
                                                       op0=ALU.mult, op1=ALU.mult, accum_out=st4[:, 0:1]),
               R=[src_key], W=[jk, sk])
            op('dve', lambda e: e.tensor_scalar(out=st4[:, 1:2], in0=st4[:, 0:1], scalar1=1.0 / D, scalar2=EPS,
                                                op0=ALU.mult, op1=ALU.add), R=[sk], W=[sk])
            op('act', lambda e: e.activation(out=st4[:, 2:3], in_=st4[:, 1:2], func=AF.Ln), R=[sk], W=[sk])
            op('act', lambda e: e.activation(out=st4[:, 3:4], in_=st4[:, 2:3], func=AF.Exp, scale=-0.5),
               R=[sk], W=[sk])
            op('act', lambda e: e.activation(out=hn[:, :], in_=src, func=AF.Identity, scale=st4[:, 3:4]),
               R=[src_key, sk], W=[hk])
            return (hn, hk)

        def norm_B(state, gt, gkey, dst_fn, dst_key):
            hn, hk = state
            for kc in range(KC):
                op('pe', lambda e, kc=kc: e.transpose(out=tp_ps[:, kc * 128:(kc + 1) * 128],
                                                     in_=hn[:, kc * 128:(kc + 1) * 128], identity=ident[:, :]),
                   R=[hk, 'ident'], W=['tp_ps'])
            for kc in range(KC):
                op('dve', lambda e, kc=kc: e.tensor_scalar(out=dst_fn(kc), in0=tp_ps[:, kc * 128:(kc + 1) * 128],
                                                           scalar1=gt[:, kc:kc + 1], scalar2=None, op0=ALU.mult),
                   R=['tp_ps', gkey], W=[dst_key])

        def norm_T(src_key, src, gt, gkey, dst_fn, dst_key):
            norm_B(norm_A(src_key, src), gt, gkey, dst_fn, dst_key)

        def load_x_chunk(t):
            i = xc_i[0] % 2
            xc_i[0] += 1
            op('sp', lambda e: e.dma_start(out=xc[i][:, :], in_=xk[t * 128:(t + 1) * 128, :]),
               W=['xc%d' % i], dma='xc%d' % i)
            return i

        def chk(k):
            if cfg.stop == k:
                raise _Stop()

        with contextlib.ExitStack() as st1:
          try:
              def sb1(name, shape, dt=F32):
                  return st1.enter_context(nc.sbuf_tensor(name, list(shape), dt))

              def ps1(name, shape, dt=F32):
                  return st1.enter_context(nc.psum_tensor(name, list(shape), dt))
              wgb = sb1("s_wgb", [128, KC * 1024], BF16)
              hT = sb1("s_hT", [128, KC, 512], BF16)
              cs = sb1("s_cs", [128, 512])
              sn = sb1("s_sn", [128, 512])
              kaT = sb1("s_kaT", [128, S], BF16)
              kbT = sb1("s_kbT", [128, S], BF16)
              vA = sb1("s_vA", [128, NT, 2 * 66], BF16)
              vB = sb1("s_vB", [128, NT, 128], BF16)
              qaT = sb1("s_qaT", [128, NQ], BF16)
              qbT = sb1("s_qbT", [128, NQ], BF16)
              km = sb1("s_km", [128, 32])
              kmb = sb1("s_kmb", [128, 32], BF16)
              gm = sb1("s_gm", [128, 32])
              m8 = sb1("s_m8", [128, 8])
              sel = sb1("s_sel", [128, 32])
              sel2 = sb1("s_sel2", [128, 32])
              biasq = sb1("s_biasq", [128, 32], BF16)
              biasT = [sb1("s_biasT0", [32, NQ], BF16), sb1("s_biasT1", [32, NQ], BF16)]
              pbuf = [sb1("s_p0", [128, 512], BF16), sb1("s_p1", [128, 512], BF16)]
              ebuf = [sb1("s_e0", [128, 512]), sb1("s_e1", [128, 512])]
              rt1, rt2 = ebuf[0], ebuf[1]
              spbuf = [sb1("s_sp0", [128, 512], BF16), sb1("s_sp1", [128, 512], BF16)]
              Ebuf = [sb1("s_E0", [128, 512], BF16), sb1("s_E1", [128, 512], BF16)]
              abuf = [sb1("s_a0", [128, 512], BF16), sb1("s_a1", [128, 512], BF16)]
              lsum = sb1("s_lsum", [128, 512], BF16)
              lsum2 = sb1("s_lsum2", [128, 512], BF16)
              ebuf2 = sb1("s_e2", [128, 512])
              ytok = sb1("s_ytok", [128, 4, 128], BF16)
              rden = sb1("s_rden", [128, 1])
              pj = [ps1("pj0", [128, 512]), ps1("pj1", [128, 512])]
              sc_ps = [ps1("sc0", [128, 512]), ps1("sc1", [128, 512])]
              af_ps = ps1("af", [128, 512])
              acc_ps = ps1("acc", [128, 512])
              sm_ps = ps1("sm", [128, 512], BF16)

              chk(100)
              for hh in range(2):
                  op('dve', lambda e, hh=hh: e.memset(vA[:, :, hh * 66 + 64: hh * 66 + 65], 1.0), W=['vA'])

              pj_i = [0]
              halt = [pbuf[0], pbuf[1], spbuf[0], spbuf[1], Ebuf[0], Ebuf[1], abuf[0], abuf[1]]
              NHB = 2 if KC <= len(halt) else 1
              cur = [0]

              def hTv(kc, bi=None):
                  bi = cur[0] if bi is None else bi
                  return hT[:, kc, :] if bi == 0 else halt[kc][:, :]

              def hTk(bi=None):
                  return 'hTb%d' % (cur[0] if bi is None else bi)

              def proj(col0, ncols_tok, tok_off):
                  i = pj_i[0] % 2
                  pj_i[0] += 1
                  for kc in range(KC):
                      op('pe', lambda e, kc=kc: e.matmul(pj[i][:, 0:ncols_tok],
                                                         lhsT=wgb[:, kc * 1024 + col0: kc * 1024 + col0 + 128],
                                                         rhs=hTv(kc)[:, tok_off:tok_off + ncols_tok],
                                                         start=(kc == 0), stop=(kc == KC - 1)),
                         R=['wgb', hTk()], W=['pj%d' % i])
                  return i

              def rope_to(dst, dst_key, c_main, c_perm, n, tok_off):
                  i1 = proj(c_main, n, tok_off)
                  op('dve', lambda e: e.tensor_tensor(out=rt1[:, 0:n], in0=pj[i1][:, 0:n],
                                                      in1=cs[:, tok_off:tok_off + n], op=ALU.mult),
                     R=['pj%d' % i1, 'cs'], W=['e0'])
                  i2 = proj(c_perm, n, tok_off)
                  op('dve', lambda e: e.tensor_tensor(out=rt2[:, 0:n], in0=pj[i2][:, 0:n],
                                                      in1=sn[:, tok_off:tok_off + n], op=ALU.mult),
                     R=['pj%d' % i2, 'sn'], W=['e1'])
                  op('pool', lambda e: e.tensor_tensor(out=dst, in0=rt1[:, 0:n], in1=rt2[:, 0:n], op=ALU.add),
                     R=['e0', 'e1'], W=[dst_key])

              chk(1)
              for g in range(NG):
                  for kc in range(0, KC, 2):
                      n = min(2, KC - kc) * 1024
                      load_cast(wgb[:, kc * 1024: kc * 1024 + n], 'wgb', wg[g, :, kc * 1024: kc * 1024 + n], n,
                                cast_eng='dve')
                  chk(11)
                  for tt in range(S // 512):
                      cur[0] = tt % NHB
                      if g == 0:
                          pend = None
                          for c in range(4):
                              t = tt * 4 + c
                              i = load_x_chunk(t)
                              stt = norm_A('xc%d' % i, xc[i][:, :])
                              if pend is not None:
                                  norm_B(pend[0], gmix, 'gmix', lambda kc, c=pend[1]: hTv(kc)[:, c * 128:(c + 1) * 128], hTk())
                              pend = (stt, c)
                              chk(12)
                          norm_B(pend[0], gmix, 'gmix', lambda kc, c=pend[1]: hTv(kc)[:, c * 128:(c + 1) * 128], hTk())
                          for kc in range(KC):
                              op('sp', lambda e, kc=kc: e.dma_start(out=hsc[tt][:, kc * 512:(kc + 1) * 512], in_=hTv(kc)),
                                 R=[hTk()], W=['hsc%d' % tt], dma='hscw%d' % cur[0])
                      else:
                          for kc in range(KC):
                              op('sp', lambda e, kc=kc: e.dma_start(out=hTv(kc), in_=hsc[tt][:, kc * 512:(kc + 1) * 512]),
                                 R=['hsc%d' % tt], W=[hTk()], dma='hTl%d' % cur[0])
                      load_f32(cs[:, :], 'cs', cosT[:, tt * 512:(tt + 1) * 512])
                      load_f32(sn[:, :], 'sn', sinT[:, tt * 512:(tt + 1) * 512])
                      tk = 'kv%d' % tt
                      rope_to(kaT[:, tt * 512:(tt + 1) * 512], 'kaT' + tk, 256, 384, 512, 0)
                      chk(13)
                      i = proj(640, 512, 0)
                      op('act', lambda e, i=i: e.activation(out=kbT[:, tt * 512:(tt + 1) * 512], in_=pj[i][:, :],
                                                            func=AF.Copy), R=['pj%d' % i], W=['kbT' + tk])
                      chk(14)
                      for c in range(4):
                          t = tt * 4 + c
                          i = pj_i[0] % 2
                          pj_i[0] += 1
                          for kc in range(KC):
                              op('pe', lambda e, kc=kc, c=c, i=i: e.matmul(
                                  pj[i][:, 0:256], lhsT=hTv(kc)[:, c * 128:(c + 1) * 128],
                                  rhs=wgb[:, kc * 1024 + 768: kc * 1024 + 1024],
                                  start=(kc == 0), stop=(kc == KC - 1)), R=['wgb', hTk()], W=['pj%d' % i])
                          chk(151)
                          for hh in range(2):
                              op('act', lambda e, t=t, i=i, hh=hh: e.activation(
                                  out=vA[:, t, hh * 66: hh * 66 + 64], in_=pj[i][:, hh * 64:(hh + 1) * 64], func=AF.Copy),
                                 R=['pj%d' % i], W=['vA'])
                          chk(152)
                          op('act', lambda e, t=t, i=i: e.activation(out=vB[:, t, :], in_=pj[i][:, 128:256], func=AF.Copy),
                             R=['pj%d' % i], W=['vB'])
                      chk(15)
                      lo = max(tt * 512, Q0)
                      if lo < (tt + 1) * 512:
                          off = lo - tt * 512
                          n = 512 - off
                          rope_to(qaT[:, lo - Q0: lo - Q0 + n], 'qaT', 0, 128, n, off)
                          i = proj(512, n, off)
                          op('act', lambda e, i=i, lo=lo, n=n: e.activation(out=qbT[:, lo - Q0: lo - Q0 + n],
                                                                            in_=pj[i][:, 0:n], func=AF.Copy),
                             R=['pj%d' % i], W=['qbT'])
                  chk(2)
                  kv_keys = ['kv%d' % tt for tt in range(S // 512)]
                  for nb0 in range(0, NBLK, 8):
                      op('dve', lambda e, nb0=nb0: e.tensor_reduce(
                          out=km[:, nb0:nb0 + 8],
                          in_=kaT[:, nb0 * 256:(nb0 + 8) * 256].rearrange("p (n k) -> p n k", k=256),
                          op=ALU.add, axis=AX.X), R=['kaT' + k for k in kv_keys], W=['km'])
                  op('dve', lambda e: e.tensor_scalar(out=kmb[:, 0:NBLK], in0=km[:, 0:NBLK], scalar1=1.0 / 256,
                                                      scalar2=None, op0=ALU.mult), R=['km'], W=['kmb'])
                  for h in range(2):
                      hp = slice(64 * h, 64 * h + 64)
                      for qc in range(NQC):
                          Bq = (Q0 + qc * 128) // 256
                          op('pe', lambda e, qc=qc: e.matmul(sc_ps[0][:, 0:NBLK], lhsT=qaT[hp, qc * 128:(qc + 1) * 128],
                                                            rhs=kmb[hp, 0:NBLK], start=True, stop=True),
                             R=['qaT', 'kmb'], W=['sc0'])
                          op('dve', lambda e: e.memset(gm[:, :], NEGINF), W=['gm'])
                          op('dve', lambda e, Bq=Bq: e.tensor_tensor(out=gm[:, 0:Bq], in0=sc_ps[0][:, 0:Bq],
                                                                     in1=bflag[:, 0:Bq], op=ALU.add),
                             R=['sc0', 'bflag'], W=['gm'])
                          op('dve', lambda e: e.max(out=m8[:, :], in_=gm[:, 0:NBLK]), R=['gm'], W=['m8'])
                          op('dve', lambda e: e.tensor_scalar(out=sel[:, :], in0=gm[:, :], scalar1=m8[:, 2:3],
                                                              scalar2=None, op0=ALU.is_ge), R=['gm', 'm8'], W=['sel'])
                          op('dve', lambda e: e.tensor_scalar(out=sel2[:, :], in0=gm[:, :], scalar1=-1e29,
                                                              scalar2=None, op0=ALU.is_gt), R=['gm'], W=['sel2'])
                          op('dve', lambda e: e.tensor_tensor(out=sel[:, :], in0=sel[:, :], in1=sel2[:, :],
                                                              op=ALU.mult), R=['sel', 'sel2'], W=['sel'])
                          op('dve', lambda e: e.tensor_scalar(out=biasq[:, :], in0=sel[:, :], scalar1=BIG,
                                                              scalar2=-BIG, op0=ALU.mult, op1=ALU.add),
                             R=['sel'], W=['biasq'])
                          op('dve', lambda e, Bq=Bq: e.memset(biasq[:, Bq:Bq + 1], 0.0), W=['biasq'])
                          op('pe', lambda e: e.transpose(out=sm_ps[0:32, 0:128], in_=biasq[:, :], identity=ident[:, :]),
                             R=['biasq', 'ident'], W=['sm'])
                          op('act', lambda e, h=h, qc=qc: e.activation(out=biasT[h][:, qc * 128:(qc + 1) * 128],
                                                                        in_=sm_ps[0:32, 0:128], func=AF.Copy),
                             R=['sm'], W=['biasT%d' % h])
                  chk(3)
                  tr.barrier()
                  ebufs = [(ebuf[0], 'e0'), (ebuf[1], 'e1'), (cs, 'cs'), (sn, 'sn'), (ebuf2, 'e2')]
                  hsl = [(hT[:, kc, :], 'hTs%d' % kc) for kc in range(KC)]
                  bfb = [(pbuf[0], 'p0'), (pbuf[1], 'p1'), (spbuf[0], 'sp0'), (spbuf[1], 'sp1'), (Ebuf[0], 'E0'),
                         (Ebuf[1], 'E1'), (abuf[0], 'a0'), (abuf[1], 'a1')] + hsl
                  assert len(bfb) >= 10
                  sp_b, E_b, a_b, p_b = bfb[0:3], bfb[3:5], bfb[5:8], bfb[8:8 + 3] if len(bfb) >= 11 else bfb[8:10]
                  lsums = [(lsum, 'lsum'), (lsum2, 'lsum2')]
                  fbanks = [(sc_ps[0], 'sc0'), (sc_ps[1], 'sc1'), (pj[0], 'pj0'), (pj[1], 'pj1'), (af_ps, 'af'), (acc_ps, 'acc')]

                  def run_pipeline(nblocks, stages):
                      ns = len(stages)
                      for t in range(nblocks + ns - 1):
                          for s in range(ns - 1, -1, -1):
                              i = t - s
                              if 0 <= i < nblocks:
                                  stages[s](i)

                  for (c0, ncq) in cfg.tiles:
                      W = ncq * 128
                      q0 = Q0 + c0 * 128
                      qs = slice(c0 * 128, c0 * 128 + W)
                      kb_max = (q0 + W) // 128 - 1
                      kb_diag = q0 // 128
                      mb = [(kb, h) for kb in range(kb_max + 1) for h in range(2)]
                      m_sc = fbanks[0:4]
                      m_acc = [fbanks[5], fbanks[4]]

                      def m_s0(i):
                          kb, h = mb[i]
                          hp = slice(64 * h, 64 * h + 64)
                          sc, sck = m_sc[i % 4]
                          op('pe', lambda e: e.matmul(sc[:, 0:W], lhsT=kaT[hp, kb * 128:(kb + 1) * 128], rhs=qaT[hp, qs],
                                                      start=True, stop=False), R=['kaTkv%d' % (kb // 4), 'qaT'], W=[sck])
                          n = kb // 2
                          op('pe', lambda e: e.matmul(sc[:, 0:W], lhsT=eall[:, n * 128:(n + 1) * 128], rhs=biasT[h][:, qs],
                                                      start=False, stop=True), R=['eall', 'biasT%d' % h], W=[sck])

                      def m_s1(i):
                          kb, h = mb[i]
                          sc, sck = m_sc[i % 4]
                          p, pk = p_b[i % len(p_b)]
                          op('act', lambda e: e.activation(out=p[:, 0:W], in_=sc[:, 0:W], func=AF.Exp, scale=scale),
                             R=[sck], W=[pk])
                          if kb >= kb_diag:
                              v = 4 + (kb - kb_diag)
                              op('dve', lambda e: e.tensor_tensor(out=p[:, 0:W], in0=p[:, 0:W],
                                                                  in1=masks[:, v * 512: v * 512 + W], op=ALU.mult),
                                 R=[pk, 'masks'], W=[pk])

                      def m_s2(i):
                          kb, h = mb[i]
                          p, pk = p_b[i % len(p_b)]
                          acc, acck = m_acc[h]
                          for c in range(ncq):
                              op('pe', lambda e, c=c: e.matmul(acc[:, c * 65:(c + 1) * 65], lhsT=p[:, c * 128:(c + 1) * 128],
                                                               rhs=vA[:, kb, h * 66: h * 66 + 65],
                                                               start=(kb == 0 and c == 0),
                                                               stop=(kb == kb_max and c == ncq - 1)),
                                 R=[pk, 'vA'], W=[acck])
                      run_pipeline(len(mb), [m_s0, m_s1, m_s2])
                      for h in range(2):
                          acc, acck = m_acc[h]
                          for c in range(ncq):
                              op('dve', lambda e, c=c: e.reciprocal(out=rden[:, :], in_=acc[:, c * 65 + 64: c * 65 + 65]),
                                 R=[acck], W=['rden'])
                              op('dve', lambda e, c=c, h=h: e.tensor_scalar(out=ytok[:, c, 64 * h:64 * h + 64],
                                                                            in0=acc[:, c * 65: c * 65 + 64],
                                                                            scalar1=rden[:, 0:1], scalar2=None,
                                                                            op0=ALU.mult),
                                 R=[acck, 'rden'], W=['ytok'])
                      for c in range(ncq):
                          op('pe', lambda e, c=c: e.transpose(out=sm_ps[:, 128:256], in_=ytok[:, c, :], identity=ident[:, :]),
                             R=['ytok', 'ident'], W=['sm'])
                          op('act', lambda e, c=c: e.activation(out=yaT[:, g, (c0 + c) * 128:(c0 + c + 1) * 128],
                                                                in_=sm_ps[:, 128:256], func=AF.Copy),
                             R=['sm'], W=['yaT'])
                      chk(4)
                      sbk = [(kb, h) for kb in range(kb_max, -1, -1) for h in range(2)]
                      s_sc = fbanks[0:2]
                      s_af = [fbanks[4], fbanks[2]]
                      s_acc = [fbanks[5], fbanks[3]]

                      def s_s0(i):
                          kb, h = sbk[i]
                          hp = slice(64 * h, 64 * h + 64)
                          sc, sck = s_sc[i % 2]
                          op('pe', lambda e: e.matmul(sc[:, 0:W], lhsT=kbT[hp, kb * 128:(kb + 1) * 128], rhs=qbT[hp, qs],
                                                      start=True, stop=True), R=['kbTkv%d' % (kb // 4), 'qbT'], W=[sck])

                      def s_s1(i):
                          kb, h = sbk[i]
                          sc, sck = s_sc[i % 2]
                          eb, ek = ebufs[i % 5]
                          op('act', lambda e: e.activation(out=eb[:, 0:W], in_=sc[:, 0:W], func=AF.Exp, scale=scale,
                                                           bias=kbias[:, kb:kb + 1]), R=[sck, 'kbias'], W=[ek])
                          if kb >= kb_diag:
                              v = kb - kb_diag
                              op('dve', lambda e: e.tensor_tensor(out=eb[:, 0:W], in0=eb[:, 0:W],
                                                                  in1=masks[:, v * 512: v * 512 + W], op=ALU.mult),
                                 R=[ek, 'masks'], W=[ek])

                      def s_s2(i):
                          eb, ek = ebufs[i % 5]
                          spb, spk = sp_b[i % 3]
                          op('act', lambda e: e.activation(out=spb[:, 0:W], in_=eb[:, 0:W], func=AF.Ln, bias=1.0),
                             R=[ek], W=[spk])

                      def s_s3(i):
                          kb, h = sbk[i]
                          first = (kb == kb_max)
                          spb, spk = sp_b[i % 3]
                          af, afk = s_af[i % 2]
                          ls, lsk = lsums[h]
                          op('pe', lambda e: e.matmul(af[:, 0:W], lhsT=tneg[:, :], rhs=spb[:, 0:W], start=True, stop=first),
                             R=['tneg', spk], W=[afk])
                          if not first:
                              op('pe', lambda e: e.matmul(af[:, 0:W], lhsT=onesneg[:, :], rhs=ls[:, 0:W], start=False,
                                                          stop=True), R=['onesneg', lsk], W=[afk])
                          if kb != 0:
                              if first:
                                  op('dve', lambda e: e.tensor_copy(out=ls[:, 0:W], in_=spb[:, 0:W]), R=[spk], W=[lsk])
                              else:
                                  op('pool', lambda e: e.tensor_tensor(out=ls[:, 0:W], in0=ls[:, 0:W], in1=spb[:, 0:W],
                                                                       op=ALU.add), R=[spk, lsk], W=[lsk])

                      def s_s4(i):
                          af, afk = s_af[i % 2]
                          Eb, Ek = E_b[i % 2]
                          op('act', lambda e: e.activation(out=Eb[:, 0:W], in_=af[:, 0:W], func=AF.Exp), R=[afk], W=[Ek])

                      def s_s5(i):
                          eb, ek = ebufs[i % 5]
                          Eb, Ek = E_b[i % 2]
                          ab, ak = a_b[i % 3]
                          op('dve', lambda e: e.tensor_tensor(out=ab[:, 0:W], in0=eb[:, 0:W], in1=Eb[:, 0:W], op=ALU.mult),
                             R=[ek, Ek], W=[ak])

                      def s_s6(i):
                          kb, h = sbk[i]
                          hp = slice(64 * h, 64 * h + 64)
                          ab, ak = a_b[i % 3]
                          acc, acck = s_acc[h]
                          op('pe', lambda e: e.matmul(acc[hp, 0:W], lhsT=vB[:, kb, 64 * h:64 * h + 64], rhs=ab[:, 0:W],
                                                      start=(kb == kb_max), stop=(kb == 0)), R=['vB', ak], W=[acck])
                      run_pipeline(len(sbk), [s_s0, s_s1, s_s2, s_s3, s_s4, s_s5, s_s6])
                      for h in range(2):
                          hp = slice(64 * h, 64 * h + 64)
                          acc, acck = s_acc[h]
                          op('act', lambda e: e.activation(out=ybT[hp, g, qs], in_=acc[hp, 0:W], func=AF.Copy),
                             R=[acck], W=['ybT'])
                  tr.barrier()

          except _Stop:
            pass
          if 0 < cfg.stop < 200:
            tr.barrier(engines=['sp'])
          tr.flush()
        if 0 < cfg.stop < 200:
            return nc
        tr.barrier()
        with contextlib.ExitStack() as st2:
          try:
              def sb2(name, shape, dt=F32):
                  return st2.enter_context(nc.sbuf_tensor(name, list(shape), dt))

              def ps2(name, shape, dt=F32):
                  return st2.enter_context(nc.psum_tensor(name, list(shape), dt))
              stg.append(sb2("s_stg1", [128, 2048]))
              x1 = sb2("s_x1", [128, 4, D])
              hTo = sb2("s_hTo", [128, KC, 512], BF16)
              gT = sb2("s_gT", [128, 2 * KC, 512], BF16)
              mT = sb2("s_mT", [128, KC, 512], BF16)
              mt1 = sb2("s_mt1", [128, 512])
              mt2 = sb2("s_mt2", [128, 512])
              wsm = [sb2("s_wsm%d" % q, [128, 1024], BF16) for q in range(4)]
              woutb = sb2("s_woutb", [128, KC * D], BF16)
              h2T = sb2("s_h2T", [128, KC, 512], BF16)
              ur = [sb2("s_ur0", [128, 2 + 512]), sb2("s_ur1", [128, 2 + 512])]
              carry = sb2("s_carry", [128, NF * 2, 2])
              cv = [sb2("s_cv0", [128, 512]), sb2("s_cv1", [128, 512])]
              sg = sb2("s_sg", [128, 512])
              urs = [[(ur[0], 'ur0'), (ur[1], 'ur1')],
                     [(stg[1][:, 0:514], 'stg1.a'), (stg[1][:, 514:1028], 'stg1.b')]]
              cvs = [[(cv[0], 'cv0'), (cv[1], 'cv1')],
                     [(stg[0][:, 0:512], 'stg0.a'), (stg[0][:, 512:1024], 'stg0.b')]]
              sgs = [(sg, 'sg'), (stg[0][:, 1024:1536], 'stg0.c')]
              prod = sb2("s_prod", [128, 2, 512], BF16)
              wup = [sb2("s_wup%d" % q, [128, KC * 256], BF16) for q in range(4)]
              wdn = [sb2("s_wdn%d" % q, [128, D], BF16) for q in range(4)]
              gfin = sb2("s_gfin", [128, D])
              st4f = sb2("s_st4f", [128, 4])
              ob = [xc[0], xc[1]]
              g_ps = [ps2("g0", [128, 512]), ps2("g1", [128, 512])]
              b_ps = [ps2("b0", [128, 512]), ps2("b1", [128, 512])]
              o_ps = ps2("o_ps", [128, 1024])
              wsm_i = [0]
              load_f32(gfin[:, :], 'gfin', gfin_d)
              chk(200)

              wc_off = {}
              wc_next = [0]

              def cached_load(dst, dst_key, src_ap, n, name, first, eng='act'):
                  if name not in wc_off:
                      wc_off[name] = wc_next[0]
                      wc_next[0] += n
                  o = wc_off[name]
                  if first:
                      load_cast(dst, dst_key, src_ap, n, cast_eng=eng)
                      op('sp', lambda e: e.dma_start(out=wcache[:, o:o + n], in_=dst), R=[dst_key], W=['wc_' + name],
                         dma='wcw_' + dst_key)
                  else:
                      op('sp', lambda e: e.dma_start(out=dst, in_=wcache[:, o:o + n]), R=['wc_' + name], W=[dst_key],
                         dma='L_' + dst_key)

              def wload(src_ap, n, name, first, eng='act'):
                  i = wsm_i[0] % 4
                  wsm_i[0] += 1
                  cached_load(wsm[i][:, 0:n], 'wsm%d' % i, src_ap, n, name, first, eng=eng)
                  return i

              gi = [0]
              ob_i = [0]
              for ti, (c0, ncq) in enumerate(cfg.tiles):
                  W = ncq * 128
                  qs = slice(c0 * 128, c0 * 128 + W)
                  x1k = ['x1_%d' % c for c in range(ncq)]
                  for c in range(ncq):
                      t = Q0 // 128 + c0 + c
                      op('sp', lambda e, c=c, t=t: e.dma_start(out=x1[:, c, :], in_=xk[t * 128:(t + 1) * 128, :]),
                         W=[x1k[c]], dma=x1k[c])
                      norm_T(x1k[c], x1[:, c, :], gmix, 'gmix',
                             lambda kc, c=c: hTo[:, kc, c * 128:(c + 1) * 128], 'hTo')
                      chk(201)
                  wst = min(2048, KC * D)
                  for q4 in range(0, KC * D, wst):
                      cached_load(woutb[:, q4:q4 + wst], 'woutb', wout_d[:, q4:q4 + wst], wst, 'wout%d' % q4, ti == 0)
                  for fc in range(2 * KC):
                      wi = wload(wgate_d[:, fc, :], KC * 128, 'wg%d' % fc, ti == 0)
                      b = gi[0] % 2
                      gi[0] += 1
                      for kc in range(KC):
                          op('pe', lambda e, kc=kc, wi=wi, b=b: e.matmul(g_ps[b][:, 0:W], lhsT=wsm[wi][:, kc * 128:(kc + 1) * 128],
                                                                         rhs=hTo[:, kc, 0:W], start=(kc == 0),
                                                                         stop=(kc == KC - 1)),
                             R=['wsm%d' % wi, 'hTo'], W=['g%d' % b])
                      op('act', lambda e, fc=fc, b=b: e.activation(out=gT[:, fc, 0:W], in_=g_ps[b][:, 0:W],
                                                                   func=AF.Sigmoid, bias=bgate[:, fc:fc + 1]),
                         R=['g%d' % b, 'bgate'], W=['gT'])
                      chk(202)
                  for fc in range(KC):
                      wia = wload(wa_d[:, fc, :], NJ * 128, 'wa%d' % fc, ti == 0, eng='dve')
                      wib = wload(wb_d[:, fc, :], NJ * 128, 'wb%d' % fc, ti == 0, eng='dve')
                      b = gi[0] % 2
                      gi[0] += 1
                      for j in range(NJ):
                          op('pe', lambda e, j=j, wia=wia, b=b: e.matmul(g_ps[b][:, 0:W], lhsT=wsm[wia][:, j * 128:(j + 1) * 128],
                                                                         rhs=yaT[:, j, qs], start=(j == 0), stop=(j == NJ - 1)),
                             R=['wsm%d' % wia, 'yaT'], W=['g%d' % b])
                      for j in range(NJ):
                          op('pe', lambda e, j=j, wib=wib, b=b: e.matmul(b_ps[b][:, 0:W], lhsT=wsm[wib][:, j * 128:(j + 1) * 128],
                                                                         rhs=ybT[:, j, qs], start=(j == 0), stop=(j == NJ - 1)),
                             R=['wsm%d' % wib, 'ybT'], W=['b%d' % b])
                      op('dve', lambda e, fc=fc, b=b: e.tensor_tensor(out=mt1[:, 0:W], in0=g_ps[b][:, 0:W],
                                                                      in1=gT[:, fc, 0:W], op=ALU.mult),
                         R=['g%d' % b, 'gT'], W=['mt1'])
                      op('dve', lambda e, fc=fc, b=b: e.tensor_tensor(out=mt2[:, 0:W], in0=b_ps[b][:, 0:W],
                                                                      in1=gT[:, KC + fc, 0:W], op=ALU.mult),
                         R=['b%d' % b, 'gT'], W=['mt2'])
                      op('pool', lambda e, fc=fc: e.tensor_tensor(out=mT[:, fc, 0:W], in0=mt1[:, 0:W], in1=mt2[:, 0:W],
                                                                  op=ALU.add), R=['mt1', 'mt2'], W=['mT'])
                      chk(203)
                  for c in range(ncq):
                      for nb in range(NBN):
                          for kc in range(KC):
                              op('pe', lambda e, kc=kc, c=c, nb=nb: e.matmul(
                                  o_ps[:, nb * NBW:(nb + 1) * NBW], lhsT=mT[:, kc, c * 128:(c + 1) * 128],
                                  rhs=woutb[:, kc * D + nb * NBW: kc * D + nb * NBW + NBW], start=(kc == 0),
                                  stop=(kc == KC - 1)), R=['woutb', 'mT'], W=['o_ps'])
                      op('dve', lambda e, c=c: e.tensor_tensor(out=x1[:, c, :], in0=o_ps[:, 0:D], in1=x1[:, c, :],
                                                               op=ALU.add), R=['o_ps', x1k[c]], W=[x1k[c]])
                      norm_T(x1k[c], x1[:, c, :], gffn, 'gffn',
                             lambda kc, c=c: h2T[:, kc, c * 128:(c + 1) * 128], 'h2T')
                      chk(204)
                  gbanks = [(g_ps[0], 'g0'), (g_ps[1], 'g1'), (b_ps[0], 'b0'), (b_ps[1], 'b1')]
                  if ti == 0:
                      cv3 = [cvs[0], [(mt1, 'mt1'), (mt2, 'mt2')]]
                      ur_s, sg_s = [urs[0]], [sgs[0]]
                  else:
                      cv3 = cvs + [[(mt1, 'mt1'), (mt2, 'mt2')]]
                      ur_s, sg_s = urs, sgs
                  NCV, NUR = len(cv3), len(ur_s)
                  prod4 = [(masks[:, q * 512:(q + 1) * 512], 'masks.p%d' % q) for q in range(4)]
                  wdn6 = [(wdn[q][:, 0:D], 'wdn%d' % q) for q in range(4)]
                  if D <= 1024:
                      wdn6 += [(masks[:, 2048 + q * 1024: 2048 + q * 1024 + D], 'masks.w%d' % q) for q in range(2)]
                  NWD = len(wdn6)

                  def f_load(f):
                      if f >= NF:
                          return
                      cached_load(wup[f % 4][:, :], 'wup%d' % (f % 4), wup_d[:, f, :], KC * 256, 'wup%d' % f, ti == 0,
                                  eng=('act' if f % 2 == 0 else 'dve'))

                  def f_load_dn(f):
                      wd, wdk = wdn6[f % NWD]
                      cached_load(wd, wdk, wdn_d[:, f * D:(f + 1) * D], D, 'wdn%d' % f, ti == 0,
                                  eng=('dve' if f % 2 == 0 else 'act'))

                  def f_s0(f):
                      f_load(f + 2)
                      for gv in range(2):
                          gb, gbk = gbanks[(f % 2) * 2 + gv]
                          for kc in range(KC):
                              op('pe', lambda e, kc=kc: e.matmul(
                                  gb[:, 0:W], lhsT=wup[f % 4][:, kc * 256 + gv * 128: kc * 256 + gv * 128 + 128],
                                  rhs=h2T[:, kc, 0:W], start=(kc == 0), stop=(kc == KC - 1)),
                                 R=['wup%d' % (f % 4), 'h2T'], W=[gbk])

                  def f_s1(f):
                      for gv in range(2):
                          gb, gbk = gbanks[(f % 2) * 2 + gv]
                          u, uk = ur_s[f % NUR][gv]
                          cidx = f * 2 + gv
                          if ti == 0:
                              op('dve', lambda e: e.memset(u[:, 0:2], 0.0), W=[uk])
                          else:
                              op('dve', lambda e: e.tensor_copy(out=u[:, 0:2], in_=carry[:, cidx, :]),
                                 R=['carry.%d' % cidx], W=[uk])
                          op('act', lambda e: e.activation(out=u[:, 2:2 + W], in_=gb[:, 0:W], func=AF.Copy),
                             R=[gbk], W=[uk])
                          if ti == 0:
                              op('dve', lambda e: e.tensor_scalar(out=u[:, 2:130], in0=u[:, 2:130], scalar1=halof[:, 0:1],
                                                                  scalar2=None, op0=ALU.mult), R=[uk, 'halof'], W=[uk])

                  def f_s2(f):
                      f_load_dn(f)
                      for gv in range(2):
                          u, uk = ur_s[f % NUR][gv]
                          cvt, cvk = cv3[f % NCV][gv]
                          cidx = f * 2 + gv
                          ci = f * 6 + gv * 3
                          op('dve', lambda e: e.tensor_copy(out=carry[:, cidx, :], in_=u[:, W:W + 2]),
                             R=[uk], W=['carry.%d' % cidx])
                          op('dve', lambda e: e.tensor_scalar(
                              out=cvt[:, 0:W], in0=u[:, 2:2 + W], scalar1=cw[:, ci + 2:ci + 3],
                              scalar2=cb[:, f * 2 + gv:f * 2 + gv + 1], op0=ALU.mult, op1=ALU.add),
                             R=[uk, 'cw', 'cb'], W=[cvk])
                          op('dve', lambda e: e.scalar_tensor_tensor(
                              out=cvt[:, 0:W], in0=u[:, 1:1 + W], scalar=cw[:, ci + 1:ci + 2], in1=cvt[:, 0:W],
                              op0=ALU.mult, op1=ALU.add), R=[uk, 'cw', cvk], W=[cvk])
                          op('dve', lambda e: e.scalar_tensor_tensor(
                              out=cvt[:, 0:W], in0=u[:, 0:W], scalar=cw[:, ci:ci + 1], in1=cvt[:, 0:W],
                              op0=ALU.mult, op1=ALU.add), R=[uk, 'cw', cvk], W=[cvk])

                  def f_s3(f):
                      sgt, sgk = sg_s[f % NUR]
                      cg, cgk = cv3[f % NCV][0]
                      op('act', lambda e: e.activation(out=sgt[:, 0:W], in_=cg[:, 0:W], func=AF.Sigmoid), R=[cgk], W=[sgk])
                      op('pool', lambda e: e.tensor_tensor(out=sgt[:, 0:W], in0=sgt[:, 0:W], in1=cg[:, 0:W], op=ALU.mult),
                         R=[sgk, cgk], W=[sgk])

                  def f_s4(f):
                      sgt, sgk = sg_s[f % NUR]
                      cvv, cvvk = cv3[f % NCV][1]
                      pr, prk = prod4[f % 4]
                      op('dve', lambda e: e.tensor_tensor(out=pr[:, 0:W], in0=sgt[:, 0:W], in1=cvv[:, 0:W], op=ALU.mult),
                         R=[sgk, cvvk], W=[prk])

                  def f_s5(f):
                      if f % 2 == 0 and f != NF - 1:
                          return
                      fls = [f - 1, f] if f % 2 == 1 else [f]
                      for c in range(ncq):
                          if c0 + c == 0:
                              continue
                          for nb in range(NBN):
                              for q, ff in enumerate(fls):
                                  pr, prk = prod4[ff % 4]
                                  wd, wdk = wdn6[ff % NWD]
                                  op('pe', lambda e: e.matmul(o_ps[:, nb * NBW:(nb + 1) * NBW],
                                                              lhsT=pr[:, c * 128:(c + 1) * 128],
                                                              rhs=wd[:, nb * NBW:(nb + 1) * NBW], start=(q == 0),
                                                              stop=(q == len(fls) - 1)), R=[prk, wdk], W=['o_ps'])
                          op('dve', lambda e: e.tensor_tensor(out=x1[:, c, :], in0=o_ps[:, 0:D], in1=x1[:, c, :], op=ALU.add),
                             R=['o_ps', x1k[c]], W=[x1k[c]])

                  f_load(0)
                  f_load(1)
                  stages = [f_s0, f_s1, f_s2, f_s3, f_s4, f_s5]
                  for t in range(NF + len(stages) - 1):
                      for s in range(len(stages) - 1, -1, -1):
                          f = t - s
                          if 0 <= f < NF:
                              stages[s](f)
                  chk(206)
                  if ti == 0:
                      tr.barrier(engines=['sp'])
                  chk(206)
                  for c in range(ncq):
                      if c0 + c == 0:
                          continue
                      i = ob_i[0] % 2
                      ob_i[0] += 1
                      src = x1[:, c, :]
                      op('dve', lambda e, src=src: e.scalar_tensor_tensor(out=junk[:, :], in0=src, scalar=1.0, in1=src,
                                                                          op0=ALU.mult, op1=ALU.mult, accum_out=st4f[:, 0:1]),
                         R=[x1k[c]], W=['junk', 'st4f'])
                      op('dve', lambda e: e.tensor_scalar(out=st4f[:, 1:2], in0=st4f[:, 0:1], scalar1=1.0 / D, scalar2=EPS,
                                                          op0=ALU.mult, op1=ALU.add), R=['st4f'], W=['st4f'])
                      op('act', lambda e: e.activation(out=st4f[:, 2:3], in_=st4f[:, 1:2], func=AF.Ln), R=['st4f'], W=['st4f'])
                      op('act', lambda e: e.activation(out=st4f[:, 3:4], in_=st4f[:, 2:3], func=AF.Exp, scale=-0.5),
                         R=['st4f'], W=['st4f'])
                      op('dve', lambda e, src=src, i=i: e.scalar_tensor_tensor(out=ob[i][:, :], in0=src, scalar=st4f[:, 3:4],
                                                                               in1=gfin[:, :], op0=ALU.mult, op1=ALU.mult),
                         R=[x1k[c], 'st4f', 'gfin'], W=['xc%d' % i])
                      r0 = (c0 + c - 1) * 128
                      op('sp', lambda e, r0=r0, i=i: e.dma_start(out=y[r0:r0 + 128, :], in_=ob[i][:, :]),
                         R=['xc%d' % i], W=['y%d' % i], dma='y%d' % i)
                      chk(207)
          except _Stop:
            pass
          tr.barrier(engines=['sp'])
          tr.flush()
    return nc


def _fm(v, nchunk):
    return np.ascontiguousarray(np.asarray(v, np.float32).reshape(nchunk, 128).T)


def make_core_inputs(cfg, inputs, theta=10000.0):
    D, KC, NG, S, NT, NBLK, QS, NF, F = cfg.D, cfg.KC, cfg.NG, cfg.S, cfg.NT, cfg.NBLK, cfg.QS, cfg.NF, cfg.F
    HW = cfg.HW
    NJ = HW // 128
    f32 = np.float32
    x = np.asarray(inputs["x"], f32)
    w_in = np.asarray(inputs["w_in"], f32)[0]
    perm64 = np.concatenate([np.arange(32, 64), np.arange(0, 32)])
    offs = dict(qa=0, ka=HW, va=2 * HW, qb=3 * HW, kb=4 * HW, vb=5 * HW)
    wg = np.zeros((NG, 128, KC * 1024), f32)
    for g in range(NG):
        cols = []
        base = np.arange(g * 128, (g + 1) * 128)
        pbase = np.concatenate([g * 128 + perm64, g * 128 + 64 + perm64])
        for nm, idx in (("qa", base), ("qa", pbase), ("ka", base), ("ka", pbase), ("qb", base), ("kb", base),
                        ("va", base), ("vb", base)):
            cols.append(offs[nm] + idx)
        cols = np.concatenate(cols)
        wsel = w_in[:, cols]
        wg[g] = wsel.reshape(KC, 128, 1024).transpose(1, 0, 2).reshape(128, KC * 1024)
    wgate_full = w_in[:, 6 * HW:]
    wgate = wgate_full.reshape(KC, 128, 2 * KC, 128).transpose(1, 2, 0, 3).reshape(128, 2 * KC, KC * 128)
    bgate = _fm(np.asarray(inputs["b_gate"], f32)[0], 2 * KC)
    wa = np.asarray(inputs["w_branch_a"], f32)[0].reshape(NJ, 128, KC, 128).transpose(1, 2, 0, 3).reshape(128, KC, NJ * 128)
    wb = np.asarray(inputs["w_branch_b"], f32)[0].reshape(NJ, 128, KC, 128).transpose(1, 2, 0, 3).reshape(128, KC, NJ * 128)
    wout = np.asarray(inputs["w_out"], f32)[0].reshape(KC, 128, D).transpose(1, 0, 2).reshape(128, KC * D)
    w_up = np.asarray(inputs["w_up"], f32)[0]
    wup = np.zeros((128, NF, KC * 256), f32)
    wu4 = w_up.reshape(KC, 128, 2, NF, 128)
    wup = wu4.transpose(1, 3, 0, 2, 4).reshape(128, NF, KC * 256)
    conv_w = np.asarray(inputs["conv_w"], f32)[0]
    cw = conv_w.reshape(3, 2, NF, 128).transpose(3, 2, 1, 0).reshape(128, NF * 6)
    conv_b = np.asarray(inputs["conv_b"], f32)[0]
    cb = conv_b.reshape(2, NF, 128).transpose(2, 1, 0).reshape(128, NF * 2)
    wdn = np.asarray(inputs["w_down"], f32)[0].reshape(NF, 128, D).transpose(1, 0, 2).reshape(128, NF * D)
    gmix = _fm(np.asarray(inputs["g_mix"], f32)[0], KC)
    gffn = _fm(np.asarray(inputs["g_ffn"], f32)[0], KC)
    gfin = np.ascontiguousarray(np.broadcast_to(np.asarray(inputs["g_final"], f32)[None, :], (128, D)))
    ident = np.eye(128, dtype=f32)
    kk = np.arange(128)
    tneg = -(kk[:, None] >= kk[None, :]).astype(f32)
    eall = np.zeros((32, 32, 128), f32)
    for n in range(32):
        eall[n, n, :] = 1.0
    eall = eall.reshape(32, 32 * 128)
    masks = np.zeros((128, 8, 512), f32)
    qq = np.arange(512)
    for v in range(4):
        masks[:, v, :] = ((v * 128 + kk)[:, None] < qq[None, :])
        masks[:, 4 + v, :] = ((v * 128 + kk)[:, None] <= qq[None, :])
    masks = masks.reshape(128, 8 * 512)
    half = HD // 2
    inv = (theta ** (-np.arange(half, dtype=f32) / half)).astype(f32)
    shared = dict(wg=wg, wgate=np.ascontiguousarray(wgate), bgate=bgate, wa=np.ascontiguousarray(wa),
                  wb=np.ascontiguousarray(wb), wout=np.ascontiguousarray(wout), wup=np.ascontiguousarray(wup),
                  cw=np.ascontiguousarray(cw), cb=np.ascontiguousarray(cb), wdn=np.ascontiguousarray(wdn),
                  gmix=gmix, gffn=gffn, gfin=gfin, ident=ident, tneg=tneg, eall=eall, masks=masks)
    in_maps = []
    for core in range(cfg.B * 4):
        b, j = core // 4, core % 4
        quarters = [(j + 1) % 4, (j + 2) % 4, (j + 3) % 4, j]
        xk = np.concatenate([x[b, q * QS:(q + 1) * QS] for q in quarters], axis=0)
        pos = np.concatenate([np.arange(q * QS, (q + 1) * QS) for q in quarters]).astype(f32)
        ang = pos[None, :] * np.tile(inv, 4)[:, None]
        cosT = np.cos(ang).astype(f32)
        sgn = np.tile(np.concatenate([-np.ones(32, f32), np.ones(32, f32)]), 2)
        sinT = (np.sin(ang) * sgn[:, None]).astype(f32)
        vis = np.array([1.0 if q < j else 0.0 for q in quarters[:3]] + [1.0], f32)
        kb_row = np.repeat(np.where(vis > 0, 0.0, -BIG).astype(f32), QS // 128)
        bf_row = np.full(32, 0.0, f32)
        bf_row[:NBLK] = np.repeat(np.where(vis > 0, 0.0, NEGINF).astype(f32), QS // 256)
        m = dict(shared)
        m.update(xk=np.ascontiguousarray(xk), cosT=cosT, sinT=sinT,
                 kbias=np.ascontiguousarray(np.broadcast_to(kb_row[None, :], (128, NT))),
                 bflag=np.ascontiguousarray(np.broadcast_to(bf_row[None, :], (128, 32))),
                 halof=np.full((128, 1), 1.0 if j > 0 else 0.0, f32))
        in_maps.append(m)
    return in_maps


_CACHE = {}


def kernel(**inputs):
    cfg = Cfg()
    in_maps = make_core_inputs(cfg, inputs)
    if "nc" not in _CACHE:
        _CACHE["nc"] = build(cfg)
    res = run_bass_kernel_spmd(_CACHE["nc"], in_maps, core_ids=list(range(8)))
    out = np.zeros((cfg.B, cfg.S, cfg.D), np.float32)
    for core in range(8):
        b, j = core // 4, core % 4
        out[b, j * cfg.QS:(j + 1) * cfg.QS] = np.asarray(res.results[core]["y"], np.float32)
    return out
```

```python
import contextlib
import numpy as np
import concourse.bass as bass
import concourse.mybir as mybir
from concourse.bass_utils import run_bass_kernel_spmd

F32 = mybir.dt.float32
BF16 = mybir.dt.bfloat16
AF = mybir.ActivationFunctionType
ALU = mybir.AluOpType
AX = mybir.AxisListType

HD = 64
BIG = 30000.0
NEGINF = -1e30
EPS = 1e-6


class Cfg:
    def __init__(s, D=1024, NH=8, QS=2048, F=2816, B=2):
        s.D, s.NH, s.QS, s.F, s.B = D, NH, QS, F, B
        s.KC = D // 128
        s.NG = NH // 2
        s.S = 4 * QS
        s.NT = s.S // 128
        s.NBLK = s.S // 256
        s.Q0 = s.S - QS - 128
        s.NQC = (QS + 128) // 128
        s.NQ = s.NQC * 128
        s.NF = F // 128
        s.HW = NH * HD
        s.stop = 0
        s.tiles = []
        c = 0
        while c < s.NQC:
            n = min(4, s.NQC - c)
            s.tiles.append((c, n))
            c += n


class _Rec:
    def __init__(s):
        s.call = None

    def __getattr__(s, name):
        def f(*a, **k):
            s.call = (name, a, k)
            return s
        return f


class TR:
    ENG = ('sp', 'act', 'dve', 'pool', 'pe')

    def __init__(s, nc, stack):
        s.nc = nc
        s.sem, s.cnt, s.lastw, s.readers = {}, {}, {}, {}
        s.waited = {e: {} for e in s.ENG}
        s.q = {e: [] for e in s.ENG}
        s.children = {}
        s.stack = stack
        s.n = 0

    def _rel(s, k):
        out = [k]
        if '.' in k:
            p = k.split('.')[0]
            out.append(p)
            s.children.setdefault(p, set()).add(k)
        else:
            out.extend(s.children.get(k, ()))
        return out

    def _sem(s, key):
        if key not in s.sem:
            s.sem[key] = s.stack.enter_context(s.nc.semaphore(key))
            s.cnt[key] = 0
        return s.sem[key]

    def op(s, e, fn, R=(), W=(), dma=None):
        deps = {}

        def add(ev):
            if ev is not None and ev[1] > deps.get(ev[0], 0):
                deps[ev[0]] = ev[1]
        for r in R:
            for rr in s._rel(r):
                add(s.lastw.get(rr))
        for w in W:
            for ww in s._rel(w):
                add(s.lastw.get(ww))
                for k, v in s.readers.get(ww, {}).items():
                    add((k, v))
        key, inc = ('E_' + e, 1) if dma is None else ('D_' + dma, 16)
        s._sem(key)
        waits = []
        for k, v in deps.items():
            if e == 'pe' and k == 'E_pe':
                continue
            if s.waited[e].get(k, 0) >= v:
                continue
            waits.append((s.sem[k], v))
            s.waited[e][k] = v
        rec = _Rec()
        fn(rec)
        s.cnt[key] += inc
        s.q[e].append((waits, rec.call, s.sem[key], inc))
        s.n += 1
        ev = (key, s.cnt[key])
        for r in R:
            d = s.readers.setdefault(r, {})
            d[key] = max(d.get(key, 0), ev[1])
        for w in W:
            s.lastw[w] = ev
            s.readers[w] = {}
        return ev

    def barrier(s, engines=None):
        for e in (engines or s.ENG):
            waits = []
            for k, v in s.cnt.items():
                if v > 0 and s.waited[e].get(k, 0) < v:
                    waits.append((s.sem[k], v))
                    s.waited[e][k] = v
            if waits:
                s.q[e].append((waits, None, None, 0))

    def flush(s):
        with s.nc.Block() as block:
            sect = {'sp': block.sync, 'act': block.scalar, 'dve': block.vector, 'pool': block.gpsimd,
                    'pe': block.tensor}
            for e in s.ENG:
                q = s.q[e]
                s.q[e] = []
                if not q:
                    continue

                def body(eng, q=q):
                    for waits, call, sem, inc in q:
                        for (wsem, v) in waits:
                            eng.wait_ge(wsem, v)
                        if call is not None:
                            name, a, k = call
                            getattr(eng, name)(*a, **k).then_inc(sem, inc)
                sect[e](body)


class _Stop(Exception):
    pass


def build(cfg):
    D, KC, NG, S, NT, NBLK, Q0, NQC, NQ, NF, QS = (cfg.D, cfg.KC, cfg.NG, cfg.S, cfg.NT, cfg.NBLK, cfg.Q0,
                                                  cfg.NQC, cfg.NQ, cfg.NF, cfg.QS)
    HW = cfg.HW
    NJ = HW // 128
    scale = HD ** -0.5
    NBW = min(512, D)
    NBN = D // NBW
    nc = bass.Bass("TRN2", target_bir_lowering=False)

    def din(name, shape):
        return nc.dram_tensor(name, list(shape), F32, kind="ExternalInput").ap()
    xk = din("xk", [S, D])
    wg = din("wg", [NG, 128, KC * 1024])
    cosT = din("cosT", [128, S])
    sinT = din("sinT", [128, S])
    kbias_d = din("kbias", [128, NT])
    bflag_d = din("bflag", [128, 32])
    halof_d = din("halof", [128, 1])
    ident_d = din("ident", [128, 128])
    tneg_d = din("tneg", [128, 128])
    eall_d = din("eall", [32, 32 * 128])
    masks_d = din("masks", [128, 8 * 512])
    gmix_d = din("gmix", [128, KC])
    gffn_d = din("gffn", [128, KC])
    gfin_d = din("gfin", [128, D])
    wgate_d = din("wgate", [128, 2 * KC, KC * 128])
    bgate_d = din("bgate", [128, 2 * KC])
    wa_d = din("wa", [128, KC, NJ * 128])
    wb_d = din("wb", [128, KC, NJ * 128])
    wout_d = din("wout", [128, KC * D])
    wup_d = din("wup", [128, NF, KC * 256])
    cw_d = din("cw", [128, NF * 6])
    cb_d = din("cb", [128, NF * 2])
    wdn_d = din("wdn", [128, NF * D])
    y = nc.dram_tensor("y", [QS, D], F32, kind="ExternalOutput").ap()
    hsc = nc.dram_tensor("hsc", [S // 512, 128, KC * 512], BF16).ap()
    WC_COLS = 2 * KC * KC * 128 + 2 * KC * NJ * 128 + KC * D + NF * (KC * 256 + D)
    wcache = nc.dram_tensor("wcache", [128, WC_COLS], BF16).ap()

    with contextlib.ExitStack() as stack:
        tr = TR(nc, stack)
        op = tr.op

        def sb(name, shape, dt=F32):
            return stack.enter_context(nc.sbuf_tensor(name, list(shape), dt))

        def ps(name, shape, dt=F32):
            return stack.enter_context(nc.psum_tensor(name, list(shape), dt))

        ident = sb("s_ident", [128, 128], BF16)
        tneg = sb("s_tneg", [128, 128], BF16)
        onesneg = sb("s_onesneg", [128, 128], BF16)
        eall = sb("s_eall", [32, 32 * 128], BF16)
        masks = sb("s_masks", [128, 8 * 512], BF16)
        kbias = sb("s_kbias", [128, NT])
        bflag = sb("s_bflag", [128, 32])
        halof = sb("s_halof", [128, 1])
        gmix = sb("s_gmix", [128, KC])
        gffn = sb("s_gffn", [128, KC])
        bgate = sb("s_bgate", [128, 2 * KC])
        cw = sb("s_cw", [128, NF * 6])
        cb = sb("s_cb", [128, NF * 2])
        stg = [sb("s_stg0", [128, 2048])]
        stg_i = [0]

        def load_cast(dst, dst_key, src_ap, n, cast_eng='dve'):
            i = stg_i[0] % len(stg)
            stg_i[0] += 1
            st = stg[i]
            op('sp', lambda e: e.dma_start(out=st[:, 0:n], in_=src_ap), W=['stg%d' % i], dma='stg%d' % i)
            if cast_eng == 'act':
                op('act', lambda e: e.activation(out=dst, in_=st[:, 0:n], func=AF.Copy), R=['stg%d' % i], W=[dst_key])
            else:
                op(cast_eng, lambda e: e.tensor_copy(out=dst, in_=st[:, 0:n]), R=['stg%d' % i], W=[dst_key])

        def load_f32(dst, key, src_ap):
            op('sp', lambda e: e.dma_start(out=dst, in_=src_ap), W=[key], dma=key)

        load_cast(ident[:, :], 'ident', ident_d, 128)
        load_cast(tneg[:, :], 'tneg', tneg_d, 128)
        for i in range(4):
            load_cast(masks[:, i * 1024:(i + 1) * 1024], 'masks', masks_d[:, i * 1024:(i + 1) * 1024], 1024)
        op('dve', lambda e: e.memset(onesneg[:, :], -1.0), W=['onesneg'])
        for hh in range(2):
            i = stg_i[0] % len(stg)
            stg_i[0] += 1
            op('sp', lambda e, i=i, hh=hh: e.dma_start(out=stg[i][0:32, 0:2048], in_=eall_d[:, hh * 2048:(hh + 1) * 2048]),
               W=['stg%d' % i], dma='stg%d' % i)
            op('dve', lambda e, i=i, hh=hh: e.tensor_copy(out=eall[:, hh * 2048:(hh + 1) * 2048], in_=stg[i][0:32, 0:2048]),
               R=['stg%d' % i], W=['eall'])
        load_f32(kbias[:, :], 'kbias', kbias_d)
        load_f32(bflag[:, :], 'bflag', bflag_d)
        load_f32(halof[:, :], 'halof', halof_d)
        load_f32(gmix[:, :], 'gmix', gmix_d)
        load_f32(gffn[:, :], 'gffn', gffn_d)
        load_f32(bgate[:, :], 'bgate', bgate_d)
        load_f32(cw[:, :], 'cw', cw_d)
        load_f32(cb[:, :], 'cb', cb_d)

        wc_off = {}
        wc_next = [0]
        pieces = []

        def add_piece(name, src_ap, n):
            wc_off[name] = wc_next[0]
            wc_next[0] += n
            pieces.append((name, src_ap, n))
        wst = min(2048, KC * D)
        for q4 in range(0, KC * D, wst):
            add_piece('wout%d' % q4, wout_d[:, q4:q4 + wst], wst)
        for fc in range(2 * KC):
            add_piece('wg%d' % fc, wgate_d[:, fc, :], KC * 128)
        for fc in range(KC):
            add_piece('wa%d' % fc, wa_d[:, fc, :], NJ * 128)
            add_piece('wb%d' % fc, wb_d[:, fc, :], NJ * 128)
        for f in range(NF):
            add_piece('wup%d' % f, wup_d[:, f, :], KC * 256)
            add_piece('wdn%d' % f, wdn_d[:, f * D:(f + 1) * D], D)
        assert wc_next[0] == WC_COLS, (wc_next[0], WC_COLS)

        yaT = sb("s_yaT", [128, NJ, NQ], BF16)
        ybT = sb("s_ybT", [128, NJ, NQ], BF16)

        xc = [sb("s_xc0", [128, D]), sb("s_xc1", [128, D])]
        junk = sb("s_junk", [128, D], BF16)
        hn2 = [sb("s_hn", [128, D], BF16), sb("s_hn1", [128, D], BF16)]
        st42 = [sb("s_st4", [128, 4]), sb("s_st41", [128, 4])]
        st4 = st42[0]
        nrm_i = [0]
        tp_ps = ps("tp_ps", [128, KC * 128], BF16)
        xc_i = [0]

        def norm_A(src_key, src):
            ni = nrm_i[0] % 2
            nrm_i[0] += 1
            st4, hn = st42[ni], hn2[ni]
            sk, hk, jk = 'st4_%d' % ni, 'hn_%d' % ni, 'junk'
            op('dve', lambda e: e.scalar_tensor_tensor(out=junk[:, :], in0=src, scalar=1.0, in1=src,
                                                       op0=ALU.mult, op1=ALU.mult, accum_out=st4[:, 0:1]),
               R=[src_key], W=[jk, sk])
            op('dve', lambda e: e.tensor_scalar(out=st4[:, 1:2], in0=st4[:, 0:1], scalar1=1.0 / D, scalar2=EPS,
                                                op0=ALU.mult, op1=ALU.add), R=[sk], W=[sk])
            op('act', lambda e: e.activation(out=st4[:, 2:3], in_=st4[:, 1:2], func=AF.Ln), R=[sk], W=[sk])
            op('act', lambda e: e.activation(out=st4[:, 3:4], in_=st4[:, 2:3], func=AF.Exp, scale=-0.5),
               R=[sk], W=[sk])
            op('act', lambda e: e.activation(out=hn[:, :], in_=src, func=AF.Identity, scale=st4[:, 3:4]),
               R=[src_key, sk], W=[hk])
            return (hn, hk)

        def norm_B(state, gt, gkey, dst_fn, dst_key):
            hn, hk = state
            for kc in range(KC):
                op('pe', lambda e, kc=kc: e.transpose(out=tp_ps[:, kc * 128:(kc + 1) * 128],
                                                     in_=hn[:, kc * 128:(kc + 1) * 128], identity=ident[:, :]),
                   R=[hk, 'ident'], W=['tp_ps'])
            for kc in range(KC):
                op('dve', lambda e, kc=kc: e.tensor_scalar(out=dst_fn(kc), in0=tp_ps[:, kc * 128:(kc + 1) * 128],
                                                           scalar1=gt[:, kc:kc + 1], scalar2=None, op0=ALU.mult),
                   R=['tp_ps', gkey], W=[dst_key])

        def norm_T(src_key, src, gt, gkey, dst_fn, dst_key):
            norm_B(norm_A(src_key, src), gt, gkey, dst_fn, dst_key)

        def load_x_chunk(t):
            i = xc_i[0] % 2
            xc_i[0] += 1
            op('sp', lambda e: e.dma_start(out=xc[i][:, :], in_=xk[t * 128:(t + 1) * 128, :]),
               W=['xc%d' % i], dma='xc%d' % i)
            return i

        def chk(k):
            if cfg.stop == k:
                raise _Stop()

        with contextlib.ExitStack() as st1:
          try:
              def sb1(name, shape, dt=F32):
                  return st1.enter_context(nc.sbuf_tensor(name, list(shape), dt))

              def ps1(name, shape, dt=F32):
                  return st1.enter_context(nc.psum_tensor(name, list(shape), dt))
              wgb = sb1("s_wgb", [128, KC * 1024], BF16)
              hT = sb1("s_hT", [128, KC, 512], BF16)
              cs = sb1("s_cs", [128, 512])
              sn = sb1("s_sn", [128, 512])
              kaT = sb1("s_kaT", [128, S], BF16)
              kbT = sb1("s_kbT", [128, S], BF16)
              vA = sb1("s_vA", [128, NT, 2 * 66], BF16)
              vB = sb1("s_vB", [128, NT, 128], BF16)
              qaT = sb1("s_qaT", [128, NQ], BF16)
              qbT = sb1("s_qbT", [128, NQ], BF16)
              km = sb1("s_km", [128, 32])
              kmb = sb1("s_kmb", [128, 32], BF16)
              gm = sb1("s_gm", [128, 32])
              m8 = sb1("s_m8", [128, 8])
              sel = sb1("s_sel", [128, 32])
              sel2 = sb1("s_sel2", [128, 32])
              biasq = sb1("s_biasq", [128, 32], BF16)
              biasT = [sb1("s_biasT0", [32, NQ], BF16), sb1("s_biasT1", [32, NQ], BF16)]
              hT2 = sb1("s_hT2", [128, KC, 512], BF16)
              xtra = [sb1("s_xb%d" % q, [128, 512], BF16) for q in range(max(0, 11 - 2 * KC))]
              ebuf = [sb1("s_e0", [128, 512]), sb1("s_e1", [128, 512])]
              rt1, rt2 = ebuf[0], ebuf[1]
              lsum = sb1("s_lsum", [128, 512], BF16)
              lsum2 = sb1("s_lsum2", [128, 512], BF16)
              ebuf2 = sb1("s_e2", [128, 512])
              ytok = sb1("s_ytok", [128, 4, 128], BF16)
              rden = sb1("s_rden", [128, 1])
              pj = [ps1("pj0", [128, 512]), ps1("pj1", [128, 512])]
              sc_ps = [ps1("sc0", [128, 512]), ps1("sc1", [128, 512])]
              af_ps = ps1("af", [128, 512])
              acc_ps = ps1("acc", [128, 512])
              sm_ps = ps1("sm", [128, 512], BF16)

              chk(100)
              for hh in range(2):
                  op('dve', lambda e, hh=hh: e.memset(vA[:, :, hh * 66 + 64: hh * 66 + 65], 1.0), W=['vA'])

              pj_i = [0]
              wtmp = sb1("s_wtmp", [128, 1024], BF16)

              def precast_gen():
                  for (name, src_ap, n) in pieces:
                      o = wc_off[name]
                      for sub in range(0, n, 1024):
                          m = min(1024, n - sub)
                          op('sp', lambda e: e.dma_start(out=stg[0][:, 0:m], in_=src_ap[:, sub:sub + m]), W=['stg0'],
                             dma='stg0')
                          op('dve', lambda e: e.tensor_copy(out=wtmp[:, 0:m], in_=stg[0][:, 0:m]), R=['stg0'], W=['wtmp'])
                          op('sp', lambda e: e.dma_start(out=wcache[:, o + sub:o + sub + m], in_=wtmp[:, 0:m]),
                             R=['wtmp'], W=['wc_' + name], dma='wcw_pre')
                          yield
              pc_gen = precast_gen()
              pc_step = [0]

              def precast_tick(force=False):
                  pc_step[0] += 1
                  if force or pc_step[0] % 12 == 0:
                      next(pc_gen, None)
              NHB = 2
              cur = [0]
              hTs = [hT, hT2]

              def hTv(kc, bi=None):
                  bi = cur[0] if bi is None else bi
                  return hTs[bi][:, kc, :]

              def hTk(bi=None):
                  return 'hTb%d' % (cur[0] if bi is None else bi)

              def proj(col0, ncols_tok, tok_off):
                  i = pj_i[0] % 2
                  pj_i[0] += 1
                  for kc in range(KC):
                      op('pe', lambda e, kc=kc: e.matmul(pj[i][:, 0:ncols_tok],
                                                         lhsT=wgb[:, kc * 1024 + col0: kc * 1024 + col0 + 128],
                                                         rhs=hTv(kc)[:, tok_off:tok_off + ncols_tok],
                                                         start=(kc == 0), stop=(kc == KC - 1)),
                         R=['wgb', hTk()], W=['pj%d' % i])
                  return i

              def rope_to(dst, dst_key, c_main, c_perm, n, tok_off):
                  i1 = proj(c_main, n, tok_off)
                  op('dve', lambda e: e.tensor_tensor(out=rt1[:, 0:n], in0=pj[i1][:, 0:n],
                                                      in1=cs[:, tok_off:tok_off + n], op=ALU.mult),
                     R=['pj%d' % i1, 'cs'], W=['e0'])
                  i2 = proj(c_perm, n, tok_off)
                  op('dve', lambda e: e.tensor_tensor(out=rt2[:, 0:n], in0=pj[i2][:, 0:n],
                                                      in1=sn[:, tok_off:tok_off + n], op=ALU.mult),
                     R=['pj%d' % i2, 'sn'], W=['e1'])
                  op('pool', lambda e: e.tensor_tensor(out=dst, in0=rt1[:, 0:n], in1=rt2[:, 0:n], op=ALU.add),
                     R=['e0', 'e1'], W=[dst_key])

              chk(1)
              for g in range(NG):
                  for kc in range(0, KC, 2):
                      n = min(2, KC - kc) * 1024
                      load_cast(wgb[:, kc * 1024: kc * 1024 + n], 'wgb', wg[g, :, kc * 1024: kc * 1024 + n], n,
                                cast_eng='dve')
                  chk(11)
                  for tt in range(S // 512):
                      cur[0] = tt % NHB
                      if g == 0:
                          pend = None
                          for c in range(4):
                              t = tt * 4 + c
                              i = load_x_chunk(t)
                              stt = norm_A('xc%d' % i, xc[i][:, :])
                              if pend is not None:
                                  norm_B(pend[0], gmix, 'gmix', lambda kc, c=pend[1]: hTv(kc)[:, c * 128:(c + 1) * 128], hTk())
                              pend = (stt, c)
                              chk(12)
                          norm_B(pend[0], gmix, 'gmix', lambda kc, c=pend[1]: hTv(kc)[:, c * 128:(c + 1) * 128], hTk())
                          op('sp', lambda e: e.dma_start(out=hsc[tt], in_=hTs[cur[0]][:, :, :].rearrange("p k t -> p (k t)")),
                             R=[hTk()], W=['hsc%d' % tt], dma='hscw%d' % cur[0])
                      else:
                          op('sp', lambda e: e.dma_start(out=hTs[cur[0]][:, :, :].rearrange("p k t -> p (k t)"), in_=hsc[tt]),
                             R=['hsc%d' % tt], W=[hTk()], dma='hTl%d' % cur[0])
                      load_f32(cs[:, :], 'cs', cosT[:, tt * 512:(tt + 1) * 512])
                      load_f32(sn[:, :], 'sn', sinT[:, tt * 512:(tt + 1) * 512])
                      tk = 'kv%d' % tt
                      rope_to(kaT[:, tt * 512:(tt + 1) * 512], 'kaT' + tk, 256, 384, 512, 0)
                      chk(13)
                      i = proj(640, 512, 0)
                      op('act', lambda e, i=i: e.activation(out=kbT[:, tt * 512:(tt + 1) * 512], in_=pj[i][:, :],
                                                            func=AF.Copy), R=['pj%d' % i], W=['kbT' + tk])
                      chk(14)
                      for c in range(4):
                          t = tt * 4 + c
                          i = pj_i[0] % 2
                          pj_i[0] += 1
                          for kc in range(KC):
                              op('pe', lambda e, kc=kc, c=c, i=i: e.matmul(
                                  pj[i][:, 0:256], lhsT=hTv(kc)[:, c * 128:(c + 1) * 128],
                                  rhs=wgb[:, kc * 1024 + 768: kc * 1024 + 1024],
                                  start=(kc == 0), stop=(kc == KC - 1)), R=['wgb', hTk()], W=['pj%d' % i])
                          chk(151)
                          for hh in range(2):
                              op('act', lambda e, t=t, i=i, hh=hh: e.activation(
                                  out=vA[:, t, hh * 66: hh * 66 + 64], in_=pj[i][:, hh * 64:(hh + 1) * 64], func=AF.Copy),
                                 R=['pj%d' % i], W=['vA'])
                          chk(152)
                          op('act', lambda e, t=t, i=i: e.activation(out=vB[:, t, :], in_=pj[i][:, 128:256], func=AF.Copy),
                             R=['pj%d' % i], W=['vB'])
                      chk(15)
                      lo = max(tt * 512, Q0)
                      if lo < (tt + 1) * 512:
                          off = lo - tt * 512
                          n = 512 - off
                          rope_to(qaT[:, lo - Q0: lo - Q0 + n], 'qaT', 0, 128, n, off)
                          i = proj(512, n, off)
                          op('act', lambda e, i=i, lo=lo, n=n: e.activation(out=qbT[:, lo - Q0: lo - Q0 + n],
                                                                            in_=pj[i][:, 0:n], func=AF.Copy),
                             R=['pj%d' % i], W=['qbT'])
                  chk(2)
                  kv_keys = ['kv%d' % tt for tt in range(S // 512)]
                  for nb0 in range(0, NBLK, 8):
                      op('dve', lambda e, nb0=nb0: e.tensor_reduce(
                          out=km[:, nb0:nb0 + 8],
                          in_=kaT[:, nb0 * 256:(nb0 + 8) * 256].rearrange("p (n k) -> p n k", k=256),
                          op=ALU.add, axis=AX.X), R=['kaT' + k for k in kv_keys], W=['km'])
                  op('dve', lambda e: e.tensor_scalar(out=kmb[:, 0:NBLK], in0=km[:, 0:NBLK], scalar1=1.0 / 256,
                                                      scalar2=None, op0=ALU.mult), R=['km'], W=['kmb'])
                  for h in range(2):
                      hp = slice(64 * h, 64 * h + 64)
                      for qc in range(NQC):
                          Bq = (Q0 + qc * 128) // 256
                          op('pe', lambda e, qc=qc: e.matmul(sc_ps[0][:, 0:NBLK], lhsT=qaT[hp, qc * 128:(qc + 1) * 128],
                                                            rhs=kmb[hp, 0:NBLK], start=True, stop=True),
                             R=['qaT', 'kmb'], W=['sc0'])
                          op('dve', lambda e: e.memset(gm[:, :], NEGINF), W=['gm'])
                          op('dve', lambda e, Bq=Bq: e.tensor_tensor(out=gm[:, 0:Bq], in0=sc_ps[0][:, 0:Bq],
                                                                     in1=bflag[:, 0:Bq], op=ALU.add),
                             R=['sc0', 'bflag'], W=['gm'])
                          op('dve', lambda e: e.max(out=m8[:, :], in_=gm[:, 0:NBLK]), R=['gm'], W=['m8'])
                          op('dve', lambda e: e.tensor_scalar(out=sel[:, :], in0=gm[:, :], scalar1=m8[:, 2:3],
                                                              scalar2=None, op0=ALU.is_ge), R=['gm', 'm8'], W=['sel'])
                          op('dve', lambda e: e.tensor_scalar(out=sel2[:, :], in0=gm[:, :], scalar1=-1e29,
                                                              scalar2=None, op0=ALU.is_gt), R=['gm'], W=['sel2'])
                          op('dve', lambda e: e.tensor_tensor(out=sel[:, :], in0=sel[:, :], in1=sel2[:, :],
                                                              op=ALU.mult), R=['sel', 'sel2'], W=['sel'])
                          op('dve', lambda e: e.tensor_scalar(out=biasq[:, :], in0=sel[:, :], scalar1=BIG,
                                                              scalar2=-BIG, op0=ALU.mult, op1=ALU.add),
                             R=['sel'], W=['biasq'])
                          op('dve', lambda e, Bq=Bq: e.memset(biasq[:, Bq:Bq + 1], 0.0), W=['biasq'])
                          op('pe', lambda e: e.transpose(out=sm_ps[0:32, 0:128], in_=biasq[:, :], identity=ident[:, :]),
                             R=['biasq', 'ident'], W=['sm'])
                          op('act', lambda e, h=h, qc=qc: e.activation(out=biasT[h][:, qc * 128:(qc + 1) * 128],
                                                                        in_=sm_ps[0:32, 0:128], func=AF.Copy),
                             R=['sm'], W=['biasT%d' % h])
                  chk(3)
                  tr.barrier()
                  ebufs = [(ebuf[0], 'e0'), (ebuf[1], 'e1'), (cs, 'cs'), (sn, 'sn'), (ebuf2, 'e2')]
                  hsl = [(hT[:, kc, :], 'hTs%d' % kc) for kc in range(KC)]
                  bfb = [(hT2[:, kc, :], 'hT2s%d' % kc) for kc in range(KC)] + \
                        [(xtra[q], 'xb%d' % q) for q in range(len(xtra))] + hsl
                  assert len(bfb) >= 10
                  sp_b, E_b, a_b, p_b = bfb[0:3], bfb[3:5], bfb[5:8], bfb[8:8 + 3] if len(bfb) >= 11 else bfb[8:10]
                  lsums = [(lsum, 'lsum'), (lsum2, 'lsum2')]
                  fbanks = [(sc_ps[0], 'sc0'), (sc_ps[1], 'sc1'), (pj[0], 'pj0'), (pj[1], 'pj1'), (af_ps, 'af'), (acc_ps, 'acc')]

                  def run_pipeline(nblocks, stages):
                      ns = len(stages)
                      for t in range(nblocks + ns - 1):
                          precast_tick()
                          for s in range(ns - 1, -1, -1):
                              i = t - s
                              if 0 <= i < nblocks:
                                  stages[s](i)

                  for (c0, ncq) in cfg.tiles:
                      W = ncq * 128
                      q0 = Q0 + c0 * 128
                      qs = slice(c0 * 128, c0 * 128 + W)
                      kb_max = (q0 + W) // 128 - 1
                      kb_diag = q0 // 128
                      mb = [(kb, h) for kb in range(kb_max + 1) for h in range(2)]
                      m_sc = fbanks[0:4]
                      m_acc = [fbanks[5], fbanks[4]]

                      def m_s0(i):
                          kb, h = mb[i]
                          hp = slice(64 * h, 64 * h + 64)
                          sc, sck = m_sc[i % 4]
                          op('pe', lambda e: e.matmul(sc[:, 0:W], lhsT=kaT[hp, kb * 128:(kb + 1) * 128], rhs=qaT[hp, qs],
                                                      start=True, stop=False), R=['kaTkv%d' % (kb // 4), 'qaT'], W=[sck])
                          n = kb // 2
                          op('pe', lambda e: e.matmul(sc[:, 0:W], lhsT=eall[:, n * 128:(n + 1) * 128], rhs=biasT[h][:, qs],
                                                      start=False, stop=True), R=['eall', 'biasT%d' % h], W=[sck])

                      def m_s1(i):
                          kb, h = mb[i]
                          sc, sck = m_sc[i % 4]
                          p, pk = p_b[i % len(p_b)]
                          op('act', lambda e: e.activation(out=p[:, 0:W], in_=sc[:, 0:W], func=AF.Exp, scale=scale),
                             R=[sck], W=[pk])
                          if kb >= kb_diag:
                              v = 4 + (kb - kb_diag)
                              op('dve', lambda e: e.tensor_tensor(out=p[:, 0:W], in0=p[:, 0:W],
                                                                  in1=masks[:, v * 512: v * 512 + W], op=ALU.mult),
                                 R=[pk, 'masks'], W=[pk])

                      def m_s2(i):
                          kb, h = mb[i]
                          p, pk = p_b[i % len(p_b)]
                          acc, acck = m_acc[h]
                          for c in range(ncq):
                              op('pe', lambda e, c=c: e.matmul(acc[:, c * 65:(c + 1) * 65], lhsT=p[:, c * 128:(c + 1) * 128],
                                                               rhs=vA[:, kb, h * 66: h * 66 + 65],
                                                               start=(kb == 0 and c == 0),
                                                               stop=(kb == kb_max and c == ncq - 1)),
                                 R=[pk, 'vA'], W=[acck])
                      run_pipeline(len(mb), [m_s0, m_s1, m_s2])
                      for h in range(2):
                          acc, acck = m_acc[h]
                          for c in range(ncq):
                              op('dve', lambda e, c=c: e.reciprocal(out=rden[:, :], in_=acc[:, c * 65 + 64: c * 65 + 65]),
                                 R=[acck], W=['rden'])
                              op('dve', lambda e, c=c, h=h: e.tensor_scalar(out=ytok[:, c, 64 * h:64 * h + 64],
                                                                            in0=acc[:, c * 65: c * 65 + 64],
                                                                            scalar1=rden[:, 0:1], scalar2=None,
                                                                            op0=ALU.mult),
                                 R=[acck, 'rden'], W=['ytok'])
                      for c in range(ncq):
                          op('pe', lambda e, c=c: e.transpose(out=sm_ps[:, 128:256], in_=ytok[:, c, :], identity=ident[:, :]),
                             R=['ytok', 'ident'], W=['sm'])
                          op('act', lambda e, c=c: e.activation(out=yaT[:, g, (c0 + c) * 128:(c0 + c + 1) * 128],
                                                                in_=sm_ps[:, 128:256], func=AF.Copy),
                             R=['sm'], W=['yaT'])
                      chk(4)
                      sbk = [(kb, h) for kb in range(kb_max, -1, -1) for h in range(2)]
                      s_sc = fbanks[0:2]
                      s_af = [fbanks[4], fbanks[2]]
                      s_acc = [fbanks[5], fbanks[3]]

                      def s_s0(i):
                          kb, h = sbk[i]
                          hp = slice(64 * h, 64 * h + 64)
                          sc, sck = s_sc[i % 2]
                          op('pe', lambda e: e.matmul(sc[:, 0:W], lhsT=kbT[hp, kb * 128:(kb + 1) * 128], rhs=qbT[hp, qs],
                                                      start=True, stop=True), R=['kbTkv%d' % (kb // 4), 'qbT'], W=[sck])

                      def s_s1(i):
                          kb, h = sbk[i]
                          sc, sck = s_sc[i % 2]
                          eb, ek = ebufs[i % 5]
                          op('act', lambda e: e.activation(out=eb[:, 0:W], in_=sc[:, 0:W], func=AF.Exp, scale=scale,
                                                           bias=kbias[:, kb:kb + 1]), R=[sck, 'kbias'], W=[ek])
                          if kb >= kb_diag:
                              v = kb - kb_diag
                              op('dve', lambda e: e.tensor_tensor(out=eb[:, 0:W], in0=eb[:, 0:W],
                                                                  in1=masks[:, v * 512: v * 512 + W], op=ALU.mult),
                                 R=[ek, 'masks'], W=[ek])

                      def s_s2(i):
                          eb, ek = ebufs[i % 5]
                          spb, spk = sp_b[i % 3]
                          op('act', lambda e: e.activation(out=spb[:, 0:W], in_=eb[:, 0:W], func=AF.Ln, bias=1.0),
                             R=[ek], W=[spk])

                      def s_s3(i):
                          kb, h = sbk[i]
                          first = (kb == kb_max)
                          spb, spk = sp_b[i % 3]
                          af, afk = s_af[i % 2]
                          ls, lsk = lsums[h]
                          op('pe', lambda e: e.matmul(af[:, 0:W], lhsT=tneg[:, :], rhs=spb[:, 0:W], start=True, stop=first),
                             R=['tneg', spk], W=[afk])
                          if not first:
                              op('pe', lambda e: e.matmul(af[:, 0:W], lhsT=onesneg[:, :], rhs=ls[:, 0:W], start=False,
                                                          stop=True), R=['onesneg', lsk], W=[afk])
                          if kb != 0:
                              if first:
                                  op('dve', lambda e: e.tensor_copy(out=ls[:, 0:W], in_=spb[:, 0:W]), R=[spk], W=[lsk])
                              else:
                                  op('pool', lambda e: e.tensor_tensor(out=ls[:, 0:W], in0=ls[:, 0:W], in1=spb[:, 0:W],
                                                                       op=ALU.add), R=[spk, lsk], W=[lsk])

                      def s_s4(i):
                          af, afk = s_af[i % 2]
                          Eb, Ek = E_b[i % 2]
                          op('act', lambda e: e.activation(out=Eb[:, 0:W], in_=af[:, 0:W], func=AF.Exp), R=[afk], W=[Ek])

                      def s_s5(i):
                          eb, ek = ebufs[i % 5]
                          Eb, Ek = E_b[i % 2]
                          ab, ak = a_b[i % 3]
                          op('dve', lambda e: e.tensor_tensor(out=ab[:, 0:W], in0=eb[:, 0:W], in1=Eb[:, 0:W], op=ALU.mult),
                             R=[ek, Ek], W=[ak])

                      def s_s6(i):
                          kb, h = sbk[i]
                          hp = slice(64 * h, 64 * h + 64)
                          ab, ak = a_b[i % 3]
                          acc, acck = s_acc[h]
                          op('pe', lambda e: e.matmul(acc[hp, 0:W], lhsT=vB[:, kb, 64 * h:64 * h + 64], rhs=ab[:, 0:W],
                                                      start=(kb == kb_max), stop=(kb == 0)), R=['vB', ak], W=[acck])
                      run_pipeline(len(sbk), [s_s0, s_s1, s_s2, s_s3, s_s4, s_s5, s_s6])
                      for h in range(2):
                          hp = slice(64 * h, 64 * h + 64)
                          acc, acck = s_acc[h]
                          op('act', lambda e: e.activation(out=ybT[hp, g, qs], in_=acc[hp, 0:W], func=AF.Copy),
                             R=[acck], W=['ybT'])
                  tr.barrier()
              for _ in pc_gen:
                  pass

          except _Stop:
            pass
          if 0 < cfg.stop < 200:
            tr.barrier(engines=['sp'])
          tr.flush()
        if 0 < cfg.stop < 200:
            return nc
        tr.barrier()
        with contextlib.ExitStack() as st2:
          try:
              def sb2(name, shape, dt=F32):
                  return st2.enter_context(nc.sbuf_tensor(name, list(shape), dt))

              def ps2(name, shape, dt=F32):
                  return st2.enter_context(nc.psum_tensor(name, list(shape), dt))
              stg.append(sb2("s_stg1", [128, 2048]))
              x1 = sb2("s_x1", [128, 4, D])
              hTo = sb2("s_hTo", [128, KC, 512], BF16)
              gT = sb2("s_gT", [128, 2 * KC, 512], BF16)
              mT = sb2("s_mT", [128, KC, 512], BF16)
              mt1 = sb2("s_mt1", [128, 512])
              mt2 = sb2("s_mt2", [128, 512])
              wsm = [sb2("s_wsm%d" % q, [128, 1024], BF16) for q in range(4)]
              woutb = sb2("s_woutb", [128, KC * D], BF16)
              h2T = sb2("s_h2T", [128, KC, 512], BF16)
              ur = [sb2("s_ur0", [128, 2 + 512]), sb2("s_ur1", [128, 2 + 512])]
              carry = sb2("s_carry", [128, NF * 2, 2])
              cv = [sb2("s_cv0", [128, 512]), sb2("s_cv1", [128, 512])]
              sg = sb2("s_sg", [128, 512])
              urs = [[(ur[0], 'ur0'), (ur[1], 'ur1')],
                     [(stg[1][:, 0:514], 'stg1.a'), (stg[1][:, 514:1028], 'stg1.b')]]
              cvs = [[(cv[0], 'cv0'), (cv[1], 'cv1')],
                     [(stg[0][:, 0:512], 'stg0.a'), (stg[0][:, 512:1024], 'stg0.b')]]
              sgs = [(sg, 'sg'), (stg[0][:, 1024:1536], 'stg0.c')]
              prod = sb2("s_prod", [128, 2, 512], BF16)
              wup = [sb2("s_wup%d" % q, [128, KC * 256], BF16) for q in range(4)]
              wdn = [sb2("s_wdn%d" % q, [128, D], BF16) for q in range(4)]
              gfin = sb2("s_gfin", [128, D])
              st4f = sb2("s_st4f", [128, 4])
              ob = [xc[0], xc[1]]
              g_ps = [ps2("g0", [128, 512]), ps2("g1", [128, 512])]
              b_ps = [ps2("b0", [128, 512]), ps2("b1", [128, 512])]
              o_ps = ps2("o_ps", [128, 1024])
              wsm_i = [0]
              load_f32(gfin[:, :], 'gfin', gfin_d)
              chk(200)

              def cached_load(dst, dst_key, src_ap, n, name, first, eng='act'):
                  o = wc_off[name]
                  first = False
                  if first:
                      load_cast(dst, dst_key, src_ap, n, cast_eng=eng)
                      op('sp', lambda e: e.dma_start(out=wcache[:, o:o + n], in_=dst), R=[dst_key], W=['wc_' + name],
                         dma='wcw_' + dst_key)
                  else:
                      op('sp', lambda e: e.dma_start(out=dst, in_=wcache[:, o:o + n]), R=['wc_' + name], W=[dst_key],
                         dma='L_' + dst_key)

              def wload(src_ap, n, name, first, eng='act'):
                  i = wsm_i[0] % 4
                  wsm_i[0] += 1
                  cached_load(wsm[i][:, 0:n], 'wsm%d' % i, src_ap, n, name, first, eng=eng)
                  return i

              gi = [0]
              ob_i = [0]
              for ti, (c0, ncq) in enumerate(cfg.tiles):
                  W = ncq * 128
                  qs = slice(c0 * 128, c0 * 128 + W)
                  x1k = ['x1_%d' % c for c in range(ncq)]
                  for c in range(ncq):
                      t = Q0 // 128 + c0 + c
                      op('sp', lambda e, c=c, t=t: e.dma_start(out=x1[:, c, :], in_=xk[t * 128:(t + 1) * 128, :]),
                         W=[x1k[c]], dma=x1k[c])
                      norm_T(x1k[c], x1[:, c, :], gmix, 'gmix',
                             lambda kc, c=c: hTo[:, kc, c * 128:(c + 1) * 128], 'hTo')
                      chk(201)
                  wst = min(2048, KC * D)
                  for q4 in range(0, KC * D, wst):
                      cached_load(woutb[:, q4:q4 + wst], 'woutb', wout_d[:, q4:q4 + wst], wst, 'wout%d' % q4, ti == 0)
                  for fc in range(2 * KC):
                      wi = wload(wgate_d[:, fc, :], KC * 128, 'wg%d' % fc, ti == 0)
                      b = gi[0] % 2
                      gi[0] += 1
                      for kc in range(KC):
                          op('pe', lambda e, kc=kc, wi=wi, b=b: e.matmul(g_ps[b][:, 0:W], lhsT=wsm[wi][:, kc * 128:(kc + 1) * 128],
                                                                         rhs=hTo[:, kc, 0:W], start=(kc == 0),
                                                                         stop=(kc == KC - 1)),
                             R=['wsm%d' % wi, 'hTo'], W=['g%d' % b])
                      op('act', lambda e, fc=fc, b=b: e.activation(out=gT[:, fc, 0:W], in_=g_ps[b][:, 0:W],
                                                                   func=AF.Sigmoid, bias=bgate[:, fc:fc + 1]),
                         R=['g%d' % b, 'bgate'], W=['gT'])
                      chk(202)
                  for fc in range(KC):
                      wia = wload(wa_d[:, fc, :], NJ * 128, 'wa%d' % fc, ti == 0, eng='dve')
                      wib = wload(wb_d[:, fc, :], NJ * 128, 'wb%d' % fc, ti == 0, eng='dve')
                      b = gi[0] % 2
                      gi[0] += 1
                      for j in range(NJ):
                          op('pe', lambda e, j=j, wia=wia, b=b: e.matmul(g_ps[b][:, 0:W], lhsT=wsm[wia][:, j * 128:(j + 1) * 128],
                                                                         rhs=yaT[:, j, qs], start=(j == 0), stop=(j == NJ - 1)),
                             R=['wsm%d' % wia, 'yaT'], W=['g%d' % b])
                      for j in range(NJ):
                          op('pe', lambda e, j=j, wib=wib, b=b: e.matmul(b_ps[b][:, 0:W], lhsT=wsm[wib][:, j * 128:(j + 1) * 128],
                                                                         rhs=ybT[:, j, qs], start=(j == 0), stop=(j == NJ - 1)),
                             R=['wsm%d' % wib, 'ybT'], W=['b%d' % b])
                      op('dve', lambda e, fc=fc, b=b: e.tensor_tensor(out=mt1[:, 0:W], in0=g_ps[b][:, 0:W],
                                                                      in1=gT[:, fc, 0:W], op=ALU.mult),
                         R=['g%d' % b, 'gT'], W=['mt1'])
                      op('dve', lambda e, fc=fc, b=b: e.tensor_tensor(out=mt2[:, 0:W], in0=b_ps[b][:, 0:W],
                                                                      in1=gT[:, KC + fc, 0:W], op=ALU.mult),
                         R=['b%d' % b, 'gT'], W=['mt2'])
                      op('pool', lambda e, fc=fc: e.tensor_tensor(out=mT[:, fc, 0:W], in0=mt1[:, 0:W], in1=mt2[:, 0:W],
                                                                  op=ALU.add), R=['mt1', 'mt2'], W=['mT'])
                      chk(203)
                  for c in range(ncq):
                      for nb in range(NBN):
                          for kc in range(KC):
                              op('pe', lambda e, kc=kc, c=c, nb=nb: e.matmul(
                                  o_ps[:, nb * NBW:(nb + 1) * NBW], lhsT=mT[:, kc, c * 128:(c + 1) * 128],
                                  rhs=woutb[:, kc * D + nb * NBW: kc * D + nb * NBW + NBW], start=(kc == 0),
                                  stop=(kc == KC - 1)), R=['woutb', 'mT'], W=['o_ps'])
                      op('dve', lambda e, c=c: e.tensor_tensor(out=x1[:, c, :], in0=o_ps[:, 0:D], in1=x1[:, c, :],
                                                               op=ALU.add), R=['o_ps', x1k[c]], W=[x1k[c]])
                      norm_T(x1k[c], x1[:, c, :], gffn, 'gffn',
                             lambda kc, c=c: h2T[:, kc, c * 128:(c + 1) * 128], 'h2T')
                      chk(204)
                  gbanks = [(g_ps[0], 'g0'), (g_ps[1], 'g1'), (b_ps[0], 'b0'), (b_ps[1], 'b1')]
                  if ti == 0:
                      cv3 = [cvs[0], [(mt1, 'mt1'), (mt2, 'mt2')]]
                      ur_s, sg_s = [urs[0]], [sgs[0]]
                  else:
                      cv3 = cvs + [[(mt1, 'mt1'), (mt2, 'mt2')]]
                      ur_s, sg_s = urs, sgs
                  NCV, NUR = len(cv3), len(ur_s)
                  prod4 = [(masks[:, q * 512:(q + 1) * 512], 'masks.p%d' % q) for q in range(4)]
                  wdn6 = [(wdn[q][:, 0:D], 'wdn%d' % q) for q in range(4)]
                  if D <= 1024:
                      wdn6 += [(masks[:, 2048 + q * 1024: 2048 + q * 1024 + D], 'masks.w%d' % q) for q in range(2)]
                  NWD = len(wdn6)

                  def f_load(f):
                      if f >= NF:
                          return
                      cached_load(wup[f % 4][:, :], 'wup%d' % (f % 4), wup_d[:, f, :], KC * 256, 'wup%d' % f, ti == 0,
                                  eng=('act' if f % 2 == 0 else 'dve'))

                  def f_load_dn(f):
                      wd, wdk = wdn6[f % NWD]
                      cached_load(wd, wdk, wdn_d[:, f * D:(f + 1) * D], D, 'wdn%d' % f, ti == 0,
                                  eng=('dve' if f % 2 == 0 else 'act'))

                  def f_s0(f):
                      f_load(f + 2)
                      for gv in range(2):
                          gb, gbk = gbanks[(f % 2) * 2 + gv]
                          for kc in range(KC):
                              op('pe', lambda e, kc=kc: e.matmul(
                                  gb[:, 0:W], lhsT=wup[f % 4][:, kc * 256 + gv * 128: kc * 256 + gv * 128 + 128],
                                  rhs=h2T[:, kc, 0:W], start=(kc == 0), stop=(kc == KC - 1)),
                                 R=['wup%d' % (f % 4), 'h2T'], W=[gbk])

                  def f_s1(f):
                      for gv in range(2):
                          gb, gbk = gbanks[(f % 2) * 2 + gv]
                          u, uk = ur_s[f % NUR][gv]
                          cidx = f * 2 + gv
                          if ti == 0:
                              op('dve', lambda e: e.memset(u[:, 0:2], 0.0), W=[uk])
                          else:
                              op('dve', lambda e: e.tensor_copy(out=u[:, 0:2], in_=carry[:, cidx, :]),
                                 R=['carry.%d' % cidx], W=[uk])
                          op('act', lambda e: e.activation(out=u[:, 2:2 + W], in_=gb[:, 0:W], func=AF.Copy),
                             R=[gbk], W=[uk])
                          if ti == 0:
                              op('dve', lambda e: e.tensor_scalar(out=u[:, 2:130], in0=u[:, 2:130], scalar1=halof[:, 0:1],
                                                                  scalar2=None, op0=ALU.mult), R=[uk, 'halof'], W=[uk])

                  def f_s2(f):
                      f_load_dn(f)
                      for gv in range(2):
                          u, uk = ur_s[f % NUR][gv]
                          cvt, cvk = cv3[f % NCV][gv]
                          cidx = f * 2 + gv
                          ci = f * 6 + gv * 3
                          op('dve', lambda e: e.tensor_copy(out=carry[:, cidx, :], in_=u[:, W:W + 2]),
                             R=[uk], W=['carry.%d' % cidx])
                          op('dve', lambda e: e.tensor_scalar(
                              out=cvt[:, 0:W], in0=u[:, 2:2 + W], scalar1=cw[:, ci + 2:ci + 3],
                              scalar2=cb[:, f * 2 + gv:f * 2 + gv + 1], op0=ALU.mult, op1=ALU.add),
                             R=[uk, 'cw', 'cb'], W=[cvk])
                          op('dve', lambda e: e.scalar_tensor_tensor(
                              out=cvt[:, 0:W], in0=u[:, 1:1 + W], scalar=cw[:, ci + 1:ci + 2], in1=cvt[:, 0:W],
                              op0=ALU.mult, op1=ALU.add), R=[uk, 'cw', cvk], W=[cvk])
                          op('dve', lambda e: e.scalar_tensor_tensor(
                              out=cvt[:, 0:W], in0=u[:, 0:W], scalar=cw[:, ci:ci + 1], in1=cvt[:, 0:W],
                              op0=ALU.mult, op1=ALU.add), R=[uk, 'cw', cvk], W=[cvk])

                  def f_s3(f):
                      sgt, sgk = sg_s[f % NUR]
                      cg, cgk = cv3[f % NCV][0]
                      op('act', lambda e: e.activation(out=sgt[:, 0:W], in_=cg[:, 0:W], func=AF.Sigmoid), R=[cgk], W=[sgk])
                      op('pool', lambda e: e.tensor_tensor(out=sgt[:, 0:W], in0=sgt[:, 0:W], in1=cg[:, 0:W], op=ALU.mult),
                         R=[sgk, cgk], W=[sgk])

                  def f_s4(f):
                      sgt, sgk = sg_s[f % NUR]
                      cvv, cvvk = cv3[f % NCV][1]
                      pr, prk = prod4[f % 4]
                      op('dve', lambda e: e.tensor_tensor(out=pr[:, 0:W], in0=sgt[:, 0:W], in1=cvv[:, 0:W], op=ALU.mult),
                         R=[sgk, cvvk], W=[prk])

                  def f_s5(f):
                      if f % 2 == 0 and f != NF - 1:
                          return
                      fls = [f - 1, f] if f % 2 == 1 else [f]
                      for c in range(ncq):
                          if c0 + c == 0:
                              continue
                          for nb in range(NBN):
                              for q, ff in enumerate(fls):
                                  pr, prk = prod4[ff % 4]
                                  wd, wdk = wdn6[ff % NWD]
                                  op('pe', lambda e: e.matmul(o_ps[:, nb * NBW:(nb + 1) * NBW],
                                                              lhsT=pr[:, c * 128:(c + 1) * 128],
                                                              rhs=wd[:, nb * NBW:(nb + 1) * NBW], start=(q == 0),
                                                              stop=(q == len(fls) - 1)), R=[prk, wdk], W=['o_ps'])
                          op('dve', lambda e: e.tensor_tensor(out=x1[:, c, :], in0=o_ps[:, 0:D], in1=x1[:, c, :], op=ALU.add),
                             R=['o_ps', x1k[c]], W=[x1k[c]])

                  f_load(0)
                  f_load(1)
                  stages = [f_s0, f_s1, f_s2, f_s3, f_s4, f_s5]
                  for t in range(NF + len(stages) - 1):
                      for s in range(len(stages) - 1, -1, -1):
                          f = t - s
                          if 0 <= f < NF:
                              stages[s](f)
                  chk(206)
                  if ti == 0:
                      tr.barrier(engines=['sp'])
                  chk(206)
                  for c in range(ncq):
                      if c0 + c == 0:
                          continue
                      i = ob_i[0] % 2
                      ob_i[0] += 1
                      src = x1[:, c, :]
                      op('dve', lambda e, src=src: e.scalar_tensor_tensor(out=junk[:, :], in0=src, scalar=1.0, in1=src,
                                                                          op0=ALU.mult, op1=ALU.mult, accum_out=st4f[:, 0:1]),
                         R=[x1k[c]], W=['junk', 'st4f'])
                      op('dve', lambda e: e.tensor_scalar(out=st4f[:, 1:2], in0=st4f[:, 0:1], scalar1=1.0 / D, scalar2=EPS,
                                                          op0=ALU.mult, op1=ALU.add), R=['st4f'], W=['st4f'])
                      op('act', lambda e: e.activation(out=st4f[:, 2:3], in_=st4f[:, 1:2], func=AF.Ln), R=['st4f'], W=['st4f'])
                      op('act', lambda e: e.activation(out=st4f[:, 3:4], in_=st4f[:, 2:3], func=AF.Exp, scale=-0.5),
                         R=['st4f'], W=['st4f'])
                      op('dve', lambda e, src=src, i=i: e.scalar_tensor_tensor(out=ob[i][:, :], in0=src, scalar=st4f[:, 3:4],
                                                                               in1=gfin[:, :], op0=ALU.mult, op1=ALU.mult),
                         R=[x1k[c], 'st4f', 'gfin'], W=['xc%d' % i])
                      r0 = (c0 + c - 1) * 128
                      op('sp', lambda e, r0=r0, i=i: e.dma_start(out=y[r0:r0 + 128, :], in_=ob[i][:, :]),
                         R=['xc%d' % i], W=['y%d' % i], dma='y%d' % i)
                      chk(207)
          except _Stop:
            pass
          tr.barrier(engines=['sp'])
          tr.flush()
    return nc


def _fm(v, nchunk):
    return np.ascontiguousarray(np.asarray(v, np.float32).reshape(nchunk, 128).T)


def make_core_inputs(cfg, inputs, theta=10000.0):
    D, KC, NG, S, NT, NBLK, QS, NF, F = cfg.D, cfg.KC, cfg.NG, cfg.S, cfg.NT, cfg.NBLK, cfg.QS, cfg.NF, cfg.F
    HW = cfg.HW
    NJ = HW // 128
    f32 = np.float32
    x = np.asarray(inputs["x"], f32)
    w_in = np.asarray(inputs["w_in"], f32)[0]
    perm64 = np.concatenate([np.arange(32, 64), np.arange(0, 32)])
    offs = dict(qa=0, ka=HW, va=2 * HW, qb=3 * HW, kb=4 * HW, vb=5 * HW)
    wg = np.zeros((NG, 128, KC * 1024), f32)
    for g in range(NG):
        cols = []
        base = np.arange(g * 128, (g + 1) * 128)
        pbase = np.concatenate([g * 128 + perm64, g * 128 + 64 + perm64])
        for nm, idx in (("qa", base), ("qa", pbase), ("ka", base), ("ka", pbase), ("qb", base), ("kb", base),
                        ("va", base), ("vb", base)):
            cols.append(offs[nm] + idx)
        cols = np.concatenate(cols)
        wsel = w_in[:, cols]
        wg[g] = wsel.reshape(KC, 128, 1024).transpose(1, 0, 2).reshape(128, KC * 1024)
    wgate_full = w_in[:, 6 * HW:]
    wgate = wgate_full.reshape(KC, 128, 2 * KC, 128).transpose(1, 2, 0, 3).reshape(128, 2 * KC, KC * 128)
    bgate = _fm(np.asarray(inputs["b_gate"], f32)[0], 2 * KC)
    wa = np.asarray(inputs["w_branch_a"], f32)[0].reshape(NJ, 128, KC, 128).transpose(1, 2, 0, 3).reshape(128, KC, NJ * 128)
    wb = np.asarray(inputs["w_branch_b"], f32)[0].reshape(NJ, 128, KC, 128).transpose(1, 2, 0, 3).reshape(128, KC, NJ * 128)
    wout = np.asarray(inputs["w_out"], f32)[0].reshape(KC, 128, D).transpose(1, 0, 2).reshape(128, KC * D)
    w_up = np.asarray(inputs["w_up"], f32)[0]
    wup = np.zeros((128, NF, KC * 256), f32)
    wu4 = w_up.reshape(KC, 128, 2, NF, 128)
    wup = wu4.transpose(1, 3, 0, 2, 4).reshape(128, NF, KC * 256)
    conv_w = np.asarray(inputs["conv_w"], f32)[0]
    cw = conv_w.reshape(3, 2, NF, 128).transpose(3, 2, 1, 0).reshape(128, NF * 6)
    conv_b = np.asarray(inputs["conv_b"], f32)[0]
    cb = conv_b.reshape(2, NF, 128).transpose(2, 1, 0).reshape(128, NF * 2)
    wdn = np.asarray(inputs["w_down"], f32)[0].reshape(NF, 128, D).transpose(1, 0, 2).reshape(128, NF * D)
    gmix = _fm(np.asarray(inputs["g_mix"], f32)[0], KC)
    gffn = _fm(np.asarray(inputs["g_ffn"], f32)[0], KC)
    gfin = np.ascontiguousarray(np.broadcast_to(np.asarray(inputs["g_final"], f32)[None, :], (128, D)))
    ident = np.eye(128, dtype=f32)
    kk = np.arange(128)
    tneg = -(kk[:, None] >= kk[None, :]).astype(f32)
    eall = np.zeros((32, 32, 128), f32)
    for n in range(32):
        eall[n, n, :] = 1.0
    eall = eall.reshape(32, 32 * 128)
    masks = np.zeros((128, 8, 512), f32)
    qq = np.arange(512)
    for v in range(4):
        masks[:, v, :] = ((v * 128 + kk)[:, None] < qq[None, :])
        masks[:, 4 + v, :] = ((v * 128 + kk)[:, None] <= qq[None, :])
    masks = masks.reshape(128, 8 * 512)
    half = HD // 2
    inv = (theta ** (-np.arange(half, dtype=f32) / half)).astype(f32)
    shared = dict(wg=wg, wgate=np.ascontiguousarray(wgate), bgate=bgate, wa=np.ascontiguousarray(wa),
                  wb=np.ascontiguousarray(wb), wout=np.ascontiguousarray(wout), wup=np.ascontiguousarray(wup),
                  cw=np.ascontiguousarray(cw), cb=np.ascontiguousarray(cb), wdn=np.ascontiguousarray(wdn),
                  gmix=gmix, gffn=gffn, gfin=gfin, ident=ident, tneg=tneg, eall=eall, masks=masks)
    in_maps = []
    for core in range(cfg.B * 4):
        b, j = core // 4, core % 4
        quarters = [(j + 1) % 4, (j + 2) % 4, (j + 3) % 4, j]
        xk = np.concatenate([x[b, q * QS:(q + 1) * QS] for q in quarters], axis=0)
        pos = np.concatenate([np.arange(q * QS, (q + 1) * QS) for q in quarters]).astype(f32)
        ang = pos[None, :] * np.tile(inv, 4)[:, None]
        cosT = np.cos(ang).astype(f32)
        sgn = np.tile(np.concatenate([-np.ones(32, f32), np.ones(32, f32)]), 2)
        sinT = (np.sin(ang) * sgn[:, None]).astype(f32)
        vis = np.array([1.0 if q < j else 0.0 for q in quarters[:3]] + [1.0], f32)
        kb_row = np.repeat(np.where(vis > 0, 0.0, -BIG).astype(f32), QS // 128)
        bf_row = np.full(32, 0.0, f32)
        bf_row[:NBLK] = np.repeat(np.where(vis > 0, 0.0, NEGINF).astype(f32), QS // 256)
        m = dict(shared)
        m.update(xk=np.ascontiguousarray(xk), cosT=cosT, sinT=sinT,
                 kbias=np.ascontiguousarray(np.broadcast_to(kb_row[None, :], (128, NT))),
                 bflag=np.ascontiguousarray(np.broadcast_to(bf_row[None, :], (128, 32))),
                 halof=np.full((128, 1), 1.0 if j > 0 else 0.0, f32))
        in_maps.append(m)
    return in_maps


_CACHE = {}


def kernel(**inputs):
    cfg = Cfg()
    in_maps = make_core_inputs(cfg, inputs)
    if "nc" not in _CACHE:
        _CACHE["nc"] = build(cfg)
    res = run_bass_kernel_spmd(_CACHE["nc"], in_maps, core_ids=list(range(8)))
    out = np.zeros((cfg.B, cfg.S, cfg.D), np.float32)
    for core in range(8):
        b, j = core // 4, core % 4
        out[b, j * cfg.QS:(j + 1) * cfg.QS] = np.asarray(res.results[core]["y"], np.float32)
    return out
```

```python
import contextlib
import numpy as np
import concourse.bass as bass
import concourse.mybir as mybir
from concourse.bass_utils import run_bass_kernel_spmd

F32 = mybir.dt.float32
BF16 = mybir.dt.bfloat16
AF = mybir.ActivationFunctionType
ALU = mybir.AluOpType
AX = mybir.AxisListType

HD = 64
BIG = 30000.0
NEGINF = -1e30
EPS = 1e-6


class Cfg:
    def __init__(s, D=1024, NH=8, QS=2048, F=2816, B=2):
        s.D, s.NH, s.QS, s.F, s.B = D, NH, QS, F, B
        s.KC = D // 128
        s.NG = NH // 2
        s.S = 4 * QS
        s.NT = s.S // 128
        s.NBLK = s.S // 256
        s.Q0 = s.S - QS - 128
        s.NQC = (QS + 128) // 128
        s.NQ = s.NQC * 128
        s.NF = F // 128
        s.HW = NH * HD
        s.stop = 0
        s.tiles = []
        c = 0
        while c < s.NQC:
            n = min(4, s.NQC - c)
            s.tiles.append((c, n))
            c += n


class _Rec:
    def __init__(s):
        s.call = None

    def __getattr__(s, name):
        def f(*a, **k):
            s.call = (name, a, k)
            return s
        return f


class TR:
    ENG = ('sp', 'act', 'dve', 'pool', 'pe')

    def __init__(s, nc, stack):
        s.nc = nc
        s.sem, s.cnt, s.lastw, s.readers = {}, {}, {}, {}
        s.waited = {e: {} for e in s.ENG}
        s.q = {e: [] for e in s.ENG}
        s.children = {}
        s.stack = stack
        s.n = 0

    def _rel(s, k):
        out = [k]
        if '.' in k:
            p = k.split('.')[0]
            out.append(p)
            s.children.setdefault(p, set()).add(k)
        else:
            out.extend(s.children.get(k, ()))
        return out

    def _sem(s, key):
        if key not in s.sem:
            s.sem[key] = s.stack.enter_context(s.nc.semaphore(key))
            s.cnt[key] = 0
        return s.sem[key]

    def op(s, e, fn, R=(), W=(), dma=None):
        deps = {}

        def add(ev):
            if ev is not None and ev[1] > deps.get(ev[0], 0):
                deps[ev[0]] = ev[1]
        for r in R:
            for rr in s._rel(r):
                add(s.lastw.get(rr))
        for w in W:
            for ww in s._rel(w):
                add(s.lastw.get(ww))
                for k, v in s.readers.get(ww, {}).items():
                    add((k, v))
        key, inc = ('E_' + e, 1) if dma is None else ('D_' + dma, 16)
        s._sem(key)
        waits = []
        for k, v in deps.items():
            if e == 'pe' and k == 'E_pe':
                continue
            if s.waited[e].get(k, 0) >= v:
                continue
            waits.append((s.sem[k], v))
            s.waited[e][k] = v
        rec = _Rec()
        fn(rec)
        s.cnt[key] += inc
        s.q[e].append((waits, rec.call, s.sem[key], inc))
        s.n += 1
        ev = (key, s.cnt[key])
        for r in R:
            d = s.readers.setdefault(r, {})
            d[key] = max(d.get(key, 0), ev[1])
        for w in W:
            s.lastw[w] = ev
            s.readers[w] = {}
        return ev

    def barrier(s, engines=None):
        for e in (engines or s.ENG):
            waits = []
            for k, v in s.cnt.items():
                if v > 0 and s.waited[e].get(k, 0) < v:
                    waits.append((s.sem[k], v))
                    s.waited[e][k] = v
            if waits:
                s.q[e].append((waits, None, None, 0))

    def flush(s):
        with s.nc.Block() as block:
            sect = {'sp': block.sync, 'act': block.scalar, 'dve': block.vector, 'pool': block.gpsimd,
                    'pe': block.tensor}
            for e in s.ENG:
                q = s.q[e]
                s.q[e] = []
                if not q:
                    continue

                def body(eng, q=q):
                    for waits, call, sem, inc in q:
                        for (wsem, v) in waits:
                            eng.wait_ge(wsem, v)
                        if call is not None:
                            name, a, k = call
                            getattr(eng, name)(*a, **k).then_inc(sem, inc)
                sect[e](body)


class _Stop(Exception):
    pass


def build(cfg):
    D, KC, NG, S, NT, NBLK, Q0, NQC, NQ, NF, QS = (cfg.D, cfg.KC, cfg.NG, cfg.S, cfg.NT, cfg.NBLK, cfg.Q0,
                                                  cfg.NQC, cfg.NQ, cfg.NF, cfg.QS)
    HW = cfg.HW
    NJ = HW // 128
    scale = HD ** -0.5
    NBW = min(512, D)
    NBN = D // NBW
    nc = bass.Bass("TRN2", target_bir_lowering=False)

    def din(name, shape):
        return nc.dram_tensor(name, list(shape), F32, kind="ExternalInput").ap()
    xk = din("xk", [S, D])
    wg = din("wg", [NG, 128, KC * 1024])
    cosT = din("cosT", [128, S])
    sinT = din("sinT", [128, S])
    kbias_d = din("kbias", [128, NT])
    bflag_d = din("bflag", [128, 32])
    halof_d = din("halof", [128, 1])
    ident_d = din("ident", [128, 128])
    tneg_d = din("tneg", [128, 128])
    eall_d = din("eall", [32, 32 * 128])
    masks_d = din("masks", [128, 8 * 512])
    gmix_d = din("gmix", [128, KC])
    gffn_d = din("gffn", [128, KC])
    gfin_d = din("gfin", [128, D])
    wgate_d = din("wgate", [128, 2 * KC, KC * 128])
    bgate_d = din("bgate", [128, 2 * KC])
    wa_d = din("wa", [128, KC, NJ * 128])
    wb_d = din("wb", [128, KC, NJ * 128])
    wout_d = din("wout", [128, KC * D])
    wup_d = din("wup", [128, NF, KC * 256])
    cw_d = din("cw", [128, NF * 6])
    cb_d = din("cb", [128, NF * 2])
    wdn_d = din("wdn", [128, NF * D])
    y = nc.dram_tensor("y", [QS, D], F32, kind="ExternalOutput").ap()
    hsc = nc.dram_tensor("hsc", [S // 512, 128, KC * 512], BF16).ap()
    WC_COLS = 2 * KC * KC * 128 + 2 * KC * NJ * 128 + KC * D + NF * (KC * 256 + D)
    wcache = nc.dram_tensor("wcache", [128, WC_COLS], BF16).ap()

    with contextlib.ExitStack() as stack:
        tr = TR(nc, stack)
        op = tr.op

        def sb(name, shape, dt=F32):
            return stack.enter_context(nc.sbuf_tensor(name, list(shape), dt))

        def ps(name, shape, dt=F32):
            return stack.enter_context(nc.psum_tensor(name, list(shape), dt))

        ident = sb("s_ident", [128, 128], BF16)
        tneg = sb("s_tneg", [128, 128], BF16)
        onesneg = sb("s_onesneg", [128, 128], BF16)
        eall = sb("s_eall", [32, 32 * 128], BF16)
        masks = sb("s_masks", [128, 8 * 512], BF16)
        kbias = sb("s_kbias", [128, NT])
        bflag = sb("s_bflag", [128, 32])
        halof = sb("s_halof", [128, 1])
        gmix = sb("s_gmix", [128, KC])
        gffn = sb("s_gffn", [128, KC])
        bgate = sb("s_bgate", [128, 2 * KC])
        cw = sb("s_cw", [128, NF * 6])
        cb = sb("s_cb", [128, NF * 2])
        stg = [sb("s_stg0", [128, 2048])]
        stg_i = [0]

        def load_cast(dst, dst_key, src_ap, n, cast_eng='dve'):
            i = stg_i[0] % len(stg)
            stg_i[0] += 1
            st = stg[i]
            op('sp', lambda e: e.dma_start(out=st[:, 0:n], in_=src_ap), W=['stg%d' % i], dma='stg%d' % i)
            if cast_eng == 'act':
                op('act', lambda e: e.activation(out=dst, in_=st[:, 0:n], func=AF.Copy), R=['stg%d' % i], W=[dst_key])
            else:
                op(cast_eng, lambda e: e.tensor_copy(out=dst, in_=st[:, 0:n]), R=['stg%d' % i], W=[dst_key])

        def load_f32(dst, key, src_ap):
            op('sp', lambda e: e.dma_start(out=dst, in_=src_ap), W=[key], dma=key)

        load_cast(ident[:, :], 'ident', ident_d, 128)
        load_cast(tneg[:, :], 'tneg', tneg_d, 128)
        for i in range(4):
            load_cast(masks[:, i * 1024:(i + 1) * 1024], 'masks', masks_d[:, i * 1024:(i + 1) * 1024], 1024)
        op('dve', lambda e: e.memset(onesneg[:, :], -1.0), W=['onesneg'])
        for hh in range(2):
            i = stg_i[0] % len(stg)
            stg_i[0] += 1
            op('sp', lambda e, i=i, hh=hh: e.dma_start(out=stg[i][0:32, 0:2048], in_=eall_d[:, hh * 2048:(hh + 1) * 2048]),
               W=['stg%d' % i], dma='stg%d' % i)
            op('dve', lambda e, i=i, hh=hh: e.tensor_copy(out=eall[:, hh * 2048:(hh + 1) * 2048], in_=stg[i][0:32, 0:2048]),
               R=['stg%d' % i], W=['eall'])
        load_f32(kbias[:, :], 'kbias', kbias_d)
        load_f32(bflag[:, :], 'bflag', bflag_d)
        load_f32(halof[:, :], 'halof', halof_d)
        load_f32(gmix[:, :], 'gmix', gmix_d)
        load_f32(gffn[:, :], 'gffn', gffn_d)
        load_f32(bgate[:, :], 'bgate', bgate_d)
        load_f32(cw[:, :], 'cw', cw_d)
        load_f32(cb[:, :], 'cb', cb_d)

        wc_off = {}
        wc_next = [0]
        pieces = []

        def add_piece(name, src_ap, n):
            wc_off[name] = wc_next[0]
            wc_next[0] += n
            pieces.append((name, src_ap, n))
        wst = min(2048, KC * D)
        for q4 in range(0, KC * D, wst):
            add_piece('wout%d' % q4, wout_d[:, q4:q4 + wst], wst)
        for fc in range(2 * KC):
            add_piece('wg%d' % fc, wgate_d[:, fc, :], KC * 128)
        for fc in range(KC):
            add_piece('wa%d' % fc, wa_d[:, fc, :], NJ * 128)
            add_piece('wb%d' % fc, wb_d[:, fc, :], NJ * 128)
        for f in range(NF):
            add_piece('wup%d' % f, wup_d[:, f, :], KC * 256)
            add_piece('wdn%d' % f, wdn_d[:, f * D:(f + 1) * D], D)
        assert wc_next[0] == WC_COLS, (wc_next[0], WC_COLS)

        yaT = sb("s_yaT", [128, NJ, NQ], BF16)
        ybT = sb("s_ybT", [128, NJ, NQ], BF16)

        xc = [sb("s_xc0", [128, D]), sb("s_xc1", [128, D])]
        junk = sb("s_junk", [128, D], BF16)
        hn2 = [sb("s_hn", [128, D], BF16), sb("s_hn1", [128, D], BF16)]
        st42 = [sb("s_st4", [128, 4]), sb("s_st41", [128, 4])]
        st4 = st42[0]
        nrm_i = [0]
        tp_ps = ps("tp_ps", [128, KC * 128], BF16)
        xc_i = [0]

        def norm_A(src_key, src):
            ni = nrm_i[0] % 2
            nrm_i[0] += 1
            st4, hn = st42[ni], hn2[ni]
            sk, hk, jk = 'st4_%d' % ni, 'hn_%d' % ni, 'junk'
            op('dve', lambda e: e.scalar_tensor_tensor(out=junk[:, :], in0=src, scalar=1.0, in1=src,
                                                       op0=ALU.mult, op1=ALU.mult, accum_out=st4[:, 0:1]),
               R=[src_key], W=[jk, sk])
            op('dve', lambda e: e.tensor_scalar(out=st4[:, 1:2], in0=st4[:, 0:1], scalar1=1.0 / D, scalar2=EPS,
                                                op0=ALU.mult, op1=ALU.add), R=[sk], W=[sk])
            op('act', lambda e: e.activation(out=st4[:, 2:3], in_=st4[:, 1:2], func=AF.Ln), R=[sk], W=[sk])
            op('act', lambda e: e.activation(out=st4[:, 3:4], in_=st4[:, 2:3], func=AF.Exp, scale=-0.5),
               R=[sk], W=[sk])
            op('act', lambda e: e.activation(out=hn[:, :], in_=src, func=AF.Identity, scale=st4[:, 3:4]),
               R=[src_key, sk], W=[hk])
            return (hn, hk)

        def norm_B(state, gt, gkey, dst_fn, dst_key):
            hn, hk = state
            for kc in range(KC):
                op('pe', lambda e, kc=kc: e.transpose(out=tp_ps[:, kc * 128:(kc + 1) * 128],
                                                     in_=hn[:, kc * 128:(kc + 1) * 128], identity=ident[:, :]),
                   R=[hk, 'ident'], W=['tp_ps'])
            for kc in range(KC):
                op('dve', lambda e, kc=kc: e.tensor_scalar(out=dst_fn(kc), in0=tp_ps[:, kc * 128:(kc + 1) * 128],
                                                           scalar1=gt[:, kc:kc + 1], scalar2=None, op0=ALU.mult),
                   R=['tp_ps', gkey], W=[dst_key])

        def norm_T(src_key, src, gt, gkey, dst_fn, dst_key):
            norm_B(norm_A(src_key, src), gt, gkey, dst_fn, dst_key)

        def load_x_chunk(t):
            i = xc_i[0] % 2
            xc_i[0] += 1
            op('sp', lambda e: e.dma_start(out=xc[i][:, :], in_=xk[t * 128:(t + 1) * 128, :]),
               W=['xc%d' % i], dma='xc%d' % i)
            return i

        def chk(k):
            if cfg.stop == k:
                raise _Stop()

        with contextlib.ExitStack() as st1:
          try:
              def sb1(name, shape, dt=F32):
                  return st1.enter_context(nc.sbuf_tensor(name, list(shape), dt))

              def ps1(name, shape, dt=F32):
                  return st1.enter_context(nc.psum_tensor(name, list(shape), dt))
              wgb = sb1("s_wgb", [128, KC * 1024], BF16)
              hT = sb1("s_hT", [128, KC, 512], BF16)
              cs = sb1("s_cs", [128, 512])
              sn = sb1("s_sn", [128, 512])
              kaT = sb1("s_kaT", [128, S], BF16)
              kbT = sb1("s_kbT", [128, S], BF16)
              vA = sb1("s_vA", [128, NT, 2 * 66], BF16)
              vB = sb1("s_vB", [128, NT, 128], BF16)
              qaT = sb1("s_qaT", [128, NQ], BF16)
              qbT = sb1("s_qbT", [128, NQ], BF16)
              km = sb1("s_km", [128, 32])
              kmb = sb1("s_kmb", [128, 32], BF16)
              gm = sb1("s_gm", [128, 32])
              m8 = sb1("s_m8", [128, 8])
              sel = sb1("s_sel", [128, 32])
              sel2 = sb1("s_sel2", [128, 32])
              biasq = sb1("s_biasq", [128, 32], BF16)
              gset = [(gm, m8, sel, sel2, biasq),
                      (sb1("s_gm2", [128, 32]), sb1("s_m82", [128, 8]), sb1("s_sel3", [128, 32]), sb1("s_sel4", [128, 32]),
                       sb1("s_biasq2", [128, 32], BF16))]
              biasT = [sb1("s_biasT0", [32, NQ], BF16), sb1("s_biasT1", [32, NQ], BF16)]
              hT2 = sb1("s_hT2", [128, KC, 512], BF16)
              xtra = [sb1("s_xb%d" % q, [128, 512], BF16) for q in range(max(0, 11 - 2 * KC))]
              ebuf = [sb1("s_e0", [128, 512]), sb1("s_e1", [128, 512])]
              rt1, rt2 = ebuf[0], ebuf[1]
              lsum = sb1("s_lsum", [128, 512], BF16)
              lsum2 = sb1("s_lsum2", [128, 512], BF16)
              ebuf2 = sb1("s_e2", [128, 512])
              ytok = sb1("s_ytok", [128, 4, 128], BF16)
              rden = sb1("s_rden", [128, 1])
              pj = [ps1("pj0", [128, 512]), ps1("pj1", [128, 512])]
              sc_ps = [ps1("sc0", [128, 512]), ps1("sc1", [128, 512])]
              af_ps = ps1("af", [128, 512])
              acc_ps = ps1("acc", [128, 512])
              sm_ps = ps1("sm", [128, 512], BF16)

              chk(100)
              for hh in range(2):
                  op('dve', lambda e, hh=hh: e.memset(vA[:, :, hh * 66 + 64: hh * 66 + 65], 1.0), W=['vA'])

              pj_i = [0]
              wtmp = sb1("s_wtmp", [128, 1024], BF16)

              def precast_gen():
                  for (name, src_ap, n) in pieces:
                      o = wc_off[name]
                      for sub in range(0, n, 1024):
                          m = min(1024, n - sub)
                          op('sp', lambda e: e.dma_start(out=stg[0][:, 0:m], in_=src_ap[:, sub:sub + m]), W=['stg0'],
                             dma='stg0')
                          op('dve', lambda e: e.tensor_copy(out=wtmp[:, 0:m], in_=stg[0][:, 0:m]), R=['stg0'], W=['wtmp'])
                          op('sp', lambda e: e.dma_start(out=wcache[:, o + sub:o + sub + m], in_=wtmp[:, 0:m]),
                             R=['wtmp'], W=['wc_' + name], dma='wcw_pre')
                          yield
              pc_gen = precast_gen()
              pc_step = [0]

              def precast_tick(force=False):
                  pc_step[0] += 1
                  if force or pc_step[0] % 12 == 0:
                      next(pc_gen, None)
              NHB = 2
              cur = [0]
              hTs = [hT, hT2]

              def hTv(kc, bi=None):
                  bi = cur[0] if bi is None else bi
                  return hTs[bi][:, kc, :]

              def hTk(bi=None):
                  return 'hTb%d' % (cur[0] if bi is None else bi)

              def proj(col0, ncols_tok, tok_off):
                  i = pj_i[0] % 2
                  pj_i[0] += 1
                  for kc in range(KC):
                      op('pe', lambda e, kc=kc: e.matmul(pj[i][:, 0:ncols_tok],
                                                         lhsT=wgb[:, kc * 1024 + col0: kc * 1024 + col0 + 128],
                                                         rhs=hTv(kc)[:, tok_off:tok_off + ncols_tok],
                                                         start=(kc == 0), stop=(kc == KC - 1)),
                         R=['wgb', hTk()], W=['pj%d' % i])
                  return i

              def rope_to(dst, dst_key, c_main, c_perm, n, tok_off):
                  i1 = proj(c_main, n, tok_off)
                  op('dve', lambda e: e.tensor_tensor(out=rt1[:, 0:n], in0=pj[i1][:, 0:n],
                                                      in1=cs[:, tok_off:tok_off + n], op=ALU.mult),
                     R=['pj%d' % i1, 'cs'], W=['e0'])
                  i2 = proj(c_perm, n, tok_off)
                  op('dve', lambda e: e.tensor_tensor(out=rt2[:, 0:n], in0=pj[i2][:, 0:n],
                                                      in1=sn[:, tok_off:tok_off + n], op=ALU.mult),
                     R=['pj%d' % i2, 'sn'], W=['e1'])
                  op('pool', lambda e: e.tensor_tensor(out=dst, in0=rt1[:, 0:n], in1=rt2[:, 0:n], op=ALU.add),
                     R=['e0', 'e1'], W=[dst_key])

              chk(1)
              for g in range(NG):
                  for kc in range(0, KC, 2):
                      n = min(2, KC - kc) * 1024
                      load_cast(wgb[:, kc * 1024: kc * 1024 + n], 'wgb', wg[g, :, kc * 1024: kc * 1024 + n], n,
                                cast_eng='dve')
                  chk(11)
                  for tt in range(S // 512):
                      cur[0] = tt % NHB
                      if g == 0:
                          pend = None
                          for c in range(4):
                              t = tt * 4 + c
                              i = load_x_chunk(t)
                              stt = norm_A('xc%d' % i, xc[i][:, :])
                              if pend is not None:
                                  norm_B(pend[0], gmix, 'gmix', lambda kc, c=pend[1]: hTv(kc)[:, c * 128:(c + 1) * 128], hTk())
                              pend = (stt, c)
                              chk(12)
                          norm_B(pend[0], gmix, 'gmix', lambda kc, c=pend[1]: hTv(kc)[:, c * 128:(c + 1) * 128], hTk())
                          op('sp', lambda e: e.dma_start(out=hsc[tt], in_=hTs[cur[0]][:, :, :].rearrange("p k t -> p (k t)")),
                             R=[hTk()], W=['hsc%d' % tt], dma='hscw%d' % cur[0])
                      else:
                          op('sp', lambda e: e.dma_start(out=hTs[cur[0]][:, :, :].rearrange("p k t -> p (k t)"), in_=hsc[tt]),
                             R=['hsc%d' % tt], W=[hTk()], dma='hTl%d' % cur[0])
                      load_f32(cs[:, :], 'cs', cosT[:, tt * 512:(tt + 1) * 512])
                      load_f32(sn[:, :], 'sn', sinT[:, tt * 512:(tt + 1) * 512])
                      tk = 'kv%d' % tt
                      rope_to(kaT[:, tt * 512:(tt + 1) * 512], 'kaT' + tk, 256, 384, 512, 0)
                      chk(13)
                      i = proj(640, 512, 0)
                      op('act', lambda e, i=i: e.activation(out=kbT[:, tt * 512:(tt + 1) * 512], in_=pj[i][:, :],
                                                            func=AF.Copy), R=['pj%d' % i], W=['kbT' + tk])
                      chk(14)
                      for c in range(4):
                          t = tt * 4 + c
                          i = pj_i[0] % 2
                          pj_i[0] += 1
                          for kc in range(KC):
                              op('pe', lambda e, kc=kc, c=c, i=i: e.matmul(
                                  pj[i][:, 0:256], lhsT=hTv(kc)[:, c * 128:(c + 1) * 128],
                                  rhs=wgb[:, kc * 1024 + 768: kc * 1024 + 1024],
                                  start=(kc == 0), stop=(kc == KC - 1)), R=['wgb', hTk()], W=['pj%d' % i])
                          chk(151)
                          for hh in range(2):
                              op('act', lambda e, t=t, i=i, hh=hh: e.activation(
                                  out=vA[:, t, hh * 66: hh * 66 + 64], in_=pj[i][:, hh * 64:(hh + 1) * 64], func=AF.Copy),
                                 R=['pj%d' % i], W=['vA'])
                          chk(152)
                          op('act', lambda e, t=t, i=i: e.activation(out=vB[:, t, :], in_=pj[i][:, 128:256], func=AF.Copy),
                             R=['pj%d' % i], W=['vB'])
                      chk(15)
                      lo = max(tt * 512, Q0)
                      if lo < (tt + 1) * 512:
                          off = lo - tt * 512
                          n = 512 - off
                          rope_to(qaT[:, lo - Q0: lo - Q0 + n], 'qaT', 0, 128, n, off)
                          i = proj(512, n, off)
                          op('act', lambda e, i=i, lo=lo, n=n: e.activation(out=qbT[:, lo - Q0: lo - Q0 + n],
                                                                            in_=pj[i][:, 0:n], func=AF.Copy),
                             R=['pj%d' % i], W=['qbT'])
                  chk(2)
                  kv_keys = ['kv%d' % tt for tt in range(S // 512)]
                  for nb0 in range(0, NBLK, 8):
                      op('dve', lambda e, nb0=nb0: e.tensor_reduce(
                          out=km[:, nb0:nb0 + 8],
                          in_=kaT[:, nb0 * 256:(nb0 + 8) * 256].rearrange("p (n k) -> p n k", k=256),
                          op=ALU.add, axis=AX.X), R=['kaT' + k for k in kv_keys], W=['km'])
                  op('dve', lambda e: e.tensor_scalar(out=kmb[:, 0:NBLK], in0=km[:, 0:NBLK], scalar1=1.0 / 256,
                                                      scalar2=None, op0=ALU.mult), R=['km'], W=['kmb'])
                  git = 0
                  for h in range(2):
                      hp = slice(64 * h, 64 * h + 64)
                      for qc in range(NQC):
                          Bq = (Q0 + qc * 128) // 256
                          z = git % 2
                          git += 1
                          gm_, m8_, sel_, sel2_, bq_ = gset[z]
                          scz, sck = sc_ps[z], 'sc%d' % z
                          smz, smk = sm_ps[0:32, 0:128], 'sm'
                          kk = ['gm%d' % z, 'm8%d' % z, 'sel%d' % z, 'selb%d' % z, 'bq%d' % z]
                          op('pe', lambda e: e.matmul(scz[:, 0:NBLK], lhsT=qaT[hp, qc * 128:(qc + 1) * 128],
                                                      rhs=kmb[hp, 0:NBLK], start=True, stop=True),
                             R=['qaT', 'kmb'], W=[sck])
                          op('dve', lambda e: e.memset(gm_[:, :], NEGINF), W=[kk[0]])
                          op('dve', lambda e: e.tensor_tensor(out=gm_[:, 0:Bq], in0=scz[:, 0:Bq], in1=bflag[:, 0:Bq],
                                                              op=ALU.add), R=[sck, 'bflag'], W=[kk[0]])
                          op('dve', lambda e: e.max(out=m8_[:, :], in_=gm_[:, 0:NBLK]), R=[kk[0]], W=[kk[1]])
                          op('dve', lambda e: e.tensor_scalar(out=sel_[:, :], in0=gm_[:, :], scalar1=m8_[:, 2:3],
                                                              scalar2=None, op0=ALU.is_ge), R=[kk[0], kk[1]], W=[kk[2]])
                          op('dve', lambda e: e.tensor_scalar(out=sel2_[:, :], in0=gm_[:, :], scalar1=-1e29,
                                                              scalar2=None, op0=ALU.is_gt), R=[kk[0]], W=[kk[3]])
                          op('dve', lambda e: e.tensor_tensor(out=sel_[:, :], in0=sel_[:, :], in1=sel2_[:, :],
                                                              op=ALU.mult), R=[kk[2], kk[3]], W=[kk[2]])
                          op('dve', lambda e: e.tensor_scalar(out=bq_[:, :], in0=sel_[:, :], scalar1=BIG,
                                                              scalar2=-BIG, op0=ALU.mult, op1=ALU.add),
                             R=[kk[2]], W=[kk[4]])
                          op('dve', lambda e: e.memset(bq_[:, Bq:Bq + 1], 0.0), W=[kk[4]])
                          op('pe', lambda e: e.transpose(out=smz, in_=bq_[:, :], identity=ident[:, :]),
                             R=[kk[4], 'ident'], W=[smk])
                          op('act', lambda e: e.activation(out=biasT[h][:, qc * 128:(qc + 1) * 128], in_=smz, func=AF.Copy),
                             R=[smk], W=['biasT%d' % h])
                  chk(3)
                  tr.barrier()
                  ebufs = [(ebuf[0], 'e0'), (ebuf[1], 'e1'), (cs, 'cs'), (sn, 'sn'), (ebuf2, 'e2')]
                  hsl = [(hT[:, kc, :], 'hTs%d' % kc) for kc in range(KC)]
                  bfb = [(hT2[:, kc, :], 'hT2s%d' % kc) for kc in range(KC)] + \
                        [(xtra[q], 'xb%d' % q) for q in range(len(xtra))] + hsl
                  assert len(bfb) >= 10
                  sp_b, E_b, a_b, p_b = bfb[0:3], bfb[3:5], bfb[5:8], bfb[8:8 + 3] if len(bfb) >= 11 else bfb[8:10]
                  lsums = [(lsum, 'lsum'), (lsum2, 'lsum2')]
                  fbanks = [(sc_ps[0], 'sc0'), (sc_ps[1], 'sc1'), (pj[0], 'pj0'), (pj[1], 'pj1'), (af_ps, 'af'), (acc_ps, 'acc')]

                  def run_pipeline(nblocks, stages):
                      ns = len(stages)
                      for t in range(nblocks + ns - 1):
                          precast_tick()
                          for s in range(ns - 1, -1, -1):
                              i = t - s
                              if 0 <= i < nblocks:
                                  stages[s](i)

                  for (c0, ncq) in cfg.tiles:
                      W = ncq * 128
                      q0 = Q0 + c0 * 128
                      qs = slice(c0 * 128, c0 * 128 + W)
                      kb_max = (q0 + W) // 128 - 1
                      kb_diag = q0 // 128
                      mb = [(kb, h) for kb in range(kb_max + 1) for h in range(2)]
                      m_sc = fbanks[0:4]
                      m_acc = [fbanks[5], fbanks[4]]

                      def m_s0(i):
                          kb, h = mb[i]
                          hp = slice(64 * h, 64 * h + 64)
                          sc, sck = m_sc[i % 4]
                          op('pe', lambda e: e.matmul(sc[:, 0:W], lhsT=kaT[hp, kb * 128:(kb + 1) * 128], rhs=qaT[hp, qs],
                                                      start=True, stop=False), R=['kaTkv%d' % (kb // 4), 'qaT'], W=[sck])
                          n = kb // 2
                          op('pe', lambda e: e.matmul(sc[:, 0:W], lhsT=eall[:, n * 128:(n + 1) * 128], rhs=biasT[h][:, qs],
                                                      start=False, stop=True), R=['eall', 'biasT%d' % h], W=[sck])

                      def m_s1(i):
                          kb, h = mb[i]
                          sc, sck = m_sc[i % 4]
                          p, pk = p_b[i % len(p_b)]
                          op('act', lambda e: e.activation(out=p[:, 0:W], in_=sc[:, 0:W], func=AF.Exp, scale=scale),
                             R=[sck], W=[pk])
                          if kb >= kb_diag:
                              v = 4 + (kb - kb_diag)
                              op('dve', lambda e: e.tensor_tensor(out=p[:, 0:W], in0=p[:, 0:W],
                                                                  in1=masks[:, v * 512: v * 512 + W], op=ALU.mult),
                                 R=[pk, 'masks'], W=[pk])

                      def m_s2(i):
                          kb, h = mb[i]
                          p, pk = p_b[i % len(p_b)]
                          acc, acck = m_acc[h]
                          for c in range(ncq):
                              op('pe', lambda e, c=c: e.matmul(acc[:, c * 65:(c + 1) * 65], lhsT=p[:, c * 128:(c + 1) * 128],
                                                               rhs=vA[:, kb, h * 66: h * 66 + 65],
                                                               start=(kb == 0 and c == 0),
                                                               stop=(kb == kb_max and c == ncq - 1)),
                                 R=[pk, 'vA'], W=[acck])
                      run_pipeline(len(mb), [m_s0, m_s1, m_s2])
                      for h in range(2):
                          acc, acck = m_acc[h]
                          for c in range(ncq):
                              op('dve', lambda e, c=c: e.reciprocal(out=rden[:, :], in_=acc[:, c * 65 + 64: c * 65 + 65]),
                                 R=[acck], W=['rden'])
                              op('dve', lambda e, c=c, h=h: e.tensor_scalar(out=ytok[:, c, 64 * h:64 * h + 64],
                                                                            in0=acc[:, c * 65: c * 65 + 64],
                                                                            scalar1=rden[:, 0:1], scalar2=None,
                                                                            op0=ALU.mult),
                                 R=[acck, 'rden'], W=['ytok'])
                      for c in range(ncq):
                          op('pe', lambda e, c=c: e.transpose(out=sm_ps[:, 128:256], in_=ytok[:, c, :], identity=ident[:, :]),
                             R=['ytok', 'ident'], W=['sm'])
                          op('act', lambda e, c=c: e.activation(out=yaT[:, g, (c0 + c) * 128:(c0 + c + 1) * 128],
                                                                in_=sm_ps[:, 128:256], func=AF.Copy),
                             R=['sm'], W=['yaT'])
                      chk(4)
                      sbk = [(kb, h) for kb in range(kb_max, -1, -1) for h in range(2)]
                      s_sc = fbanks[0:2]
                      s_af = [fbanks[4], fbanks[2]]
                      s_acc = [fbanks[5], fbanks[3]]

                      def s_s0(i):
                          kb, h = sbk[i]
                          hp = slice(64 * h, 64 * h + 64)
                          sc, sck = s_sc[i % 2]
                          op('pe', lambda e: e.matmul(sc[:, 0:W], lhsT=kbT[hp, kb * 128:(kb + 1) * 128], rhs=qbT[hp, qs],
                                                      start=True, stop=True), R=['kbTkv%d' % (kb // 4), 'qbT'], W=[sck])

                      def s_s1(i):
                          kb, h = sbk[i]
                          sc, sck = s_sc[i % 2]
                          eb, ek = ebufs[i % 5]
                          op('act', lambda e: e.activation(out=eb[:, 0:W], in_=sc[:, 0:W], func=AF.Exp, scale=scale,
                                                           bias=kbias[:, kb:kb + 1]), R=[sck, 'kbias'], W=[ek])
                          if kb >= kb_diag:
                              v = kb - kb_diag
                              op('dve', lambda e: e.tensor_tensor(out=eb[:, 0:W], in0=eb[:, 0:W],
                                                                  in1=masks[:, v * 512: v * 512 + W], op=ALU.mult),
                                 R=[ek, 'masks'], W=[ek])

                      def s_s2(i):
                          eb, ek = ebufs[i % 5]
                          spb, spk = sp_b[i % 3]
                          op('act', lambda e: e.activation(out=spb[:, 0:W], in_=eb[:, 0:W], func=AF.Ln, bias=1.0),
                             R=[ek], W=[spk])

                      def s_s3(i):
                          kb, h = sbk[i]
                          first = (kb == kb_max)
                          spb, spk = sp_b[i % 3]
                          af, afk = s_af[i % 2]
                          ls, lsk = lsums[h]
                          op('pe', lambda e: e.matmul(af[:, 0:W], lhsT=tneg[:, :], rhs=spb[:, 0:W], start=True, stop=first),
                             R=['tneg', spk], W=[afk])
                          if not first:
                              op('pe', lambda e: e.matmul(af[:, 0:W], lhsT=onesneg[:, :], rhs=ls[:, 0:W], start=False,
                                                          stop=True), R=['onesneg', lsk], W=[afk])
                          if kb != 0:
                              if first:
                                  op('dve', lambda e: e.tensor_copy(out=ls[:, 0:W], in_=spb[:, 0:W]), R=[spk], W=[lsk])
                              else:
                                  op('pool', lambda e: e.tensor_tensor(out=ls[:, 0:W], in0=ls[:, 0:W], in1=spb[:, 0:W],
                                                                       op=ALU.add), R=[spk, lsk], W=[lsk])

                      def s_s4(i):
                          af, afk = s_af[i % 2]
                          Eb, Ek = E_b[i % 2]
                          op('act', lambda e: e.activation(out=Eb[:, 0:W], in_=af[:, 0:W], func=AF.Exp), R=[afk], W=[Ek])

                      def s_s5(i):
                          eb, ek = ebufs[i % 5]
                          Eb, Ek = E_b[i % 2]
                          ab, ak = a_b[i % 3]
                          op('dve', lambda e: e.tensor_tensor(out=ab[:, 0:W], in0=eb[:, 0:W], in1=Eb[:, 0:W], op=ALU.mult),
                             R=[ek, Ek], W=[ak])

                      def s_s6(i):
                          kb, h = sbk[i]
                          hp = slice(64 * h, 64 * h + 64)
                          ab, ak = a_b[i % 3]
                          acc, acck = s_acc[h]
                          op('pe', lambda e: e.matmul(acc[hp, 0:W], lhsT=vB[:, kb, 64 * h:64 * h + 64], rhs=ab[:, 0:W],
                                                      start=(kb == kb_max), stop=(kb == 0)), R=['vB', ak], W=[acck])
                      run_pipeline(len(sbk), [s_s0, s_s1, s_s2, s_s3, s_s4, s_s5, s_s6])
                      for h in range(2):
                          hp = slice(64 * h, 64 * h + 64)
                          acc, acck = s_acc[h]
                          op('act', lambda e: e.activation(out=ybT[hp, g, qs], in_=acc[hp, 0:W], func=AF.Copy),
                             R=[acck], W=['ybT'])
                  tr.barrier()
              for _ in pc_gen:
                  pass

          except _Stop:
            pass
          if 0 < cfg.stop < 200:
            tr.barrier(engines=['sp'])
          tr.flush()
        if 0 < cfg.stop < 200:
            return nc
        tr.barrier()
        with contextlib.ExitStack() as st2:
          try:
              def sb2(name, shape, dt=F32):
                  return st2.enter_context(nc.sbuf_tensor(name, list(shape), dt))

              def ps2(name, shape, dt=F32):
                  return st2.enter_context(nc.psum_tensor(name, list(shape), dt))
              stg.append(sb2("s_stg1", [128, 2048]))
              x1 = sb2("s_x1", [128, 4, D])
              hTo = sb2("s_hTo", [128, KC, 512], BF16)
              gT = sb2("s_gT", [128, 2 * KC, 512], BF16)
              mT = sb2("s_mT", [128, KC, 512], BF16)
              mt1 = sb2("s_mt1", [128, 512])
              mt2 = sb2("s_mt2", [128, 512])
              wsm = [sb2("s_wsm%d" % q, [128, 1024], BF16) for q in range(4)]
              woutb = sb2("s_woutb", [128, KC * D], BF16)
              h2T = sb2("s_h2T", [128, KC, 512], BF16)
              ur = [sb2("s_ur0", [128, 2 + 512]), sb2("s_ur1", [128, 2 + 512])]
              carry = sb2("s_carry", [128, NF * 2, 2])
              cv = [sb2("s_cv0", [128, 512]), sb2("s_cv1", [128, 512])]
              sg = sb2("s_sg", [128, 512])
              urs = [[(ur[0], 'ur0'), (ur[1], 'ur1')],
                     [(stg[1][:, 0:514], 'stg1.a'), (stg[1][:, 514:1028], 'stg1.b')]]
              cvs = [[(cv[0], 'cv0'), (cv[1], 'cv1')],
                     [(stg[0][:, 0:512], 'stg0.a'), (stg[0][:, 512:1024], 'stg0.b')]]
              sgs = [(sg, 'sg'), (stg[0][:, 1024:1536], 'stg0.c')]
              prod = sb2("s_prod", [128, 2, 512], BF16)
              wup = [sb2("s_wup%d" % q, [128, KC * 256], BF16) for q in range(4)]
              wdn = [sb2("s_wdn%d" % q, [128, D], BF16) for q in range(4)]
              gfin = sb2("s_gfin", [128, D])
              st4f = sb2("s_st4f", [128, 4])
              ob = [xc[0], xc[1]]
              g_ps = [ps2("g0", [128, 512]), ps2("g1", [128, 512])]
              b_ps = [ps2("b0", [128, 512]), ps2("b1", [128, 512])]
              o_ps = ps2("o_ps", [128, 1024])
              wsm_i = [0]
              load_f32(gfin[:, :], 'gfin', gfin_d)
              chk(200)

              def cached_load(dst, dst_key, src_ap, n, name, first, eng='act'):
                  o = wc_off[name]
                  first = False
                  if first:
                      load_cast(dst, dst_key, src_ap, n, cast_eng=eng)
                      op('sp', lambda e: e.dma_start(out=wcache[:, o:o + n], in_=dst), R=[dst_key], W=['wc_' + name],
                         dma='wcw_' + dst_key)
                  else:
                      op('sp', lambda e: e.dma_start(out=dst, in_=wcache[:, o:o + n]), R=['wc_' + name], W=[dst_key],
                         dma='L_' + dst_key)

              def wload(src_ap, n, name, first, eng='act'):
                  i = wsm_i[0] % 4
                  wsm_i[0] += 1
                  cached_load(wsm[i][:, 0:n], 'wsm%d' % i, src_ap, n, name, first, eng=eng)
                  return i

              gi = [0]
              ob_i = [0]
              for ti, (c0, ncq) in enumerate(cfg.tiles):
                  W = ncq * 128
                  qs = slice(c0 * 128, c0 * 128 + W)
                  x1k = ['x1_%d' % c for c in range(ncq)]
                  for c in range(ncq):
                      t = Q0 // 128 + c0 + c
                      op('sp', lambda e, c=c, t=t: e.dma_start(out=x1[:, c, :], in_=xk[t * 128:(t + 1) * 128, :]),
                         W=[x1k[c]], dma=x1k[c])
                      norm_T(x1k[c], x1[:, c, :], gmix, 'gmix',
                             lambda kc, c=c: hTo[:, kc, c * 128:(c + 1) * 128], 'hTo')
                      chk(201)
                  wst = min(2048, KC * D)
                  for q4 in range(0, KC * D, wst):
                      cached_load(woutb[:, q4:q4 + wst], 'woutb', wout_d[:, q4:q4 + wst], wst, 'wout%d' % q4, ti == 0)
                  for fc in range(2 * KC):
                      wi = wload(wgate_d[:, fc, :], KC * 128, 'wg%d' % fc, ti == 0)
                      b = gi[0] % 2
                      gi[0] += 1
                      for kc in range(KC):
                          op('pe', lambda e, kc=kc, wi=wi, b=b: e.matmul(g_ps[b][:, 0:W], lhsT=wsm[wi][:, kc * 128:(kc + 1) * 128],
                                                                         rhs=hTo[:, kc, 0:W], start=(kc == 0),
                                                                         stop=(kc == KC - 1)),
                             R=['wsm%d' % wi, 'hTo'], W=['g%d' % b])
                      op('act', lambda e, fc=fc, b=b: e.activation(out=gT[:, fc, 0:W], in_=g_ps[b][:, 0:W],
                                                                   func=AF.Sigmoid, bias=bgate[:, fc:fc + 1]),
                         R=['g%d' % b, 'bgate'], W=['gT'])
                      chk(202)
                  for fc in range(KC):
                      wia = wload(wa_d[:, fc, :], NJ * 128, 'wa%d' % fc, ti == 0, eng='dve')
                      wib = wload(wb_d[:, fc, :], NJ * 128, 'wb%d' % fc, ti == 0, eng='dve')
                      b = gi[0] % 2
                      gi[0] += 1
                      for j in range(NJ):
                          op('pe', lambda e, j=j, wia=wia, b=b: e.matmul(g_ps[b][:, 0:W], lhsT=wsm[wia][:, j * 128:(j + 1) * 128],
                                                                         rhs=yaT[:, j, qs], start=(j == 0), stop=(j == NJ - 1)),
                             R=['wsm%d' % wia, 'yaT'], W=['g%d' % b])
                      for j in range(NJ):
                          op('pe', lambda e, j=j, wib=wib, b=b: e.matmul(b_ps[b][:, 0:W], lhsT=wsm[wib][:, j * 128:(j + 1) * 128],
                                                                         rhs=ybT[:, j, qs], start=(j == 0), stop=(j == NJ - 1)),
                             R=['wsm%d' % wib, 'ybT'], W=['b%d' % b])
                      op('dve', lambda e, fc=fc, b=b: e.tensor_tensor(out=mt1[:, 0:W], in0=g_ps[b][:, 0:W],
                                                                      in1=gT[:, fc, 0:W], op=ALU.mult),
                         R=['g%d' % b, 'gT'], W=['mt1'])
                      op('dve', lambda e, fc=fc, b=b: e.tensor_tensor(out=mt2[:, 0:W], in0=b_ps[b][:, 0:W],
                                                                      in1=gT[:, KC + fc, 0:W], op=ALU.mult),
                         R=['b%d' % b, 'gT'], W=['mt2'])
                      op('pool', lambda e, fc=fc: e.tensor_tensor(out=mT[:, fc, 0:W], in0=mt1[:, 0:W], in1=mt2[:, 0:W],
                                                                  op=ALU.add), R=['mt1', 'mt2'], W=['mT'])
                      chk(203)
                  for c in range(ncq):
                      for nb in range(NBN):
                          for kc in range(KC):
                              op('pe', lambda e, kc=kc, c=c, nb=nb: e.matmul(
                                  o_ps[:, nb * NBW:(nb + 1) * NBW], lhsT=mT[:, kc, c * 128:(c + 1) * 128],
                                  rhs=woutb[:, kc * D + nb * NBW: kc * D + nb * NBW + NBW], start=(kc == 0),
                                  stop=(kc == KC - 1)), R=['woutb', 'mT'], W=['o_ps'])
                      op('dve', lambda e, c=c: e.tensor_tensor(out=x1[:, c, :], in0=o_ps[:, 0:D], in1=x1[:, c, :],
                                                               op=ALU.add), R=['o_ps', x1k[c]], W=[x1k[c]])
                      norm_T(x1k[c], x1[:, c, :], gffn, 'gffn',
                             lambda kc, c=c: h2T[:, kc, c * 128:(c + 1) * 128], 'h2T')
                      chk(204)
                  gbanks = [(g_ps[0], 'g0'), (g_ps[1], 'g1'), (b_ps[0], 'b0'), (b_ps[1], 'b1')]
                  if False:
                      cv3 = [cvs[0], [(mt1, 'mt1'), (mt2, 'mt2')]]
                      ur_s, sg_s = [urs[0]], [sgs[0]]
                  else:
                      cv3 = cvs + [[(mt1, 'mt1'), (mt2, 'mt2')]]
                      ur_s, sg_s = urs, sgs
                  NCV, NUR = len(cv3), len(ur_s)
                  prod4 = [(masks[:, q * 512:(q + 1) * 512], 'masks.p%d' % q) for q in range(4)]
                  wdn6 = [(wdn[q][:, 0:D], 'wdn%d' % q) for q in range(4)]
                  if D <= 1024:
                      wdn6 += [(masks[:, 2048 + q * 1024: 2048 + q * 1024 + D], 'masks.w%d' % q) for q in range(2)]
                  NWD = len(wdn6)

                  def f_load(f):
                      if f >= NF:
                          return
                      cached_load(wup[f % 4][:, :], 'wup%d' % (f % 4), wup_d[:, f, :], KC * 256, 'wup%d' % f, ti == 0,
                                  eng=('act' if f % 2 == 0 else 'dve'))

                  def f_load_dn(f):
                      wd, wdk = wdn6[f % NWD]
                      cached_load(wd, wdk, wdn_d[:, f * D:(f + 1) * D], D, 'wdn%d' % f, ti == 0,
                                  eng=('dve' if f % 2 == 0 else 'act'))

                  def f_s0(f):
                      f_load(f + 2)
                      for gv in range(2):
                          gb, gbk = gbanks[(f % 2) * 2 + gv]
                          for kc in range(KC):
                              op('pe', lambda e, kc=kc: e.matmul(
                                  gb[:, 0:W], lhsT=wup[f % 4][:, kc * 256 + gv * 128: kc * 256 + gv * 128 + 128],
                                  rhs=h2T[:, kc, 0:W], start=(kc == 0), stop=(kc == KC - 1)),
                                 R=['wup%d' % (f % 4), 'h2T'], W=[gbk])

                  def f_s1(f):
                      for gv in range(2):
                          gb, gbk = gbanks[(f % 2) * 2 + gv]
                          u, uk = ur_s[f % NUR][gv]
                          cidx = f * 2 + gv
                          if ti == 0:
                              op('dve', lambda e: e.memset(u[:, 0:2], 0.0), W=[uk])
                          else:
                              op('dve', lambda e: e.tensor_copy(out=u[:, 0:2], in_=carry[:, cidx, :]),
                                 R=['carry.%d' % cidx], W=[uk])
                          op('act', lambda e: e.activation(out=u[:, 2:2 + W], in_=gb[:, 0:W], func=AF.Copy),
                             R=[gbk], W=[uk])
                          if ti == 0:
                              op('dve', lambda e: e.tensor_scalar(out=u[:, 2:130], in0=u[:, 2:130], scalar1=halof[:, 0:1],
                                                                  scalar2=None, op0=ALU.mult), R=[uk, 'halof'], W=[uk])

                  def f_s2(f):
                      f_load_dn(f)
                      for gv in range(2):
                          u, uk = ur_s[f % NUR][gv]
                          cvt, cvk = cv3[f % NCV][gv]
                          cidx = f * 2 + gv
                          ci = f * 6 + gv * 3
                          op('dve', lambda e: e.tensor_copy(out=carry[:, cidx, :], in_=u[:, W:W + 2]),
                             R=[uk], W=['carry.%d' % cidx])
                          op('dve', lambda e: e.tensor_scalar(
                              out=cvt[:, 0:W], in0=u[:, 2:2 + W], scalar1=cw[:, ci + 2:ci + 3],
                              scalar2=cb[:, f * 2 + gv:f * 2 + gv + 1], op0=ALU.mult, op1=ALU.add),
                             R=[uk, 'cw', 'cb'], W=[cvk])
                          op('dve', lambda e: e.scalar_tensor_tensor(
                              out=cvt[:, 0:W], in0=u[:, 1:1 + W], scalar=cw[:, ci + 1:ci + 2], in1=cvt[:, 0:W],
                              op0=ALU.mult, op1=ALU.add), R=[uk, 'cw', cvk], W=[cvk])
                          op('dve', lambda e: e.scalar_tensor_tensor(
                              out=cvt[:, 0:W], in0=u[:, 0:W], scalar=cw[:, ci:ci + 1], in1=cvt[:, 0:W],
                              op0=ALU.mult, op1=ALU.add), R=[uk, 'cw', cvk], W=[cvk])

                  def f_s3(f):
                      sgt, sgk = sg_s[f % NUR]
                      cg, cgk = cv3[f % NCV][0]
                      op('act', lambda e: e.activation(out=sgt[:, 0:W], in_=cg[:, 0:W], func=AF.Sigmoid), R=[cgk], W=[sgk])
                      op('pool', lambda e: e.tensor_tensor(out=sgt[:, 0:W], in0=sgt[:, 0:W], in1=cg[:, 0:W], op=ALU.mult),
                         R=[sgk, cgk], W=[sgk])

                  def f_s4(f):
                      sgt, sgk = sg_s[f % NUR]
                      cvv, cvvk = cv3[f % NCV][1]
                      pr, prk = prod4[f % 4]
                      op('dve', lambda e: e.tensor_tensor(out=pr[:, 0:W], in0=sgt[:, 0:W], in1=cvv[:, 0:W], op=ALU.mult),
                         R=[sgk, cvvk], W=[prk])

                  def f_s5(f):
                      if f % 2 == 0 and f != NF - 1:
                          return
                      fls = [f - 1, f] if f % 2 == 1 else [f]
                      for c in range(ncq):
                          if c0 + c == 0:
                              continue
                          for nb in range(NBN):
                              for q, ff in enumerate(fls):
                                  pr, prk = prod4[ff % 4]
                                  wd, wdk = wdn6[ff % NWD]
                                  op('pe', lambda e: e.matmul(o_ps[:, nb * NBW:(nb + 1) * NBW],
                                                              lhsT=pr[:, c * 128:(c + 1) * 128],
                                                              rhs=wd[:, nb * NBW:(nb + 1) * NBW], start=(q == 0),
                                                              stop=(q == len(fls) - 1)), R=[prk, wdk], W=['o_ps'])
                          op('dve', lambda e: e.tensor_tensor(out=x1[:, c, :], in0=o_ps[:, 0:D], in1=x1[:, c, :], op=ALU.add),
                             R=['o_ps', x1k[c]], W=[x1k[c]])

                  f_load(0)
                  f_load(1)
                  stages = [f_s0, f_s1, f_s2, f_s3, f_s4, f_s5]
                  for t in range(NF + len(stages) - 1):
                      for s in range(len(stages) - 1, -1, -1):
                          f = t - s
                          if 0 <= f < NF:
                              stages[s](f)
                  chk(206)
                  if ti == 0:
                      tr.barrier(engines=['sp'])
                  chk(206)
                  for c in range(ncq):
                      if c0 + c == 0:
                          continue
                      i = ob_i[0] % 2
                      ob_i[0] += 1
                      src = x1[:, c, :]
                      op('dve', lambda e, src=src: e.scalar_tensor_tensor(out=junk[:, :], in0=src, scalar=1.0, in1=src,
                                                                          op0=ALU.mult, op1=ALU.mult, accum_out=st4f[:, 0:1]),
                         R=[x1k[c]], W=['junk', 'st4f'])
                      op('dve', lambda e: e.tensor_scalar(out=st4f[:, 1:2], in0=st4f[:, 0:1], scalar1=1.0 / D, scalar2=EPS,
                                                          op0=ALU.mult, op1=ALU.add), R=['st4f'], W=['st4f'])
                      op('act', lambda e: e.activation(out=st4f[:, 2:3], in_=st4f[:, 1:2], func=AF.Ln), R=['st4f'], W=['st4f'])
                      op('act', lambda e: e.activation(out=st4f[:, 3:4], in_=st4f[:, 2:3], func=AF.Exp, scale=-0.5),
                         R=['st4f'], W=['st4f'])
                      op('dve', lambda e, src=src, i=i: e.scalar_tensor_tensor(out=ob[i][:, :], in0=src, scalar=st4f[:, 3:4],
                                                                               in1=gfin[:, :], op0=ALU.mult, op1=ALU.mult),
                         R=[x1k[c], 'st4f', 'gfin'], W=['xc%d' % i])
                      r0 = (c0 + c - 1) * 128
                      op('sp', lambda e, r0=r0, i=i: e.dma_start(out=y[r0:r0 + 128, :], in_=ob[i][:, :]),
                         R=['xc%d' % i], W=['y%d' % i], dma='y%d' % i)
                      chk(207)
          except _Stop:
            pass
          tr.barrier(engines=['sp'])
          tr.flush()
    return nc


def _fm(v, nchunk):
    return np.ascontiguousarray(np.asarray(v, np.float32).reshape(nchunk, 128).T)


def make_core_inputs(cfg, inputs, theta=10000.0):
    D, KC, NG, S, NT, NBLK, QS, NF, F = cfg.D, cfg.KC, cfg.NG, cfg.S, cfg.NT, cfg.NBLK, cfg.QS, cfg.NF, cfg.F
    HW = cfg.HW
    NJ = HW // 128
    f32 = np.float32
    x = np.asarray(inputs["x"], f32)
    w_in = np.asarray(inputs["w_in"], f32)[0]
    perm64 = np.concatenate([np.arange(32, 64), np.arange(0, 32)])
    offs = dict(qa=0, ka=HW, va=2 * HW, qb=3 * HW, kb=4 * HW, vb=5 * HW)
    wg = np.zeros((NG, 128, KC * 1024), f32)
    for g in range(NG):
        cols = []
        base = np.arange(g * 128, (g + 1) * 128)
        pbase = np.concatenate([g * 128 + perm64, g * 128 + 64 + perm64])
        for nm, idx in (("qa", base), ("qa", pbase), ("ka", base), ("ka", pbase), ("qb", base), ("kb", base),
                        ("va", base), ("vb", base)):
            cols.append(offs[nm] + idx)
        cols = np.concatenate(cols)
        wsel = w_in[:, cols]
        wg[g] = wsel.reshape(KC, 128, 1024).transpose(1, 0, 2).reshape(128, KC * 1024)
    wgate_full = w_in[:, 6 * HW:]
    wgate = wgate_full.reshape(KC, 128, 2 * KC, 128).transpose(1, 2, 0, 3).reshape(128, 2 * KC, KC * 128)
    bgate = _fm(np.asarray(inputs["b_gate"], f32)[0], 2 * KC)
    wa = np.asarray(inputs["w_branch_a"], f32)[0].reshape(NJ, 128, KC, 128).transpose(1, 2, 0, 3).reshape(128, KC, NJ * 128)
    wb = np.asarray(inputs["w_branch_b"], f32)[0].reshape(NJ, 128, KC, 128).transpose(1, 2, 0, 3).reshape(128, KC, NJ * 128)
    wout = np.asarray(inputs["w_out"], f32)[0].reshape(KC, 128, D).transpose(1, 0, 2).reshape(128, KC * D)
    w_up = np.asarray(inputs["w_up"], f32)[0]
    wup = np.zeros((128, NF, KC * 256), f32)
    wu4 = w_up.reshape(KC, 128, 2, NF, 128)
    wup = wu4.transpose(1, 3, 0, 2, 4).reshape(128, NF, KC * 256)
    conv_w = np.asarray(inputs["conv_w"], f32)[0]
    cw = conv_w.reshape(3, 2, NF, 128).transpose(3, 2, 1, 0).reshape(128, NF * 6)
    conv_b = np.asarray(inputs["conv_b"], f32)[0]
    cb = conv_b.reshape(2, NF, 128).transpose(2, 1, 0).reshape(128, NF * 2)
    wdn = np.asarray(inputs["w_down"], f32)[0].reshape(NF, 128, D).transpose(1, 0, 2).reshape(128, NF * D)
    gmix = _fm(np.asarray(inputs["g_mix"], f32)[0], KC)
    gffn = _fm(np.asarray(inputs["g_ffn"], f32)[0], KC)
    gfin = np.ascontiguousarray(np.broadcast_to(np.asarray(inputs["g_final"], f32)[None, :], (128, D)))
    ident = np.eye(128, dtype=f32)
    kk = np.arange(128)
    tneg = -(kk[:, None] >= kk[None, :]).astype(f32)
    eall = np.zeros((32, 32, 128), f32)
    for n in range(32):
        eall[n, n, :] = 1.0
    eall = eall.reshape(32, 32 * 128)
    masks = np.zeros((128, 8, 512), f32)
    qq = np.arange(512)
    for v in range(4):
        masks[:, v, :] = ((v * 128 + kk)[:, None] < qq[None, :])
        masks[:, 4 + v, :] = ((v * 128 + kk)[:, None] <= qq[None, :])
    masks = masks.reshape(128, 8 * 512)
    half = HD // 2
    inv = (theta ** (-np.arange(half, dtype=f32) / half)).astype(f32)
    shared = dict(wg=wg, wgate=np.ascontiguousarray(wgate), bgate=bgate, wa=np.ascontiguousarray(wa),
                  wb=np.ascontiguousarray(wb), wout=np.ascontiguousarray(wout), wup=np.ascontiguousarray(wup),
                  cw=np.ascontiguousarray(cw), cb=np.ascontiguousarray(cb), wdn=np.ascontiguousarray(wdn),
                  gmix=gmix, gffn=gffn, gfin=gfin, ident=ident, tneg=tneg, eall=eall, masks=masks)
    in_maps = []
    for core in range(cfg.B * 4):
        b, j = core // 4, core % 4
        quarters = [(j + 1) % 4, (j + 2) % 4, (j + 3) % 4, j]
        xk = np.concatenate([x[b, q * QS:(q + 1) * QS] for q in quarters], axis=0)
        pos = np.concatenate([np.arange(q * QS, (q + 1) * QS) for q in quarters]).astype(f32)
        ang = pos[None, :] * np.tile(inv, 4)[:, None]
        cosT = np.cos(ang).astype(f32)
        sgn = np.tile(np.concatenate([-np.ones(32, f32), np.ones(32, f32)]), 2)
        sinT = (np.sin(ang) * sgn[:, None]).astype(f32)
        vis = np.array([1.0 if q < j else 0.0 for q in quarters[:3]] + [1.0], f32)
        kb_row = np.repeat(np.where(vis > 0, 0.0, -BIG).astype(f32), QS // 128)
        bf_row = np.full(32, 0.0, f32)
        bf_row[:NBLK] = np.repeat(np.where(vis > 0, 0.0, NEGINF).astype(f32), QS // 256)
        m = dict(shared)
        m.update(xk=np.ascontiguousarray(xk), cosT=cosT, sinT=sinT,
                 kbias=np.ascontiguousarray(np.broadcast_to(kb_row[None, :], (128, NT))),
                 bflag=np.ascontiguousarray(np.broadcast_to(bf_row[None, :], (128, 32))),
                 halof=np.full((128, 1), 1.0 if j > 0 else 0.0, f32))
        in_maps.append(m)
    return in_maps


_CACHE = {}


def kernel(**inputs):
    cfg = Cfg()
    in_maps = make_core_inputs(cfg, inputs)
    if "nc" not in _CACHE:
        _CACHE["nc"] = build(cfg)
    res = run_bass_kernel_spmd(_CACHE["nc"], in_maps, core_ids=list(range(8)))
    out = np.zeros((cfg.B, cfg.S, cfg.D), np.float32)
    for core in range(8):
        b, j = core // 4, core % 4
        out[b, j * cfg.QS:(j + 1) * cfg.QS] = np.asarray(res.results[core]["y"], np.float32)
    return out
```

```python
import contextlib
import numpy as np
import concourse.bass as bass
import concourse.mybir as mybir
from concourse.bass_utils import run_bass_kernel_spmd

F32 = mybir.dt.float32
BF16 = mybir.dt.bfloat16
AF = mybir.ActivationFunctionType
ALU = mybir.AluOpType
AX = mybir.AxisListType

HD = 64
BIG = 30000.0
NEGINF = -1e30
EPS = 1e-6


class Cfg:
    def __init__(s, D=1024, NH=8, QS=2048, F=2816, B=2):
        s.D, s.NH, s.QS, s.F, s.B = D, NH, QS, F, B
        s.KC = D // 128
        s.NG = NH // 2
        s.S = 4 * QS
        s.NT = s.S // 128
        s.NBLK = s.S // 256
        s.Q0 = s.S - QS - 128
        s.NQC = (QS + 128) // 128
        s.NQ = s.NQC * 128
        s.NF = F // 128
        s.HW = NH * HD
        s.stop = 0
        s.tiles = []
        c = 0
        while c < s.NQC:
            n = min(4, s.NQC - c)
            s.tiles.append((c, n))
            c += n


class _Rec:
    def __init__(s):
        s.call = None

    def __getattr__(s, name):
        def f(*a, **k):
            s.call = (name, a, k)
            return s
        return f


class TR:
    ENG = ('sp', 'act', 'dve', 'pool', 'pe')

    def __init__(s, nc, stack):
        s.nc = nc
        s.sem, s.cnt, s.lastw, s.readers = {}, {}, {}, {}
        s.waited = {e: {} for e in s.ENG}
        s.q = {e: [] for e in s.ENG}
        s.children = {}
        s.stack = stack
        s.n = 0

    def _rel(s, k):
        out = [k]
        if '.' in k:
            p = k.split('.')[0]
            out.append(p)
            s.children.setdefault(p, set()).add(k)
        else:
            out.extend(s.children.get(k, ()))
        return out

    def _sem(s, key):
        if key not in s.sem:
            s.sem[key] = s.stack.enter_context(s.nc.semaphore(key))
            s.cnt[key] = 0
        return s.sem[key]

    def op(s, e, fn, R=(), W=(), dma=None):
        deps = {}

        def add(ev):
            if ev is not None and ev[1] > deps.get(ev[0], 0):
                deps[ev[0]] = ev[1]
        for r in R:
            for rr in s._rel(r):
                add(s.lastw.get(rr))
        for w in W:
            for ww in s._rel(w):
                add(s.lastw.get(ww))
                for k, v in s.readers.get(ww, {}).items():
                    add((k, v))
        key, inc = ('E_' + e, 1) if dma is None else ('D_' + dma, 16)
        s._sem(key)
        waits = []
        for k, v in deps.items():
            if e == 'pe' and k == 'E_pe':
                continue
            if s.waited[e].get(k, 0) >= v:
                continue
            waits.append((s.sem[k], v))
            s.waited[e][k] = v
        rec = _Rec()
        fn(rec)
        s.cnt[key] += inc
        s.q[e].append((waits, rec.call, s.sem[key], inc))
        s.n += 1
        ev = (key, s.cnt[key])
        for r in R:
            d = s.readers.setdefault(r, {})
            d[key] = max(d.get(key, 0), ev[1])
        for w in W:
            s.lastw[w] = ev
            s.readers[w] = {}
        return ev

    def barrier(s, engines=None):
        for e in (engines or s.ENG):
            waits = []
            for k, v in s.cnt.items():
                if v > 0 and s.waited[e].get(k, 0) < v:
                    waits.append((s.sem[k], v))
                    s.waited[e][k] = v
            if waits:
                s.q[e].append((waits, None, None, 0))

    def flush(s):
        with s.nc.Block() as block:
            sect = {'sp': block.sync, 'act': block.scalar, 'dve': block.vector, 'pool': block.gpsimd,
                    'pe': block.tensor}
            for e in s.ENG:
                q = s.q[e]
                s.q[e] = []
                if not q:
                    continue

                def body(eng, q=q):
                    for waits, call, sem, inc in q:
                        for (wsem, v) in waits:
                            eng.wait_ge(wsem, v)
                        if call is not None:
                            name, a, k = call
                            getattr(eng, name)(*a, **k).then_inc(sem, inc)
                sect[e](body)


class _Stop(Exception):
    pass


def build(cfg):
    D, KC, NG, S, NT, NBLK, Q0, NQC, NQ, NF, QS = (cfg.D, cfg.KC, cfg.NG, cfg.S, cfg.NT, cfg.NBLK, cfg.Q0,
                                                  cfg.NQC, cfg.NQ, cfg.NF, cfg.QS)
    HW = cfg.HW
    NJ = HW // 128
    scale = HD ** -0.5
    NBW = min(512, D)
    NBN = D // NBW
    nc = bass.Bass("TRN2", target_bir_lowering=False)

    def din(name, shape):
        return nc.dram_tensor(name, list(shape), F32, kind="ExternalInput").ap()
    xk = din("xk", [S, D])
    wg = din("wg", [NG, 128, KC * 1024])
    cosT = din("cosT", [128, S])
    sinT = din("sinT", [128, S])
    kbias_d = din("kbias", [128, NT])
    bflag_d = din("bflag", [128, 32])
    halof_d = din("halof", [128, 1])
    ident_d = din("ident", [128, 128])
    tneg_d = din("tneg", [128, 128])
    eall_d = din("eall", [32, 32 * 128])
    masks_d = din("masks", [128, 8 * 512])
    gmix_d = din("gmix", [128, KC])
    gffn_d = din("gffn", [128, KC])
    gfin_d = din("gfin", [128, D])
    wgate_d = din("wgate", [128, 2 * KC, KC * 128])
    bgate_d = din("bgate", [128, 2 * KC])
    wa_d = din("wa", [128, KC, NJ * 128])
    wb_d = din("wb", [128, KC, NJ * 128])
    wout_d = din("wout", [128, KC * D])
    wup_d = din("wup", [128, NF, KC * 256])
    cw_d = din("cw", [128, NF * 6])
    cb_d = din("cb", [128, NF * 2])
    wdn_d = din("wdn", [128, NF * D])
    y = nc.dram_tensor("y", [QS, D], F32, kind="ExternalOutput").ap()
    hsc = nc.dram_tensor("hsc", [S // 512, 128, KC * 512], BF16).ap()
    WC_COLS = 2 * KC * KC * 128 + 2 * KC * NJ * 128 + KC * D + NF * (KC * 256 + D)
    wcache = nc.dram_tensor("wcache", [128, WC_COLS], BF16).ap()

    with contextlib.ExitStack() as stack:
        tr = TR(nc, stack)
        op = tr.op

        def sb(name, shape, dt=F32):
            return stack.enter_context(nc.sbuf_tensor(name, list(shape), dt))

        def ps(name, shape, dt=F32):
            return stack.enter_context(nc.psum_tensor(name, list(shape), dt))

        ident = sb("s_ident", [128, 128], BF16)
        tneg = sb("s_tneg", [128, 128], BF16)
        onesneg = sb("s_onesneg", [128, 128], BF16)
        eall = sb("s_eall", [32, 32 * 128], BF16)
        masks = sb("s_masks", [128, 8 * 512], BF16)
        kbias = sb("s_kbias", [128, NT])
        bflag = sb("s_bflag", [128, 32])
        halof = sb("s_halof", [128, 1])
        gmix = sb("s_gmix", [128, KC])
        gffn = sb("s_gffn", [128, KC])
        bgate = sb("s_bgate", [128, 2 * KC])
        cw = sb("s_cw", [128, NF * 6])
        cb = sb("s_cb", [128, NF * 2])
        stg = [sb("s_stg0", [128, 2048])]
        stg_i = [0]

        def load_cast(dst, dst_key, src_ap, n, cast_eng='dve'):
            i = stg_i[0] % len(stg)
            stg_i[0] += 1
            st = stg[i]
            op('sp', lambda e: e.dma_start(out=st[:, 0:n], in_=src_ap), W=['stg%d' % i], dma='stg%d' % i)
            if cast_eng == 'act':
                op('act', lambda e: e.activation(out=dst, in_=st[:, 0:n], func=AF.Copy), R=['stg%d' % i], W=[dst_key])
            else:
                op(cast_eng, lambda e: e.tensor_copy(out=dst, in_=st[:, 0:n]), R=['stg%d' % i], W=[dst_key])

        def load_f32(dst, key, src_ap):
            op('sp', lambda e: e.dma_start(out=dst, in_=src_ap), W=[key], dma=key)

        load_cast(ident[:, :], 'ident', ident_d, 128)
        load_cast(tneg[:, :], 'tneg', tneg_d, 128)
        for i in range(4):
            load_cast(masks[:, i * 1024:(i + 1) * 1024], 'masks', masks_d[:, i * 1024:(i + 1) * 1024], 1024)
        op('dve', lambda e: e.memset(onesneg[:, :], -1.0), W=['onesneg'])
        for hh in range(2):
            i = stg_i[0] % len(stg)
            stg_i[0] += 1
            op('sp', lambda e, i=i, hh=hh: e.dma_start(out=stg[i][0:32, 0:2048], in_=eall_d[:, hh * 2048:(hh + 1) * 2048]),
               W=['stg%d' % i], dma='stg%d' % i)
            op('dve', lambda e, i=i, hh=hh: e.tensor_copy(out=eall[:, hh * 2048:(hh + 1) * 2048], in_=stg[i][0:32, 0:2048]),
               R=['stg%d' % i], W=['eall'])
        load_f32(kbias[:, :], 'kbias', kbias_d)
        load_f32(bflag[:, :], 'bflag', bflag_d)
        load_f32(halof[:, :], 'halof', halof_d)
        load_f32(gmix[:, :], 'gmix', gmix_d)
        load_f32(gffn[:, :], 'gffn', gffn_d)
        load_f32(bgate[:, :], 'bgate', bgate_d)
        load_f32(cw[:, :], 'cw', cw_d)
        load_f32(cb[:, :], 'cb', cb_d)

        wc_off = {}
        wc_next = [0]
        pieces = []

        def add_piece(name, src_ap, n):
            wc_off[name] = wc_next[0]
            wc_next[0] += n
            pieces.append((name, src_ap, n))
        wst = min(2048, KC * D)
        for q4 in range(0, KC * D, wst):
            add_piece('wout%d' % q4, wout_d[:, q4:q4 + wst], wst)
        for fc in range(2 * KC):
            add_piece('wg%d' % fc, wgate_d[:, fc, :], KC * 128)
        for fc in range(KC):
            add_piece('wa%d' % fc, wa_d[:, fc, :], NJ * 128)
            add_piece('wb%d' % fc, wb_d[:, fc, :], NJ * 128)
        for f in range(NF):
            add_piece('wup%d' % f, wup_d[:, f, :], KC * 256)
            add_piece('wdn%d' % f, wdn_d[:, f * D:(f + 1) * D], D)
        assert wc_next[0] == WC_COLS, (wc_next[0], WC_COLS)

        yaT = sb("s_yaT", [128, NJ, NQ], BF16)
        ybT = sb("s_ybT", [128, NJ, NQ], BF16)

        xc = [sb("s_xc0", [128, D]), sb("s_xc1", [128, D])]
        junk = sb("s_junk", [128, D], BF16)
        hn2 = [sb("s_hn", [128, D], BF16), sb("s_hn1", [128, D], BF16)]
        st42 = [sb("s_st4", [128, 4]), sb("s_st41", [128, 4])]
        st4 = st42[0]
        nrm_i = [0]
        tp_ps = ps("tp_ps", [128, KC * 128], BF16)
        xc_i = [0]

        def norm_A(src_key, src):
            ni = nrm_i[0] % 2
            nrm_i[0] += 1
            st4, hn = st42[ni], hn2[ni]
            sk, hk, jk = 'st4_%d' % ni, 'hn_%d' % ni, 'junk'
            op('dve', lambda e: e.scalar_tensor_tensor(out=junk[:, :], in0=src, scalar=1.0, in1=src,
                                                       op0=ALU.mult, op1=ALU.mult, accum_out=st4[:, 0:1]),
               R=[src_key], W=[jk, sk])
            op('dve', lambda e: e.tensor_scalar(out=st4[:, 1:2], in0=st4[:, 0:1], scalar1=1.0 / D, scalar2=EPS,
                                                op0=ALU.mult, op1=ALU.add), R=[sk], W=[sk])
            op('act', lambda e: e.activation(out=st4[:, 2:3], in_=st4[:, 1:2], func=AF.Ln), R=[sk], W=[sk])
            op('act', lambda e: e.activation(out=st4[:, 3:4], in_=st4[:, 2:3], func=AF.Exp, scale=-0.5),
               R=[sk], W=[sk])
            op('act', lambda e: e.activation(out=hn[:, :], in_=src, func=AF.Identity, scale=st4[:, 3:4]),
               R=[src_key, sk], W=[hk])
            return (hn, hk)

        def norm_B(state, gt, gkey, dst_fn, dst_key):
            hn, hk = state
            for kc in range(KC):
                op('pe', lambda e, kc=kc: e.transpose(out=tp_ps[:, kc * 128:(kc + 1) * 128],
                                                     in_=hn[:, kc * 128:(kc + 1) * 128], identity=ident[:, :]),
                   R=[hk, 'ident'], W=['tp_ps'])
            for kc in range(KC):
                op('dve', lambda e, kc=kc: e.tensor_scalar(out=dst_fn(kc), in0=tp_ps[:, kc * 128:(kc + 1) * 128],
                                                           scalar1=gt[:, kc:kc + 1], scalar2=None, op0=ALU.mult),
                   R=['tp_ps', gkey], W=[dst_key])

        def norm_T(src_key, src, gt, gkey, dst_fn, dst_key):
            norm_B(norm_A(src_key, src), gt, gkey, dst_fn, dst_key)

        def load_x_chunk(t):
            i = xc_i[0] % 2
            xc_i[0] += 1
            op('sp', lambda e: e.dma_start(out=xc[i][:, :], in_=xk[t * 128:(t + 1) * 128, :]),
               W=['xc%d' % i], dma='xc%d' % i)
            return i

        def chk(k):
            if cfg.stop == k:
                raise _Stop()

        with contextlib.ExitStack() as st1:
          try:
              def sb1(name, shape, dt=F32):
                  return st1.enter_context(nc.sbuf_tensor(name, list(shape), dt))

              def ps1(name, shape, dt=F32):
                  return st1.enter_context(nc.psum_tensor(name, list(shape), dt))
              wgb = sb1("s_wgb", [128, KC * 1024], BF16)
              hT = sb1("s_hT", [128, KC, 512], BF16)
              cs = sb1("s_cs", [128, 512])
              sn = sb1("s_sn", [128, 512])
              kaT = sb1("s_kaT", [128, S], BF16)
              kbT = sb1("s_kbT", [128, S], BF16)
              vA = sb1("s_vA", [128, NT, 2 * 66], BF16)
              vB = sb1("s_vB", [128, NT, 128], BF16)
              qaT = sb1("s_qaT", [128, NQ], BF16)
              qbT = sb1("s_qbT", [128, NQ], BF16)
              km = sb1("s_km", [128, 32])
              kmb = sb1("s_kmb", [128, 32], BF16)
              gm = sb1("s_gm", [128, 32])
              m8 = sb1("s_m8", [128, 8])
              sel = sb1("s_sel", [128, 32])
              sel2 = sb1("s_sel2", [128, 32])
              biasq = sb1("s_biasq", [128, 32], BF16)
              gset = [(gm, m8, sel, sel2, biasq),
                      (sb1("s_gm2", [128, 32]), sb1("s_m82", [128, 8]), sb1("s_sel3", [128, 32]), sb1("s_sel4", [128, 32]),
                       sb1("s_biasq2", [128, 32], BF16))]
              biasT = [sb1("s_biasT0", [32, NQ], BF16), sb1("s_biasT1", [32, NQ], BF16)]
              hT2 = sb1("s_hT2", [128, KC, 512], BF16)
              xtra = [sb1("s_xb%d" % q, [128, 512], BF16) for q in range(max(0, 11 - 2 * KC))]
              ebuf = [sb1("s_e0", [128, 512]), sb1("s_e1", [128, 512])]
              rt1, rt2 = ebuf[0], ebuf[1]
              lsum = sb1("s_lsum", [128, 512], BF16)
              lsum2 = sb1("s_lsum2", [128, 512], BF16)
              ebuf2 = sb1("s_e2", [128, 512])
              ytok = sb1("s_ytok", [128, 4, 128], BF16)
              rden = sb1("s_rden", [128, 1])
              pj = [ps1("pj0", [128, 512]), ps1("pj1", [128, 512])]
              sc_ps = [ps1("sc0", [128, 512]), ps1("sc1", [128, 512])]
              af_ps = ps1("af", [128, 512])
              acc_ps = ps1("acc", [128, 512])
              sm_ps = ps1("sm", [128, 512], BF16)

              chk(100)
              for hh in range(2):
                  op('dve', lambda e, hh=hh: e.memset(vA[:, :, hh * 66 + 64: hh * 66 + 65], 1.0), W=['vA'])

              pj_i = [0]
              wtmp = sb1("s_wtmp", [128, 1024], BF16)

              def precast_gen():
                  for (name, src_ap, n) in pieces:
                      o = wc_off[name]
                      for sub in range(0, n, 1024):
                          m = min(1024, n - sub)
                          op('sp', lambda e: e.dma_start(out=stg[0][:, 0:m], in_=src_ap[:, sub:sub + m]), W=['stg0'],
                             dma='stg0')
                          op('dve', lambda e: e.tensor_copy(out=wtmp[:, 0:m], in_=stg[0][:, 0:m]), R=['stg0'], W=['wtmp'])
                          op('sp', lambda e: e.dma_start(out=wcache[:, o + sub:o + sub + m], in_=wtmp[:, 0:m]),
                             R=['wtmp'], W=['wc_' + name], dma='wcw_pre')
                          yield
              pc_gen = precast_gen()
              pc_step = [0]

              def precast_tick(force=False):
                  pc_step[0] += 1
                  if force or pc_step[0] % 12 == 0:
                      next(pc_gen, None)
              NHB = 2
              cur = [0]
              hTs = [hT, hT2]

              def hTv(kc, bi=None):
                  bi = cur[0] if bi is None else bi
                  return hTs[bi][:, kc, :]

              def hTk(bi=None):
                  return 'hTb%d' % (cur[0] if bi is None else bi)

              def proj(col0, ncols_tok, tok_off):
                  i = pj_i[0] % 2
                  pj_i[0] += 1
                  for kc in range(KC):
                      op('pe', lambda e, kc=kc: e.matmul(pj[i][:, 0:ncols_tok],
                                                         lhsT=wgb[:, kc * 1024 + col0: kc * 1024 + col0 + 128],
                                                         rhs=hTv(kc)[:, tok_off:tok_off + ncols_tok],
                                                         start=(kc == 0), stop=(kc == KC - 1)),
                         R=['wgb', hTk()], W=['pj%d' % i])
                  return i

              def rope_to(dst, dst_key, c_main, c_perm, n, tok_off):
                  i1 = proj(c_main, n, tok_off)
                  op('dve', lambda e: e.tensor_tensor(out=rt1[:, 0:n], in0=pj[i1][:, 0:n],
                                                      in1=cs[:, tok_off:tok_off + n], op=ALU.mult),
                     R=['pj%d' % i1, 'cs'], W=['e0'])
                  i2 = proj(c_perm, n, tok_off)
                  op('dve', lambda e: e.tensor_tensor(out=rt2[:, 0:n], in0=pj[i2][:, 0:n],
                                                      in1=sn[:, tok_off:tok_off + n], op=ALU.mult),
                     R=['pj%d' % i2, 'sn'], W=['e1'])
                  op('pool', lambda e: e.tensor_tensor(out=dst, in0=rt1[:, 0:n], in1=rt2[:, 0:n], op=ALU.add),
                     R=['e0', 'e1'], W=[dst_key])

              chk(1)
              for g in range(NG):
                  for kc in range(0, KC, 2):
                      n = min(2, KC - kc) * 1024
                      load_cast(wgb[:, kc * 1024: kc * 1024 + n], 'wgb', wg[g, :, kc * 1024: kc * 1024 + n], n,
                                cast_eng='dve')
                  chk(11)
                  for tt in range(S // 512):
                      cur[0] = tt % NHB
                      if g == 0:
                          pend = None
                          for c in range(4):
                              t = tt * 4 + c
                              i = load_x_chunk(t)
                              stt = norm_A('xc%d' % i, xc[i][:, :])
                              if pend is not None:
                                  norm_B(pend[0], gmix, 'gmix', lambda kc, c=pend[1]: hTv(kc)[:, c * 128:(c + 1) * 128], hTk())
                              pend = (stt, c)
                              chk(12)
                          norm_B(pend[0], gmix, 'gmix', lambda kc, c=pend[1]: hTv(kc)[:, c * 128:(c + 1) * 128], hTk())
                          op('sp', lambda e: e.dma_start(out=hsc[tt], in_=hTs[cur[0]][:, :, :].rearrange("p k t -> p (k t)")),
                             R=[hTk()], W=['hsc%d' % tt], dma='hscw%d' % cur[0])
                      else:
                          op('sp', lambda e: e.dma_start(out=hTs[cur[0]][:, :, :].rearrange("p k t -> p (k t)"), in_=hsc[tt]),
                             R=['hsc%d' % tt], W=[hTk()], dma='hTl%d' % cur[0])
                      load_f32(cs[:, :], 'cs', cosT[:, tt * 512:(tt + 1) * 512])
                      load_f32(sn[:, :], 'sn', sinT[:, tt * 512:(tt + 1) * 512])
                      tk = 'kv%d' % tt
                      rope_to(kaT[:, tt * 512:(tt + 1) * 512], 'kaT' + tk, 256, 384, 512, 0)
                      chk(13)
                      i = proj(640, 512, 0)
                      op('act', lambda e, i=i: e.activation(out=kbT[:, tt * 512:(tt + 1) * 512], in_=pj[i][:, :],
                                                            func=AF.Copy), R=['pj%d' % i], W=['kbT' + tk])
                      chk(14)
                      for c in range(4):
                          t = tt * 4 + c
                          i = pj_i[0] % 2
                          pj_i[0] += 1
                          for kc in range(KC):
                              op('pe', lambda e, kc=kc, c=c, i=i: e.matmul(
                                  pj[i][:, 0:256], lhsT=hTv(kc)[:, c * 128:(c + 1) * 128],
                                  rhs=wgb[:, kc * 1024 + 768: kc * 1024 + 1024],
                                  start=(kc == 0), stop=(kc == KC - 1)), R=['wgb', hTk()], W=['pj%d' % i])
                          chk(151)
                          for hh in range(2):
                              op('act', lambda e, t=t, i=i, hh=hh: e.activation(
                                  out=vA[:, t, hh * 66: hh * 66 + 64], in_=pj[i][:, hh * 64:(hh + 1) * 64], func=AF.Copy),
                                 R=['pj%d' % i], W=['vA'])
                          chk(152)
                          op('act', lambda e, t=t, i=i: e.activation(out=vB[:, t, :], in_=pj[i][:, 128:256], func=AF.Copy),
                             R=['pj%d' % i], W=['vB'])
                      chk(15)
                      lo = max(tt * 512, Q0)
                      if lo < (tt + 1) * 512:
                          off = lo - tt * 512
                          n = 512 - off
                          rope_to(qaT[:, lo - Q0: lo - Q0 + n], 'qaT', 0, 128, n, off)
                          i = proj(512, n, off)
                          op('act', lambda e, i=i, lo=lo, n=n: e.activation(out=qbT[:, lo - Q0: lo - Q0 + n],
                                                                            in_=pj[i][:, 0:n], func=AF.Copy),
                             R=['pj%d' % i], W=['qbT'])
                  chk(2)
                  kv_keys = ['kv%d' % tt for tt in range(S // 512)]
                  for nb0 in range(0, NBLK, 8):
                      op('dve', lambda e, nb0=nb0: e.tensor_reduce(
                          out=km[:, nb0:nb0 + 8],
                          in_=kaT[:, nb0 * 256:(nb0 + 8) * 256].rearrange("p (n k) -> p n k", k=256),
                          op=ALU.add, axis=AX.X), R=['kaT' + k for k in kv_keys], W=['km'])
                  op('dve', lambda e: e.tensor_scalar(out=kmb[:, 0:NBLK], in0=km[:, 0:NBLK], scalar1=1.0 / 256,
                                                      scalar2=None, op0=ALU.mult), R=['km'], W=['kmb'])
                  git = 0
                  for h in range(2):
                      hp = slice(64 * h, 64 * h + 64)
                      for qc in range(NQC):
                          Bq = (Q0 + qc * 128) // 256
                          z = git % 2
                          git += 1
                          gm_, m8_, sel_, sel2_, bq_ = gset[z]
                          scz, sck = sc_ps[z], 'sc%d' % z
                          smz, smk = sm_ps[0:32, 0:128], 'sm'
                          kk = ['gm%d' % z, 'm8%d' % z, 'sel%d' % z, 'selb%d' % z, 'bq%d' % z]
                          op('pe', lambda e: e.matmul(scz[:, 0:NBLK], lhsT=qaT[hp, qc * 128:(qc + 1) * 128],
                                                      rhs=kmb[hp, 0:NBLK], start=True, stop=True),
                             R=['qaT', 'kmb'], W=[sck])
                          op('dve', lambda e: e.memset(gm_[:, :], NEGINF), W=[kk[0]])
                          op('dve', lambda e: e.tensor_tensor(out=gm_[:, 0:Bq], in0=scz[:, 0:Bq], in1=bflag[:, 0:Bq],
                                                              op=ALU.add), R=[sck, 'bflag'], W=[kk[0]])
                          op('dve', lambda e: e.max(out=m8_[:, :], in_=gm_[:, 0:NBLK]), R=[kk[0]], W=[kk[1]])
                          op('dve', lambda e: e.tensor_scalar(out=sel_[:, :], in0=gm_[:, :], scalar1=m8_[:, 2:3],
                                                              scalar2=None, op0=ALU.is_ge), R=[kk[0], kk[1]], W=[kk[2]])
                          op('dve', lambda e: e.tensor_scalar(out=sel2_[:, :], in0=gm_[:, :], scalar1=-1e29,
                                                              scalar2=None, op0=ALU.is_gt), R=[kk[0]], W=[kk[3]])
                          op('dve', lambda e: e.tensor_tensor(out=sel_[:, :], in0=sel_[:, :], in1=sel2_[:, :],
                                                              op=ALU.mult), R=[kk[2], kk[3]], W=[kk[2]])
                          op('dve', lambda e: e.tensor_scalar(out=bq_[:, :], in0=sel_[:, :], scalar1=BIG,
                                                              scalar2=-BIG, op0=ALU.mult, op1=ALU.add),
                             R=[kk[2]], W=[kk[4]])
                          op('dve', lambda e: e.memset(bq_[:, Bq:Bq + 1], 0.0), W=[kk[4]])
                          op('pe', lambda e: e.transpose(out=smz, in_=bq_[:, :], identity=ident[:, :]),
                             R=[kk[4], 'ident'], W=[smk])
                          op('act', lambda e: e.activation(out=biasT[h][:, qc * 128:(qc + 1) * 128], in_=smz, func=AF.Copy),
                             R=[smk], W=['biasT%d' % h])
                  chk(3)
                  tr.barrier()
                  ebufs = [(ebuf[0], 'e0'), (ebuf[1], 'e1'), (cs, 'cs'), (sn, 'sn'), (ebuf2, 'e2')]
                  hsl = [(hT[:, kc, :], 'hTs%d' % kc) for kc in range(KC)]
                  bfb = [(hT2[:, kc, :], 'hT2s%d' % kc) for kc in range(KC)] + \
                        [(xtra[q], 'xb%d' % q) for q in range(len(xtra))] + hsl
                  assert len(bfb) >= 10
                  sp_b, E_b, a_b, p_b = bfb[0:3], bfb[3:5], bfb[5:8], bfb[8:8 + 3] if len(bfb) >= 11 else bfb[8:10]
                  lsums = [(lsum, 'lsum'), (lsum2, 'lsum2')]
                  fbanks = [(sc_ps[0], 'sc0'), (sc_ps[1], 'sc1'), (pj[0], 'pj0'), (pj[1], 'pj1'), (af_ps, 'af'), (acc_ps, 'acc')]

                  def run_pipeline(nblocks, stages):
                      ns = len(stages)
                      for t in range(nblocks + ns - 1):
                          precast_tick()
                          for s in range(ns - 1, -1, -1):
                              i = t - s
                              if 0 <= i < nblocks:
                                  stages[s](i)

                  for (c0, ncq) in cfg.tiles:
                      W = ncq * 128
                      q0 = Q0 + c0 * 128
                      qs = slice(c0 * 128, c0 * 128 + W)
                      kb_max = (q0 + W) // 128 - 1
                      kb_diag = q0 // 128
                      mb = [(kb, h) for kb in range(kb_max + 1) for h in range(2)]
                      m_sc = fbanks[0:4]
                      m_acc = [fbanks[5], fbanks[4]]

                      def m_s0(i):
                          kb, h = mb[i]
                          hp = slice(64 * h, 64 * h + 64)
                          sc, sck = m_sc[i % 4]
                          op('pe', lambda e: e.matmul(sc[:, 0:W], lhsT=kaT[hp, kb * 128:(kb + 1) * 128], rhs=qaT[hp, qs],
                                                      start=True, stop=False), R=['kaTkv%d' % (kb // 4), 'qaT'], W=[sck])
                          n = kb // 2
                          op('pe', lambda e: e.matmul(sc[:, 0:W], lhsT=eall[:, n * 128:(n + 1) * 128], rhs=biasT[h][:, qs],
                                                      start=False, stop=True), R=['eall', 'biasT%d' % h], W=[sck])

                      def m_s1(i):
                          kb, h = mb[i]
                          sc, sck = m_sc[i % 4]
                          p, pk = p_b[i % len(p_b)]
                          op('act', lambda e: e.activation(out=p[:, 0:W], in_=sc[:, 0:W], func=AF.Exp, scale=scale),
                             R=[sck], W=[pk])
                          if kb >= kb_diag:
                              v = 4 + (kb - kb_diag)
                              op('dve', lambda e: e.tensor_tensor(out=p[:, 0:W], in0=p[:, 0:W],
                                                                  in1=masks[:, v * 512: v * 512 + W], op=ALU.mult),
                                 R=[pk, 'masks'], W=[pk])

                      def m_s2(i):
                          kb, h = mb[i]
                          p, pk = p_b[i % len(p_b)]
                          acc, acck = m_acc[h]
                          for c in range(ncq):
                              op('pe', lambda e, c=c: e.matmul(acc[:, c * 65:(c + 1) * 65], lhsT=p[:, c * 128:(c + 1) * 128],
                                                               rhs=vA[:, kb, h * 66: h * 66 + 65],
                                                               start=(kb == 0 and c == 0),
                                                               stop=(kb == kb_max and c == ncq - 1)),
                                 R=[pk, 'vA'], W=[acck])
                      run_pipeline(len(mb), [m_s0, m_s1, m_s2])
                      for h in range(2):
                          acc, acck = m_acc[h]
                          for c in range(ncq):
                              op('dve', lambda e, c=c: e.reciprocal(out=rden[:, :], in_=acc[:, c * 65 + 64: c * 65 + 65]),
                                 R=[acck], W=['rden'])
                              op('dve', lambda e, c=c, h=h: e.tensor_scalar(out=ytok[:, c, 64 * h:64 * h + 64],
                                                                            in0=acc[:, c * 65: c * 65 + 64],
                                                                            scalar1=rden[:, 0:1], scalar2=None,
                                                                            op0=ALU.mult),
                                 R=[acck, 'rden'], W=['ytok'])
                      for c in range(ncq):
                          op('pe', lambda e, c=c: e.transpose(out=sm_ps[:, 128:256], in_=ytok[:, c, :], identity=ident[:, :]),
                             R=['ytok', 'ident'], W=['sm'])
                          op('act', lambda e, c=c: e.activation(out=yaT[:, g, (c0 + c) * 128:(c0 + c + 1) * 128],
                                                                in_=sm_ps[:, 128:256], func=AF.Copy),
                             R=['sm'], W=['yaT'])
                      chk(4)
                      sbk = [(kb, h) for kb in range(kb_max, -1, -1) for h in range(2)]
                      s_sc = fbanks[0:2]
                      s_af = [fbanks[4], fbanks[2]]
                      s_acc = [fbanks[5], fbanks[3]]

                      def s_s0(i):
                          kb, h = sbk[i]
                          hp = slice(64 * h, 64 * h + 64)
                          sc, sck = s_sc[i % 2]
                          op('pe', lambda e: e.matmul(sc[:, 0:W], lhsT=kbT[hp, kb * 128:(kb + 1) * 128], rhs=qbT[hp, qs],
                                                      start=True, stop=True), R=['kbTkv%d' % (kb // 4), 'qbT'], W=[sck])

                      def s_s1(i):
                          kb, h = sbk[i]
                          sc, sck = s_sc[i % 2]
                          eb, ek = ebufs[i % 5]
                          op('act', lambda e: e.activation(out=eb[:, 0:W], in_=sc[:, 0:W], func=AF.Exp, scale=scale,
                                                           bias=kbias[:, kb:kb + 1]), R=[sck, 'kbias'], W=[ek])
                          if kb >= kb_diag:
                              v = kb - kb_diag
                              op('dve', lambda e: e.tensor_tensor(out=eb[:, 0:W], in0=eb[:, 0:W],
                                                                  in1=masks[:, v * 512: v * 512 + W], op=ALU.mult),
                                 R=[ek, 'masks'], W=[ek])

                      def s_s2(i):
                          eb, ek = ebufs[i % 5]
                          spb, spk = sp_b[i % 3]
                          op('act', lambda e: e.activation(out=spb[:, 0:W], in_=eb[:, 0:W], func=AF.Ln, bias=1.0),
                             R=[ek], W=[spk])

                      def s_s3(i):
                          kb, h = sbk[i]
                          first = (kb == kb_max)
                          spb, spk = sp_b[i % 3]
                          af, afk = s_af[i % 2]
                          ls, lsk = lsums[h]
                          op('pe', lambda e: e.matmul(af[:, 0:W], lhsT=tneg[:, :], rhs=spb[:, 0:W], start=True, stop=first),
                             R=['tneg', spk], W=[afk])
                          if not first:
                              op('pe', lambda e: e.matmul(af[:, 0:W], lhsT=onesneg[:, :], rhs=ls[:, 0:W], start=False,
                                                          stop=True), R=['onesneg', lsk], W=[afk])
                          if kb != 0:
                              if first:
                                  op('dve', lambda e: e.tensor_copy(out=ls[:, 0:W], in_=spb[:, 0:W]), R=[spk], W=[lsk])
                              else:
                                  op('pool', lambda e: e.tensor_tensor(out=ls[:, 0:W], in0=ls[:, 0:W], in1=spb[:, 0:W],
                                                                       op=ALU.add), R=[spk, lsk], W=[lsk])

                      def s_s4(i):
                          af, afk = s_af[i % 2]
                          Eb, Ek = E_b[i % 2]
                          op('act', lambda e: e.activation(out=Eb[:, 0:W], in_=af[:, 0:W], func=AF.Exp), R=[afk], W=[Ek])

                      def s_s5(i):
                          eb, ek = ebufs[i % 5]
                          Eb, Ek = E_b[i % 2]
                          ab, ak = a_b[i % 3]
                          op('dve', lambda e: e.tensor_tensor(out=ab[:, 0:W], in0=eb[:, 0:W], in1=Eb[:, 0:W], op=ALU.mult),
                             R=[ek, Ek], W=[ak])

                      def s_s6(i):
                          kb, h = sbk[i]
                          hp = slice(64 * h, 64 * h + 64)
                          ab, ak = a_b[i % 3]
                          acc, acck = s_acc[h]
                          op('pe', lambda e: e.matmul(acc[hp, 0:W], lhsT=vB[:, kb, 64 * h:64 * h + 64], rhs=ab[:, 0:W],
                                                      start=(kb == kb_max), stop=(kb == 0)), R=['vB', ak], W=[acck])
                      run_pipeline(len(sbk), [s_s0, s_s1, s_s2, s_s3, s_s4, s_s5, s_s6])
                      for h in range(2):
                          hp = slice(64 * h, 64 * h + 64)
                          acc, acck = s_acc[h]
                          op('act', lambda e: e.activation(out=ybT[hp, g, qs], in_=acc[hp, 0:W], func=AF.Copy),
                             R=[acck], W=['ybT'])
                  tr.barrier()
              for _ in pc_gen:
                  pass

          except _Stop:
            pass
          if 0 < cfg.stop < 200:
            tr.barrier(engines=['sp'])
          tr.flush()
        if 0 < cfg.stop < 200:
            return nc
        tr.barrier()
        with contextlib.ExitStack() as st2:
          try:
              def sb2(name, shape, dt=F32):
                  return st2.enter_context(nc.sbuf_tensor(name, list(shape), dt))

              def ps2(name, shape, dt=F32):
                  return st2.enter_context(nc.psum_tensor(name, list(shape), dt))
              stg.append(sb2("s_stg1", [128, 2048]))
              x1 = sb2("s_x1", [128, 4, D])
              hTo = sb2("s_hTo", [128, KC, 512], BF16)
              gT = sb2("s_gT", [128, 2 * KC, 512], BF16)
              mT = sb2("s_mT", [128, KC, 512], BF16)
              mt1 = sb2("s_mt1", [128, 512])
              mt2 = sb2("s_mt2", [128, 512])
              wsm = [sb2("s_wsm%d" % q, [128, 1024], BF16) for q in range(4)]
              woutb = sb2("s_woutb", [128, KC * D], BF16)
              h2T = sb2("s_h2T", [128, KC, 512], BF16)
              ur = [sb2("s_ur0", [128, 2 + 512]), sb2("s_ur1", [128, 2 + 512])]
              carry = sb2("s_carry", [128, NF * 2, 2])
              cv = [sb2("s_cv0", [128, 512]), sb2("s_cv1", [128, 512])]
              sg = sb2("s_sg", [128, 512])
              urs = [[(ur[0], 'ur0'), (ur[1], 'ur1')],
                     [(stg[1][:, 0:514], 'stg1.a'), (stg[1][:, 514:1028], 'stg1.b')]]
              cvs = [[(cv[0], 'cv0'), (cv[1], 'cv1')],
                     [(stg[0][:, 0:512], 'stg0.a'), (stg[0][:, 512:1024], 'stg0.b')]]
              sgs = [(sg, 'sg'), (stg[0][:, 1024:1536], 'stg0.c')]
              prod = sb2("s_prod", [128, 2, 512], BF16)
              wup = [sb2("s_wup%d" % q, [128, KC * 256], BF16) for q in range(4)]
              wdn = [sb2("s_wdn%d" % q, [128, D], BF16) for q in range(4)]
              gfin = sb2("s_gfin", [128, D])
              st4f = sb2("s_st4f", [128, 4])
              ob = [xc[0], xc[1]]
              g_ps = [ps2("g0", [128, 512]), ps2("g1", [128, 512])]
              b_ps = [ps2("b0", [128, 512]), ps2("b1", [128, 512])]
              o_ps = ps2("o_ps", [128, 1024])
              wsm_i = [0]
              load_f32(gfin[:, :], 'gfin', gfin_d)
              chk(200)

              def cached_load(dst, dst_key, src_ap, n, name, first, eng='act'):
                  o = wc_off[name]
                  first = False
                  if first:
                      load_cast(dst, dst_key, src_ap, n, cast_eng=eng)
                      op('sp', lambda e: e.dma_start(out=wcache[:, o:o + n], in_=dst), R=[dst_key], W=['wc_' + name],
                         dma='wcw_' + dst_key)
                  else:
                      op('sp', lambda e: e.dma_start(out=dst, in_=wcache[:, o:o + n]), R=['wc_' + name], W=[dst_key],
                         dma='L_' + dst_key)

              def wload(src_ap, n, name, first, eng='act'):
                  i = wsm_i[0] % 4
                  wsm_i[0] += 1
                  cached_load(wsm[i][:, 0:n], 'wsm%d' % i, src_ap, n, name, first, eng=eng)
                  return i

              gi = [0]
              ob_i = [0]
              for ti, (c0, ncq) in enumerate(cfg.tiles):
                  W = ncq * 128
                  qs = slice(c0 * 128, c0 * 128 + W)
                  x1k = ['x1_%d' % c for c in range(ncq)]
                  for c in range(ncq):
                      t = Q0 // 128 + c0 + c
                      op('sp', lambda e, c=c, t=t: e.dma_start(out=x1[:, c, :], in_=xk[t * 128:(t + 1) * 128, :]),
                         W=[x1k[c]], dma=x1k[c])
                      norm_T(x1k[c], x1[:, c, :], gmix, 'gmix',
                             lambda kc, c=c: hTo[:, kc, c * 128:(c + 1) * 128], 'hTo')
                      chk(201)
                  wst = min(2048, KC * D)
                  for q4 in range(0, KC * D, wst):
                      cached_load(woutb[:, q4:q4 + wst], 'woutb', wout_d[:, q4:q4 + wst], wst, 'wout%d' % q4, ti == 0)
                  for fc in range(2 * KC):
                      wi = wload(wgate_d[:, fc, :], KC * 128, 'wg%d' % fc, ti == 0)
                      b = gi[0] % 2
                      gi[0] += 1
                      for kc in range(KC):
                          op('pe', lambda e, kc=kc, wi=wi, b=b: e.matmul(g_ps[b][:, 0:W], lhsT=wsm[wi][:, kc * 128:(kc + 1) * 128],
                                                                         rhs=hTo[:, kc, 0:W], start=(kc == 0),
                                                                         stop=(kc == KC - 1)),
                             R=['wsm%d' % wi, 'hTo'], W=['g%d' % b])
                      op('act', lambda e, fc=fc, b=b: e.activation(out=gT[:, fc, 0:W], in_=g_ps[b][:, 0:W],
                                                                   func=AF.Sigmoid, bias=bgate[:, fc:fc + 1]),
                         R=['g%d' % b, 'bgate'], W=['gT'])
                      chk(202)
                  for fc in range(KC):
                      wia = wload(wa_d[:, fc, :], NJ * 128, 'wa%d' % fc, ti == 0, eng='dve')
                      wib = wload(wb_d[:, fc, :], NJ * 128, 'wb%d' % fc, ti == 0, eng='dve')
                      b = gi[0] % 2
                      gi[0] += 1
                      for j in range(NJ):
                          op('pe', lambda e, j=j, wia=wia, b=b: e.matmul(g_ps[b][:, 0:W], lhsT=wsm[wia][:, j * 128:(j + 1) * 128],
                                                                         rhs=yaT[:, j, qs], start=(j == 0), stop=(j == NJ - 1)),
                             R=['wsm%d' % wia, 'yaT'], W=['g%d' % b])
                      for j in range(NJ):
                          op('pe', lambda e, j=j, wib=wib, b=b: e.matmul(b_ps[b][:, 0:W], lhsT=wsm[wib][:, j * 128:(j + 1) * 128],
                                                                         rhs=ybT[:, j, qs], start=(j == 0), stop=(j == NJ - 1)),
                             R=['wsm%d' % wib, 'ybT'], W=['b%d' % b])
                      op('dve', lambda e, fc=fc, b=b: e.tensor_tensor(out=mt1[:, 0:W], in0=g_ps[b][:, 0:W],
                                                                      in1=gT[:, fc, 0:W], op=ALU.mult),
                         R=['g%d' % b, 'gT'], W=['mt1'])
                      op('dve', lambda e, fc=fc, b=b: e.tensor_tensor(out=mt2[:, 0:W], in0=b_ps[b][:, 0:W],
                                                                      in1=gT[:, KC + fc, 0:W], op=ALU.mult),
                         R=['b%d' % b, 'gT'], W=['mt2'])
                      op('pool', lambda e, fc=fc: e.tensor_tensor(out=mT[:, fc, 0:W], in0=mt1[:, 0:W], in1=mt2[:, 0:W],
                                                                  op=ALU.add), R=['mt1', 'mt2'], W=['mT'])
                      chk(203)
                  for c in range(ncq):
                      for nb in range(NBN):
                          for kc in range(KC):
                              op('pe', lambda e, kc=kc, c=c, nb=nb: e.matmul(
                                  o_ps[:, nb * NBW:(nb + 1) * NBW], lhsT=mT[:, kc, c * 128:(c + 1) * 128],
                                  rhs=woutb[:, kc * D + nb * NBW: kc * D + nb * NBW + NBW], start=(kc == 0),
                                  stop=(kc == KC - 1)), R=['woutb', 'mT'], W=['o_ps'])
                      op('dve', lambda e, c=c: e.tensor_tensor(out=x1[:, c, :], in0=o_ps[:, 0:D], in1=x1[:, c, :],
                                                               op=ALU.add), R=['o_ps', x1k[c]], W=[x1k[c]])
                      norm_T(x1k[c], x1[:, c, :], gffn, 'gffn',
                             lambda kc, c=c: h2T[:, kc, c * 128:(c + 1) * 128], 'h2T')
                      chk(204)
                  gbanks = [(g_ps[0], 'g0'), (g_ps[1], 'g1'), (b_ps[0], 'b0'), (b_ps[1], 'b1')]
                  if False:
                      cv3 = [cvs[0], [(mt1, 'mt1'), (mt2, 'mt2')]]
                      ur_s, sg_s = [urs[0]], [sgs[0]]
                  else:
                      cv3 = cvs + [[(mt1, 'mt1'), (mt2, 'mt2')]]
                      ur_s, sg_s = urs, sgs
                  NCV, NUR = len(cv3), len(ur_s)
                  prod4 = [(masks[:, q * 512:(q + 1) * 512], 'masks.p%d' % q) for q in range(4)]
                  wdn6 = [(wdn[q][:, 0:D], 'wdn%d' % q) for q in range(4)]
                  if D <= 1024:
                      wdn6 += [(masks[:, 2048 + q * 1024: 2048 + q * 1024 + D], 'masks.w%d' % q) for q in range(2)]
                  NWD = len(wdn6)

                  def f_load(f):
                      if f >= NF:
                          return
                      cached_load(wup[f % 4][:, :], 'wup%d' % (f % 4), wup_d[:, f, :], KC * 256, 'wup%d' % f, ti == 0,
                                  eng=('act' if f % 2 == 0 else 'dve'))

                  def f_load_dn(f):
                      wd, wdk = wdn6[f % NWD]
                      cached_load(wd, wdk, wdn_d[:, f * D:(f + 1) * D], D, 'wdn%d' % f, ti == 0,
                                  eng=('dve' if f % 2 == 0 else 'act'))

                  def f_s0(f):
                      f_load(f + 2)
                      for gv in range(2):
                          gb, gbk = gbanks[(f % 2) * 2 + gv]
                          for kc in range(KC):
                              op('pe', lambda e, kc=kc: e.matmul(
                                  gb[:, 0:W], lhsT=wup[f % 4][:, kc * 256 + gv * 128: kc * 256 + gv * 128 + 128],
                                  rhs=h2T[:, kc, 0:W], start=(kc == 0), stop=(kc == KC - 1)),
                                 R=['wup%d' % (f % 4), 'h2T'], W=[gbk])

                  def f_s1(f):
                      for gv in range(2):
                          gb, gbk = gbanks[(f % 2) * 2 + gv]
                          u, uk = ur_s[f % NUR][gv]
                          cidx = f * 2 + gv
                          if ti == 0:
                              op('dve', lambda e: e.memset(u[:, 0:2], 0.0), W=[uk])
                          else:
                              op('dve', lambda e: e.tensor_copy(out=u[:, 0:2], in_=carry[:, cidx, :]),
                                 R=['carry.%d' % cidx], W=[uk])
                          op('act', lambda e: e.activation(out=u[:, 2:2 + W], in_=gb[:, 0:W], func=AF.Copy),
                             R=[gbk], W=[uk])
                          if ti == 0:
                              op('dve', lambda e: e.tensor_scalar(out=u[:, 2:130], in0=u[:, 2:130], scalar1=halof[:, 0:1],
                                                                  scalar2=None, op0=ALU.mult), R=[uk, 'halof'], W=[uk])

                  def f_s2(f):
                      f_load_dn(f)
                      for gv in range(2):
                          u, uk = ur_s[f % NUR][gv]
                          cvt, cvk = cv3[f % NCV][gv]
                          cidx = f * 2 + gv
                          ci = f * 6 + gv * 3
                          op('dve', lambda e: e.tensor_copy(out=carry[:, cidx, :], in_=u[:, W:W + 2]),
                             R=[uk], W=['carry.%d' % cidx])
                          op('dve', lambda e: e.tensor_scalar(
                              out=cvt[:, 0:W], in0=u[:, 2:2 + W], scalar1=cw[:, ci + 2:ci + 3],
                              scalar2=cb[:, f * 2 + gv:f * 2 + gv + 1], op0=ALU.mult, op1=ALU.add),
                             R=[uk, 'cw', 'cb'], W=[cvk])
                          op('dve', lambda e: e.scalar_tensor_tensor(
                              out=cvt[:, 0:W], in0=u[:, 1:1 + W], scalar=cw[:, ci + 1:ci + 2], in1=cvt[:, 0:W],
                              op0=ALU.mult, op1=ALU.add), R=[uk, 'cw', cvk], W=[cvk])
                          op('dve', lambda e: e.scalar_tensor_tensor(
                              out=cvt[:, 0:W], in0=u[:, 0:W], scalar=cw[:, ci:ci + 1], in1=cvt[:, 0:W],
                              op0=ALU.mult, op1=ALU.add), R=[uk, 'cw', cvk], W=[cvk])

                  def f_s3(f):
                      sgt, sgk = sg_s[f % NUR]
                      cg, cgk = cv3[f % NCV][0]
                      op('act', lambda e: e.activation(out=sgt[:, 0:W], in_=cg[:, 0:W], func=AF.Sigmoid), R=[cgk], W=[sgk])
                      op('pool', lambda e: e.tensor_tensor(out=sgt[:, 0:W], in0=sgt[:, 0:W], in1=cg[:, 0:W], op=ALU.mult),
                         R=[sgk, cgk], W=[sgk])

                  def f_s4(f):
                      sgt, sgk = sg_s[f % NUR]
                      cvv, cvvk = cv3[f % NCV][1]
                      pr, prk = prod4[f % 4]
                      op('dve', lambda e: e.tensor_tensor(out=pr[:, 0:W], in0=sgt[:, 0:W], in1=cvv[:, 0:W], op=ALU.mult),
                         R=[sgk, cvvk], W=[prk])

                  def f_s5(f):
                      if f % 2 == 0 and f != NF - 1:
                          return
                      fls = [f - 1, f] if f % 2 == 1 else [f]
                      for c in range(ncq):
                          if c0 + c == 0:
                              continue
                          for nb in range(NBN):
                              for q, ff in enumerate(fls):
                                  pr, prk = prod4[ff % 4]
                                  wd, wdk = wdn6[ff % NWD]
                                  op('pe', lambda e: e.matmul(o_ps[:, nb * NBW:(nb + 1) * NBW],
                                                              lhsT=pr[:, c * 128:(c + 1) * 128],
                                                              rhs=wd[:, nb * NBW:(nb + 1) * NBW], start=(q == 0),
                                                              stop=(q == len(fls) - 1)), R=[prk, wdk], W=['o_ps'])
                          op('dve', lambda e: e.tensor_tensor(out=x1[:, c, :], in0=o_ps[:, 0:D], in1=x1[:, c, :], op=ALU.add),
                             R=['o_ps', x1k[c]], W=[x1k[c]])

                  f_load(0)
                  f_load(1)
                  stages = [f_s0, f_s1, f_s2, f_s3, f_s4, f_s5]
                  for t in range(NF + len(stages) - 1):
                      for s in range(len(stages) - 1, -1, -1):
                          f = t - s
                          if 0 <= f < NF:
                              stages[s](f)
                  chk(206)
                  if ti == 0:
                      tr.barrier(engines=['sp'])
                  chk(206)
                  for c in range(ncq):
                      if c0 + c == 0:
                          continue
                      i = ob_i[0] % 2
                      ob_i[0] += 1
                      src = x1[:, c, :]
                      op('dve', lambda e, src=src: e.scalar_tensor_tensor(out=junk[:, :], in0=src, scalar=1.0, in1=src,
                                                                          op0=ALU.mult, op1=ALU.mult, accum_out=st4f[:, 0:1]),
                         R=[x1k[c]], W=['junk', 'st4f'])
                      op('dve', lambda e: e.tensor_scalar(out=st4f[:, 1:2], in0=st4f[:, 0:1], scalar1=1.0 / D, scalar2=EPS,
                                                          op0=ALU.mult, op1=ALU.add), R=['st4f'], W=['st4f'])
                      op('act', lambda e: e.activation(out=st4f[:, 2:3], in_=st4f[:, 1:2], func=AF.Ln), R=['st4f'], W=['st4f'])
                      op('act', lambda e: e.activation(out=st4f[:, 3:4], in_=st4f[:, 2:3], func=AF.Exp, scale=-0.5),
                         R=['st4f'], W=['st4f'])
                      op('dve', lambda e, src=src, i=i: e.scalar_tensor_tensor(out=ob[i][:, :], in0=src, scalar=st4f[:, 3:4],
                                                                               in1=gfin[:, :], op0=ALU.mult, op1=ALU.mult),
                         R=[x1k[c], 'st4f', 'gfin'], W=['xc%d' % i])
                      r0 = (c0 + c - 1) * 128
                      op('pool', lambda e, r0=r0, i=i: e.dma_start(out=y[r0:r0 + 128, :], in_=ob[i][:, :]),
                         R=['xc%d' % i], W=['y%d' % i], dma='y%d' % i)
                      chk(207)
          except _Stop:
            pass
          tr.barrier(engines=['sp'])
          tr.flush()
    return nc


def _fm(v, nchunk):
    return np.ascontiguousarray(np.asarray(v, np.float32).reshape(nchunk, 128).T)


def make_core_inputs(cfg, inputs, theta=10000.0):
    D, KC, NG, S, NT, NBLK, QS, NF, F = cfg.D, cfg.KC, cfg.NG, cfg.S, cfg.NT, cfg.NBLK, cfg.QS, cfg.NF, cfg.F
    HW = cfg.HW
    NJ = HW // 128
    f32 = np.float32
    x = np.asarray(inputs["x"], f32)
    w_in = np.asarray(inputs["w_in"], f32)[0]
    perm64 = np.concatenate([np.arange(32, 64), np.arange(0, 32)])
    offs = dict(qa=0, ka=HW, va=2 * HW, qb=3 * HW, kb=4 * HW, vb=5 * HW)
    wg = np.zeros((NG, 128, KC * 1024), f32)
    for g in range(NG):
        cols = []
        base = np.arange(g * 128, (g + 1) * 128)
        pbase = np.concatenate([g * 128 + perm64, g * 128 + 64 + perm64])
        for nm, idx in (("qa", base), ("qa", pbase), ("ka", base), ("ka", pbase), ("qb", base), ("kb", base),
                        ("va", base), ("vb", base)):
            cols.append(offs[nm] + idx)
        cols = np.concatenate(cols)
        wsel = w_in[:, cols]
        wg[g] = wsel.reshape(KC, 128, 1024).transpose(1, 0, 2).reshape(128, KC * 1024)
    wgate_full = w_in[:, 6 * HW:]
    wgate = wgate_full.reshape(KC, 128, 2 * KC, 128).transpose(1, 2, 0, 3).reshape(128, 2 * KC, KC * 128)
    bgate = _fm(np.asarray(inputs["b_gate"], f32)[0], 2 * KC)
    wa = np.asarray(inputs["w_branch_a"], f32)[0].reshape(NJ, 128, KC, 128).transpose(1, 2, 0, 3).reshape(128, KC, NJ * 128)
    wb = np.asarray(inputs["w_branch_b"], f32)[0].reshape(NJ, 128, KC, 128).transpose(1, 2, 0, 3).reshape(128, KC, NJ * 128)
    wout = np.asarray(inputs["w_out"], f32)[0].reshape(KC, 128, D).transpose(1, 0, 2).reshape(128, KC * D)
    w_up = np.asarray(inputs["w_up"], f32)[0]
    wup = np.zeros((128, NF, KC * 256), f32)
    wu4 = w_up.reshape(KC, 128, 2, NF, 128)
    wup = wu4.transpose(1, 3, 0, 2, 4).reshape(128, NF, KC * 256)
    conv_w = np.asarray(inputs["conv_w"], f32)[0]
    cw = conv_w.reshape(3, 2, NF, 128).transpose(3, 2, 1, 0).reshape(128, NF * 6)
    conv_b = np.asarray(inputs["conv_b"], f32)[0]
    cb = conv_b.reshape(2, NF, 128).transpose(2, 1, 0).reshape(128, NF * 2)
    wdn = np.asarray(inputs["w_down"], f32)[0].reshape(NF, 128, D).transpose(1, 0, 2).reshape(128, NF * D)
    gmix = _fm(np.asarray(inputs["g_mix"], f32)[0], KC)
    gffn = _fm(np.asarray(inputs["g_ffn"], f32)[0], KC)
    gfin = np.ascontiguousarray(np.broadcast_to(np.asarray(inputs["g_final"], f32)[None, :], (128, D)))
    ident = np.eye(128, dtype=f32)
    kk = np.arange(128)
    tneg = -(kk[:, None] >= kk[None, :]).astype(f32)
    eall = np.zeros((32, 32, 128), f32)
    for n in range(32):
        eall[n, n, :] = 1.0
    eall = eall.reshape(32, 32 * 128)
    masks = np.zeros((128, 8, 512), f32)
    qq = np.arange(512)
    for v in range(4):
        masks[:, v, :] = ((v * 128 + kk)[:, None] < qq[None, :])
        masks[:, 4 + v, :] = ((v * 128 + kk)[:, None] <= qq[None, :])
    masks = masks.reshape(128, 8 * 512)
    half = HD // 2
    inv = (theta ** (-np.arange(half, dtype=f32) / half)).astype(f32)
    shared = dict(wg=wg, wgate=np.ascontiguousarray(wgate), bgate=bgate, wa=np.ascontiguousarray(wa),
                  wb=np.ascontiguousarray(wb), wout=np.ascontiguousarray(wout), wup=np.ascontiguousarray(wup),
                  cw=np.ascontiguousarray(cw), cb=np.ascontiguousarray(cb), wdn=np.ascontiguousarray(wdn),
                  gmix=gmix, gffn=gffn, gfin=gfin, ident=ident, tneg=tneg, eall=eall, masks=masks)
    in_maps = []
    for core in range(cfg.B * 4):
        b, j = core // 4, core % 4
        quarters = [(j + 1) % 4, (j + 2) % 4, (j + 3) % 4, j]
        xk = np.concatenate([x[b, q * QS:(q + 1) * QS] for q in quarters], axis=0)
        pos = np.concatenate([np.arange(q * QS, (q + 1) * QS) for q in quarters]).astype(f32)
        ang = pos[None, :] * np.tile(inv, 4)[:, None]
        cosT = np.cos(ang).astype(f32)
        sgn = np.tile(np.concatenate([-np.ones(32, f32), np.ones(32, f32)]), 2)
        sinT = (np.sin(ang) * sgn[:, None]).astype(f32)
        vis = np.array([1.0 if q < j else 0.0 for q in quarters[:3]] + [1.0], f32)
        kb_row = np.repeat(np.where(vis > 0, 0.0, -BIG).astype(f32), QS // 128)
        bf_row = np.full(32, 0.0, f32)
        bf_row[:NBLK] = np.repeat(np.where(vis > 0, 0.0, NEGINF).astype(f32), QS // 256)
        m = dict(shared)
        m.update(xk=np.ascontiguousarray(xk), cosT=cosT, sinT=sinT,
                 kbias=np.ascontiguousarray(np.broadcast_to(kb_row[None, :], (128, NT))),
                 bflag=np.ascontiguousarray(np.broadcast_to(bf_row[None, :], (128, 32))),
                 halof=np.full((128, 1), 1.0 if j > 0 else 0.0, f32))
        in_maps.append(m)
    return in_maps


_CACHE = {}


def kernel(**inputs):
    cfg = Cfg()
    in_maps = make_core_inputs(cfg, inputs)
    if "nc" not in _CACHE:
        _CACHE["nc"] = build(cfg)
    res = run_bass_kernel_spmd(_CACHE["nc"], in_maps, core_ids=list(range(8)))
    out = np.zeros((cfg.B, cfg.S, cfg.D), np.float32)
    for core in range(8):
        b, j = core // 4, core % 4
        out[b, j * cfg.QS:(j + 1) * cfg.QS] = np.asarray(res.results[core]["y"], np.float32)
    return out
```
